# Optimizing a Trainium2 kernel written in Bass

```python
import math
import jax, jax.numpy as jnp
from jax import lax
import numpy as np

D_MODEL = 2048
BATCH = 4
SEQ = 4096
DEPTH = 4

CHUNK = 64
N_MIXERS = 2
RMS_EPS = 1e-6

GLA_HEADS = 4
GLA_QK = D_MODEL // 2
GLA_V = D_MODEL
GLA_DK = GLA_QK // GLA_HEADS
GLA_DV = GLA_V // GLA_HEADS
GLA_GATE_RANK = 16
GLA_GATE_TAU = 16.0
GLA_IN = 2 * GLA_QK + 2 * GLA_V + GLA_GATE_RANK

DSA_HEADS = 16
DSA_LATENT = 256
DSA_DV = D_MODEL // DSA_HEADS
IDX_HEADS = 16
IDX_DIM = 128
TOPK_MAX = 256
Q_BLOCK = 128
DSA_IN = DSA_HEADS * DSA_LATENT + DSA_LATENT + IDX_HEADS * IDX_DIM + IDX_DIM + IDX_HEADS

REL_BUCKETS = 32
REL_MAX_DIST = 128

D_FF = -(-8 * D_MODEL // (3 * 256)) * 256

N_GLA = (DEPTH + 1) // 2
N_DSA = DEPTH // 2

kernel_name = "hybrid_gla_dsa_streaming_trunk"


def rmsnorm(x, g):
    xf = x.astype(jnp.float32)
    y = xf * lax.rsqrt(jnp.mean(xf * xf, axis=-1, keepdims=True) + RMS_EPS)
    return (y * g.astype(jnp.float32)).astype(x.dtype)


def t5_bucket(rel):
    nb = REL_BUCKETS // 2
    max_exact = nb // 2
    ret = (rel > 0).astype(jnp.int32) * nb
    n = jnp.abs(rel)
    large = max_exact + (jnp.log(jnp.maximum(n, 1).astype(jnp.float32) / max_exact)
                         / math.log(REL_MAX_DIST / max_exact) * (nb - max_exact)).astype(jnp.int32)
    large = jnp.minimum(large, nb - 1)
    return ret + jnp.where(n < max_exact, n, large)


def gla_mixer(h, w_in, w_a2, b_a, g_norm, w_out):
    B, T, _ = h.shape
    proj = h @ w_in
    q, k, v, g, a = jnp.split(proj, [GLA_QK, 2 * GLA_QK, 2 * GLA_QK + GLA_V, 2 * GLA_QK + 2 * GLA_V], axis=-1)
    q = q.reshape(B, T, GLA_HEADS, GLA_DK) * (GLA_DK ** -0.5)
    k = k.reshape(B, T, GLA_HEADS, GLA_DK)
    v = v.reshape(B, T, GLA_HEADS, GLA_DV)
    log_alpha = jax.nn.log_sigmoid((a @ w_a2 + b_a).astype(jnp.float32)) / GLA_GATE_TAU
    log_alpha = log_alpha.reshape(B, T, GLA_HEADS, GLA_DK)
    nc = T // CHUNK

    def to_chunks(z):
        return jnp.moveaxis(z.reshape(B, nc, CHUNK, *z.shape[2:]), 1, 0)

    def step(state, xs):
        qc, kc, vc, lac = xs
        lcum = jnp.cumsum(lac, axis=1)
        ltot = lcum[:, -1]
        kd = kc.astype(jnp.float32) * jnp.exp(ltot[:, None] - lcum)
        state = state * jnp.exp(ltot)[..., None] + jnp.einsum('bchk,bchv->bhkv', kd, vc.astype(jnp.float32))
        oc = jnp.einsum('bchk,bhkv->bchv', qc.astype(jnp.float32), state)
        return state, oc

    s0 = jnp.zeros((B, GLA_HEADS, GLA_DK, GLA_DV), jnp.float32)
    _, o = lax.scan(step, s0, (to_chunks(q), to_chunks(k), to_chunks(v), to_chunks(log_alpha)))
    o = jnp.moveaxis(o, 0, 1).reshape(B, T, GLA_HEADS, GLA_DV)
    o = rmsnorm(o, g_norm).reshape(B, T, GLA_V).astype(h.dtype)
    o = o * jax.nn.silu(g)
    return o @ w_out


def dsa_mixer(h, w_in, kv_norm, kidx_norm, w_uv, w_out, rel_bias):
    B, T, _ = h.shape
    proj = h @ w_in
    s1 = DSA_HEADS * DSA_LATENT
    s2 = s1 + DSA_LATENT
    s3 = s2 + IDX_HEADS * IDX_DIM
    s4 = s3 + IDX_DIM
    q, c, qi, ki, wi = jnp.split(proj, [s1, s2, s3, s4], axis=-1)
    q = q.reshape(B, T, DSA_HEADS, DSA_LATENT)
    c = rmsnorm(c, kv_norm)
    qi = qi.reshape(B, T, IDX_HEADS, IDX_DIM)
    ki = rmsnorm(ki, kidx_norm).astype(jnp.float32)
    wi = wi * (IDX_HEADS ** -0.5)
    k_sel = min(TOPK_MAX, T // 4)
    pos = jnp.arange(T, dtype=jnp.int32)
    key_chunk = pos // CHUNK
    nb = T // Q_BLOCK
    c32 = c.astype(jnp.float32)
    w_uv32 = w_uv.astype(jnp.float32)

    def blk(z):
        return jnp.moveaxis(z.reshape(B, nb, Q_BLOCK, *z.shape[2:]), 1, 0)

    def attend(xs):
        qb, qib, wib, tq = xs
        sc = jax.nn.relu(jnp.einsum('bqhd,bsd->bqhs', qib.astype(jnp.float32), ki) * (IDX_DIM ** -0.5))
        score = jnp.einsum('bqhs,bqh->bqs', sc, wib.astype(jnp.float32))
        q_chunk = tq // CHUNK
        adm = key_chunk[None, :] <= q_chunk[:, None]
        score = jnp.where(adm[None], score, -jnp.inf)
        _, idx = lax.top_k(score, k_sel)
        c_sel = jax.vmap(lambda cb, ib: cb[ib])(c32, idx)
        logits = jnp.einsum('bqhd,bqkd->bqhk', qb.astype(jnp.float32), c_sel) * (DSA_LATENT ** -0.5)
        bias = rel_bias.astype(jnp.float32)[t5_bucket(idx - tq[None, :, None])]
        logits = logits + jnp.swapaxes(bias, -1, -2)
        valid = (idx // CHUNK) <= q_chunk[None, :, None]
        logits = jnp.where(valid[:, :, None, :], logits, -jnp.inf)
        p = jax.nn.softmax(logits, axis=-1)
        ob = jnp.einsum('bqhk,bqkd->bqhd', p, c_sel)
        return jnp.einsum('bqhd,hde->bqhe', ob, w_uv32)

    o = lax.map(attend, (blk(q), blk(qi), blk(wi), pos.reshape(nb, Q_BLOCK)))
    o = jnp.moveaxis(o, 0, 1).reshape(B, T, DSA_HEADS * DSA_DV).astype(h.dtype)
    return o @ w_out


def swiglu(h, w1, w3, w2):
    return (jax.nn.silu(h @ w1) * (h @ w3)) @ w2


def setup_inputs(seed: int = 0) -> dict:
    key = jax.random.key(seed)
    ks = jax.random.split(key, 24)
    f32 = jnp.float32

    def nrm(k, shape, scale):
        return jax.random.normal(k, shape, f32) * scale

    out_scale = (2.0 * DEPTH) ** -0.5
    return {
        "x": nrm(ks[0], (BATCH, SEQ, D_MODEL), 1.0),
        "norm_mix": 1.0 + nrm(ks[1], (DEPTH, D_MODEL), 0.05),
        "norm_ffn": 1.0 + nrm(ks[2], (DEPTH, D_MODEL), 0.05),
        "norm_final": 1.0 + nrm(ks[3], (D_MODEL,), 0.05),
        "gla_w_in": nrm(ks[4], (N_GLA, D_MODEL, GLA_IN), D_MODEL ** -0.5),
        "gla_w_a2": nrm(ks[5], (N_GLA, GLA_GATE_RANK, GLA_QK), GLA_GATE_RANK ** -0.5),
        "gla_b_a": nrm(ks[6], (N_GLA, GLA_QK), 0.1),
        "gla_g_norm": 1.0 + nrm(ks[7], (N_GLA, GLA_DV), 0.05),
        "gla_w_out": nrm(ks[8], (N_GLA, GLA_V, D_MODEL), GLA_V ** -0.5 * out_scale),
        "dsa_w_in": nrm(ks[9], (N_DSA, D_MODEL, DSA_IN), D_MODEL ** -0.5),
        "dsa_kv_norm": 1.0 + nrm(ks[10], (N_DSA, DSA_LATENT), 0.05),
        "dsa_kidx_norm": 1.0 + nrm(ks[11], (N_DSA, IDX_DIM), 0.05),
        "dsa_w_uv": nrm(ks[12], (N_DSA, DSA_HEADS, DSA_LATENT, DSA_DV), DSA_LATENT ** -0.5),
        "dsa_w_out": nrm(ks[13], (N_DSA, DSA_HEADS * DSA_DV, D_MODEL), (DSA_HEADS * DSA_DV) ** -0.5 * out_scale),
        "rel_bias": nrm(ks[14], (REL_BUCKETS, DSA_HEADS), 0.5),
        "ffn_w1": nrm(ks[15], (DEPTH, D_MODEL, D_FF), D_MODEL ** -0.5),
        "ffn_w3": nrm(ks[16], (DEPTH, D_MODEL, D_FF), D_MODEL ** -0.5),
        "ffn_w2": nrm(ks[17], (DEPTH, D_FF, D_MODEL), D_FF ** -0.5 * out_scale),
    }


def reference(x, norm_mix, norm_ffn, norm_final,
              gla_w_in, gla_w_a2, gla_b_a, gla_g_norm, gla_w_out,
              dsa_w_in, dsa_kv_norm, dsa_kidx_norm, dsa_w_uv, dsa_w_out,
              rel_bias, ffn_w1, ffn_w3, ffn_w2):
    for i in range(DEPTH):
        h = rmsnorm(x, norm_mix[i])
        j = i // N_MIXERS
        if i % N_MIXERS == 0:
            y = gla_mixer(h, gla_w_in[j], gla_w_a2[j], gla_b_a[j], gla_g_norm[j], gla_w_out[j])
        else:
            y = dsa_mixer(h, dsa_w_in[j], dsa_kv_norm[j], dsa_kidx_norm[j], dsa_w_uv[j], dsa_w_out[j], rel_bias)
        x = x + y
        x = x + swiglu(rmsnorm(x, norm_ffn[i]), ffn_w1[i], ffn_w3[i], ffn_w2[i])
    return rmsnorm(x, norm_final)
```

```python
import math
from contextlib import ExitStack

import numpy as np
import concourse.bass as bass
import concourse.mybir as mybir
from concourse.bass_utils import run_bass_kernel_spmd

F32 = mybir.dt.float32
BF16 = mybir.dt.bfloat16
ALU = mybir.AluOpType
AF = mybir.ActivationFunctionType

P = 128
D = 2048
DT = D // P
T = 4096
NB = 4
DEPTH = 4
DFF = 5632
FT = DFF // P
CH = 64
EPS = 1e-6
GLA_IN = 6160
DSA_IN = 6544
KSEL = 256
NEG_ADM = -1.0e30
NEG_REP = -3.0e38
QB = 256
NDQ = 6
DBG = {}


class Buf:
    __slots__ = ("w", "r")

    def __init__(self):
        self.w = None
        self.r = {}


class TB:
    def __init__(self, t):
        self.t = t
        self.b = Buf()


class Sched:
    def __init__(self, nc, es):
        self.nc = nc
        self.eng = {"pe": nc.tensor, "act": nc.scalar, "dve": nc.vector,
                    "pool": nc.gpsimd, "sp": nc.sync}
        names = ["pe", "act", "dve", "pool"]
        names += [f"qsp{i}" for i in range(NDQ)] + [f"qpool{i}" for i in range(NDQ)] + [f"qact{i}" for i in range(NDQ)]
        self.sem = {n: es.enter_context(nc.semaphore("s_" + n)) for n in names}
        self.cnt = dict.fromkeys(names, 0)
        self.seen = {e: dict.fromkeys(names, 0) for e in self.eng}
        self.dq = {"sp": 0, "pool": 0, "act": 0}

    def _wait(self, E, s, c):
        if s == E and c > self.cnt[s]:
            return
        if self.seen[E][s] < c:
            self.eng[E].wait_ge(self.sem[s], c)
            self.seen[E][s] = c

    def op(self, E, fn, reads=(), writes=(), inc=True, dma=False):
        if dma:
            slot = self.dq[E]
            self.dq[E] = (slot + 1) % NDQ
            s = f"q{E}{slot}"
            self._wait(E, s, self.cnt[s])
        else:
            s = E
        for b in reads:
            if b.w:
                self._wait(E, *b.w)
        for b in writes:
            if b.w:
                self._wait(E, *b.w)
            for rs, rc in b.r.items():
                self._wait(E, rs, rc)
        ins = fn()
        if inc:
            step = 16 if dma else 1
            self.cnt[s] += step
            ins.then_inc(self.sem[s], step)
            c = self.cnt[s]
        else:
            c = self.cnt[s] + 1
        for b in writes:
            b.w = (s, c)
            b.r = {}
        for b in reads:
            if b.r.get(s, 0) < c:
                b.r[s] = c
        return ins

    def sc_begin(self, name):
        if DBG.get("scopes"):
            cm = self.nc.named_scope(name)
            cm.__enter__()
            self._scopes = getattr(self, "_scopes", []) + [cm]

    def sc_end(self):
        if DBG.get("scopes"):
            self._scopes.pop().__exit__(None, None, None)

    def barrier(self):
        for E in self.eng:
            for s, c in self.cnt.items():
                if c:
                    self._wait(E, s, c)

    def sb(self, es, shape, dt, name):
        return TB(es.enter_context(self.nc.sbuf_tensor(name, list(shape), dt)))

    def ps(self, es, shape, dt, name):
        return TB(es.enter_context(self.nc.psum_tensor(name, list(shape), dt)))

    def dma(self, out_ap, in_ap, reads=(), writes=(), q="sp"):
        eng = self.eng[q]
        return self.op(q, lambda: eng.dma_start(out=out_ap, in_=in_ap), reads, writes, dma=True)

    def mm(self, out_tb, out_ap, pairs, reads, first=True, last=True):
        n = len(pairs)
        nc = self.nc
        for i, (l, r) in enumerate(pairs):
            st = first and i == 0
            sp = last and i == n - 1
            self.op("pe", lambda l=l, r=r, st=st, sp=sp: nc.tensor.matmul(out_ap, l, r, start=st, stop=sp),
                    reads=reads if i == 0 else (), writes=[out_tb.b], inc=sp)


class Rot:
    def __init__(self, items):
        self.items = items
        self.i = 0

    def next(self):
        it = self.items[self.i % len(self.items)]
        self.i += 1
        return it


_uid = [0]


def uid(p):
    _uid[0] += 1
    return f"{p}{_uid[0]}"


def phase_norm(S, src, dst, gw, KT, nfeat, dst_f32=False):
    S.sc_begin(uid("norm"))
    nc = S.nc
    TBK = 512
    sv = src.t.rearrange("(k p) t -> p k t", p=P)
    dv = dst.t.rearrange("(k p) t -> p k t", p=P)
    with ExitStack() as es:
        g = S.sb(es, [P, KT], F32, uid("ng"))
        ones = S.sb(es, [P, P], BF16, uid("nones"))
        S.dma(g.t[:], gw.t, writes=[g.b])
        S.op("dve", lambda: nc.vector.memset(ones.t[:], 1.0), writes=[ones.b])
        xs = Rot([S.sb(es, [P, KT, TBK], F32, uid("nx")) for _ in range(2)])
        sqs = Rot([S.sb(es, [P, KT, TBK], BF16, uid("nsq")) for _ in range(2)])
        hs = Rot([S.sb(es, [P, KT, TBK], F32 if dst_f32 else BF16, uid("nh")) for _ in range(2)])
        rss = Rot([S.sb(es, [P, TBK], F32, uid("nrs")) for _ in range(2)])
        pss = Rot([S.ps(es, [P, TBK], F32, uid("nps")) for _ in range(2)])
        for tb in range(T // TBK):
            sl = slice(tb * TBK, (tb + 1) * TBK)
            x = xs.next()
            h = hs.next()
            sq = sqs.next()
            rs = rss.next()
            ps = pss.next()
            S.dma(x.t[:], sv[:, :, sl], reads=[src.b], writes=[x.b])
            S.op("act", lambda x=x, sq=sq: nc.scalar.activation(out=sq.t[:], in_=x.t[:], func=AF.Square),
                 reads=[x.b], writes=[sq.b])
            S.mm(ps, ps.t[:], [(ones.t[:], sq.t[:, k, :]) for k in range(KT)], reads=[ones.b, sq.b])
            S.op("act", lambda ps=ps, rs=rs: nc.scalar.activation(out=rs.t[:], in_=ps.t[:], func=AF.Sqrt,
                                                                 bias=EPS, scale=1.0 / nfeat),
                 reads=[ps.b], writes=[rs.b])
            S.op("dve", lambda rs=rs: nc.vector.reciprocal(rs.t[:], rs.t[:]), reads=[rs.b], writes=[rs.b])
            for k in range(KT):
                S.op("dve", lambda k=k, x=x, h=h, rs=rs: nc.vector.scalar_tensor_tensor(
                    out=h.t[:, k, :], in0=x.t[:, k, :], scalar=g.t[:, k:k + 1], in1=rs.t[:],
                    op0=ALU.mult, op1=ALU.mult), reads=[x.b, g.b, rs.b], writes=[h.b])
            S.dma(dv[:, :, sl], h.t[:], reads=[h.b], writes=[dst.b])
    S.barrier()
    S.sc_end()


def phase_linear(S, hT, KT, W, c0, ncols, mode, sink, TSB=2048):
    phase_linear_multi(S, hT, KT, W, [(c0, ncols, mode, sink)], TSB)


def phase_linear_multi(S, hT, KT, W, jobs, TSB=2048):
    nc = S.nc
    S.sc_begin(uid("lin_" + "_".join(f"{m}{c}" for c, _, m, _ in jobs) + "_"))
    hv = hT.t.rearrange("(k p) t -> p k t", p=P)
    wv = W.t.rearrange("(k p) n -> p k n", p=P)
    modes = {m for _, _, m, _ in jobs}
    with ExitStack() as es:
        hsb = S.sb(es, [P, KT, TSB], BF16, uid("lh"))
        wsm = {}
        if "fm" in modes:
            wsm["fm"] = Rot([S.sb(es, [P, KT, 128], BF16, uid("lwf")) for _ in range(3)])
        if "tm" in modes:
            wsm["tm"] = Rot([S.sb(es, [P, KT, 512], BF16, uid("lwt")) for _ in range(2 if len(modes) > 1 else 3)])
        pss = Rot([S.ps(es, [P, 512], F32, uid("lps")) for _ in range(4)])
        for sbk in range(T // TSB):
            ts0 = sbk * TSB
            for q4 in range(4):
                qs = TSB // 4
                S.dma(hsb.t[:, :, q4 * qs:(q4 + 1) * qs], hv[:, :, ts0 + q4 * qs: ts0 + (q4 + 1) * qs],
                      reads=[hT.b], writes=[hsb.b])
            for (c0, ncols, mode, sink) in jobs:
                nw = 128 if mode == "fm" else 512
                for n0 in range(0, ncols, nw):
                    nsz = min(nw, ncols - n0)
                    w = wsm[mode].next()
                    S.dma(w.t[:, :, :nsz], wv[:, :, c0 + n0:c0 + n0 + nsz], reads=[W.b], writes=[w.b], q="pool")
                    if mode == "fm":
                        for t0 in range(0, TSB, 512):
                            ps = pss.next()
                            S.mm(ps, ps.t[:nsz, :], [(w.t[:, k, :nsz], hsb.t[:, k, t0:t0 + 512]) for k in range(KT)],
                                 reads=[w.b, hsb.b])
                            sink(ps, n0, nsz, ts0 + t0)
                    else:
                        for t0 in range(0, TSB, P):
                            ps = pss.next()
                            S.mm(ps, ps.t[:, :nsz], [(hsb.t[:, k, t0:t0 + P], w.t[:, k, :nsz]) for k in range(KT)],
                                 reads=[w.b, hsb.b])
                            sink(ps, ts0 + t0, n0, nsz)
        for (_, _, _, sink) in jobs:
            if hasattr(sink, "flush"):
                sink.flush()
    S.barrier()
    S.sc_end()


def make_store_sink(S, es, dst, mode, dt, func=AF.Copy, scale=1.0, roff=0):
    nc = S.nc
    obs = Rot([S.sb(es, [P, 512], dt, uid("so")) for _ in range(3)])

    def sink_fm(ps, n0, nsz, t0):
        o = obs.next()
        S.op("act", lambda: nc.scalar.activation(out=o.t[:nsz, :], in_=ps.t[:nsz, :], func=func, scale=scale),
             reads=[ps.b], writes=[o.b])
        S.dma(dst.t[roff + n0:roff + n0 + nsz, t0:t0 + 512], o.t[:nsz, :], reads=[o.b], writes=[dst.b])

    def sink_tm(ps, t0, n0, nsz):
        o = obs.next()
        S.op("act", lambda: nc.scalar.activation(out=o.t[:, :nsz], in_=ps.t[:, :nsz], func=func, scale=scale),
             reads=[ps.b], writes=[o.b])
        S.dma(dst.t[t0:t0 + P, roff + n0:roff + n0 + nsz], o.t[:, :nsz], reads=[o.b], writes=[dst.b])

    return sink_fm if mode == "fm" else sink_tm


def make_resid_sink(S, es, xT, slab=2048):
    nc = S.nc
    xin = Rot([S.sb(es, [P, slab], F32, uid("rx")) for _ in range(2)])
    cur = {}

    def sink(ps, n0, nsz, t0):
        off = t0 % slab
        if off == 0:
            xt = xin.next()
            S.dma(xt.t[:nsz, :], xT.t[n0:n0 + nsz, t0:t0 + slab], reads=[xT.b], writes=[xt.b])
            cur["xt"] = xt
        xt = cur["xt"]
        S.op("dve", lambda: nc.vector.tensor_tensor(out=xt.t[:nsz, off:off + 512], in0=xt.t[:nsz, off:off + 512],
                                                    in1=ps.t[:nsz, :], op=ALU.add), reads=[xt.b, ps.b], writes=[xt.b])
        if off + 512 == slab:
            S.dma(xT.t[n0:n0 + nsz, t0 + 512 - slab:t0 + 512], xt.t[:nsz, :], reads=[xt.b], writes=[xT.b])

    return sink


def phase_ffn(S, hT, xT, w1, w3, w2):
    nc = S.nc
    S.sc_begin(uid("ffn"))
    TSB = 1024
    hv = hT.t.rearrange("(k p) t -> p k t", p=P)
    w1v = w1.t.rearrange("(k p) n -> p k n", p=P)
    w3v = w3.t.rearrange("(k p) n -> p k n", p=P)
    w2v = w2.t.rearrange("(f p) n -> p f n", p=P)
    with ExitStack() as es:
        hsb = S.sb(es, [P, DT, TSB], BF16, uid("fh"))
        gT = S.sb(es, [P, FT, TSB], BF16, uid("fg"))
        w13 = Rot([(S.sb(es, [P, DT, P], BF16, uid("fw1")), S.sb(es, [P, DT, P], BF16, uid("fw3")))
                   for _ in range(3)])
        w2s = Rot([S.sb(es, [P, FT, P], BF16, uid("fw2")) for _ in range(2)])
        s1s = Rot([S.sb(es, [P, 512], F32, uid("fs1")) for _ in range(2)])
        p1s = Rot([S.ps(es, [P, 512], F32, uid("fp1")) for _ in range(2)])
        p3s = Rot([S.ps(es, [P, 512], F32, uid("fp3")) for _ in range(2)])
        pys = Rot([S.ps(es, [P, 512], F32, uid("fpy")) for _ in range(2)])
        rsink = make_resid_sink(S, es, xT, slab=TSB)
        for sbk in range(T // TSB):
            ts0 = sbk * TSB
            for q4 in range(2):
                qs = TSB // 2
                S.dma(hsb.t[:, :, q4 * qs:(q4 + 1) * qs], hv[:, :, ts0 + q4 * qs:ts0 + (q4 + 1) * qs],
                      reads=[hT.b], writes=[hsb.b])
            for ft in range(FT):
                a, b = w13.next()
                S.dma(a.t[:], w1v[:, :, ft * P:(ft + 1) * P], reads=[w1.b], writes=[a.b], q="pool")
                S.dma(b.t[:], w3v[:, :, ft * P:(ft + 1) * P], reads=[w3.b], writes=[b.b], q="pool")
                for nb in range(TSB // 512):
                    tsl = slice(nb * 512, (nb + 1) * 512)
                    p1 = p1s.next()
                    p3 = p3s.next()
                    s1 = s1s.next()
                    S.mm(p1, p1.t[:], [(a.t[:, k, :], hsb.t[:, k, tsl]) for k in range(DT)], reads=[a.b, hsb.b])
                    S.mm(p3, p3.t[:], [(b.t[:, k, :], hsb.t[:, k, tsl]) for k in range(DT)], reads=[b.b, hsb.b])
                    S.op("act", lambda p1=p1, s1=s1: nc.scalar.activation(out=s1.t[:], in_=p1.t[:], func=AF.Silu),
                         reads=[p1.b], writes=[s1.b])
                    S.op("dve", lambda s1=s1, p3=p3, ft=ft, tsl=tsl: nc.vector.tensor_tensor(
                        out=gT.t[:, ft, tsl], in0=s1.t[:], in1=p3.t[:], op=ALU.mult),
                        reads=[s1.b, p3.b], writes=[gT.b])
            for dt_ in range(DT):
                w = w2s.next()
                S.dma(w.t[:, :FT // 2, :], w2v[:, :FT // 2, dt_ * P:(dt_ + 1) * P], reads=[w2.b], writes=[w.b], q="pool")
                S.dma(w.t[:, FT // 2:, :], w2v[:, FT // 2:, dt_ * P:(dt_ + 1) * P], reads=[w2.b], writes=[w.b], q="pool")
                for nb in range(TSB // 512):
                    tsl = slice(nb * 512, (nb + 1) * 512)
                    py = pys.next()
                    S.mm(py, py.t[:], [(w.t[:, f, :], gT.t[:, f, tsl]) for f in range(FT)], reads=[w.b, gT.b])
                    rsink(py, dt_ * P, P, ts0 + nb * 512)
    S.barrier()
    S.sc_end()


def phase_gla(S, es0, hT, xT, C, j, dram):
    nc = S.nc
    w_in = C["gla_w_in"][j]
    qT = dram("g_qT", [1024, T], BF16)
    sgT = dram("g_sgT", [2048, T], BF16)
    aT = dram("g_aT", [16, T], F32)
    ktm = dram("g_k", [T, 1024], F32)
    vtm = dram("g_v", [T, 2048], BF16)
    kd = dram("g_kd", [T, 1024], BF16)
    decS = dram("g_decS", [P, (T // P) * 16], F32)
    ogT = dram("g_ogT", [2048, T], BF16)
    with ExitStack() as es:
        phase_linear_multi(S, hT, DT, w_in, [
            (0, 1024, "fm", make_store_sink(S, es, qT, "fm", BF16, scale=1.0 / 16.0)),
            (4096, 2048, "fm", make_store_sink(S, es, sgT, "fm", BF16, func=AF.Silu)),
            (6144, 16, "fm", make_store_sink(S, es, aT, "fm", F32)),
            (1024, 1024, "tm", make_store_sink(S, es, ktm, "tm", F32)),
            (2048, 2048, "tm", make_store_sink(S, es, vtm, "tm", BF16)),
        ])
    S.sc_begin(uid("gla_gate"))
    with ExitStack() as es:
        waug = S.sb(es, [32, 1024], F32, uid("gw"))
        S.dma(waug.t[:], C["gla_waug"][j].t, writes=[waug.b])
        mrev = S.sb(es, [P, P], F32, uid("gm"))
        S.dma(mrev.t[:], C["mrev"].t, writes=[mrev.b])
        cind = S.sb(es, [P, 2], F32, uid("gci"))
        S.dma(cind.t[:], C["cind"].t, writes=[cind.b])
        aaug = Rot([S.sb(es, [32, P], F32, uid("ga")) for _ in range(2)])
        for a in aaug.items:
            S.op("dve", lambda a=a: nc.vector.memset(a.t[:], 1.0), writes=[a.b])
        Ls = Rot([S.sb(es, [P, 1024], F32, uid("gL")) for _ in range(2)])
        dec = S.sb(es, [P, 1024], F32, uid("gdec"))
        kts = Rot([S.sb(es, [P, 1024], F32, uid("gk")) for _ in range(2)])
        kds = Rot([S.sb(es, [P, 1024], BF16, uid("gkd")) for _ in range(2)])
        dcs = S.sb(es, [P, (T // P) * 16], F32, uid("gdcs"))
        pz = Rot([S.ps(es, [P, 512], F32, uid("gpz")) for _ in range(4)])
        pc = S.ps(es, [P, 16], F32, uid("gpc"))
        for tt in range(T // P):
            tsl = slice(tt * P, (tt + 1) * P)
            a = aaug.next()
            L = Ls.next()
            kt_ = kts.next()
            kdt = kds.next()
            S.dma(a.t[0:16, :], aT.t[:, tsl], reads=[aT.b], writes=[a.b])
            S.dma(kt_.t[:], ktm.t[tsl, :], reads=[ktm.b], writes=[kt_.b])
            for hb in range(2):
                z = pz.next()
                S.mm(z, z.t[:], [(a.t[:, :], waug.t[:, hb * 512:(hb + 1) * 512])], reads=[a.b, waug.b])
                S.op("act", lambda z=z, L=L, hb=hb: nc.scalar.activation(
                    out=L.t[:, hb * 512:(hb + 1) * 512], in_=z.t[:], func=AF.Exp, scale=-1.0),
                    reads=[z.b], writes=[L.b])
            S.op("act", lambda L=L: nc.scalar.activation(out=L.t[:], in_=L.t[:], func=AF.Ln, bias=1.0),
                 reads=[L.b], writes=[L.b])
            for hb in range(2):
                z = pz.next()
                S.mm(z, z.t[:], [(mrev.t[:], L.t[:, hb * 512:(hb + 1) * 512])], reads=[mrev.b, L.b])
                S.op("act", lambda z=z, hb=hb: nc.scalar.activation(
                    out=dec.t[:, hb * 512:(hb + 1) * 512], in_=z.t[:], func=AF.Exp, scale=-1.0 / 16.0),
                    reads=[z.b], writes=[dec.b])
            S.op("dve", lambda kt_=kt_, kdt=kdt: nc.vector.tensor_tensor(out=kdt.t[:], in0=kt_.t[:], in1=dec.t[:],
                                                                          op=ALU.mult),
                 reads=[kt_.b, dec.b], writes=[kdt.b])
            S.dma(kd.t[tsl, :], kdt.t[:], reads=[kdt.b], writes=[kd.b])
            for k8 in range(8):
                S.mm(pc, pc.t[:, k8 * 2:k8 * 2 + 2], [(L.t[:, k8 * P:(k8 + 1) * P], cind.t[:])], reads=[L.b, cind.b])
            S.op("act", lambda tt=tt: nc.scalar.activation(out=dcs.t[:, tt * 16:(tt + 1) * 16], in_=pc.t[:],
                                                           func=AF.Exp, scale=-1.0 / 16.0),
                 reads=[pc.b], writes=[dcs.b])
        S.dma(decS.t, dcs.t[:], reads=[dcs.b], writes=[decS.b])
    S.barrier()
    S.sc_end()
    S.sc_begin(uid("gla_scan"))
    with ExitStack() as es:
        dcs = S.sb(es, [P, (T // P) * 16], F32, uid("sdcs"))
        S.dma(dcs.t[:], decS.t, reads=[decS.b], writes=[dcs.b])
        gn = S.sb(es, [P, 4], F32, uid("sgn"))
        S.dma(gn.t[:], C["gla_gn"][j].t, writes=[gn.b])
        ones = S.sb(es, [P, P], F32, uid("sones"))
        S.op("dve", lambda: nc.vector.memset(ones.t[:], 1.0), writes=[ones.b])
        St = [S.sb(es, [P, 512], F32, uid("sS")) for _ in range(8)]
        Sb = [S.sb(es, [P, 512], BF16, uid("sSb")) for _ in range(8)]
        for s_ in St:
            S.op("dve", lambda s_=s_: nc.vector.memset(s_.t[:], 0.0), writes=[s_.b])
        qbs = Rot([S.sb(es, [P, 8, 512], BF16, uid("sq")) for _ in range(2)])
        sgs = Rot([S.sb(es, [P, 16, 512], BF16, uid("ssg")) for _ in range(2)])
        ogs = Rot([S.sb(es, [P, 16, 512], BF16, uid("sog")) for _ in range(2)])
        kdb = Rot([S.sb(es, [P, 1024], BF16, uid("skd")) for _ in range(2)])
        vb = Rot([S.sb(es, [P, 2048], BF16, uid("sv")) for _ in range(2)])
        sq = S.sb(es, [P, 16, CH], F32, uid("ssq"))
        rs = S.sb(es, [P, 4, CH], F32, uid("srs"))
        tmpb = S.sb(es, [P, 16, CH], F32, uid("stmpb"))
        pu = Rot([S.ps(es, [P, 512], F32, uid("spu")) for _ in range(3)])
        po = [S.ps(es, [P, 8, CH], F32, uid("spo")) for _ in range(2)]
        pss = S.ps(es, [P, 4, CH], F32, uid("spss"))
        qv = qT.t.rearrange("(k p) t -> p k t", p=P)
        sgv = sgT.t.rearrange("(k p) t -> p k t", p=P)
        ogv = ogT.t.rearrange("(k p) t -> p k t", p=P)
        NCH = T // CH
        Sb2 = [Sb, [S.sb(es, [P, 512], BF16, uid("sSb2")) for _ in range(8)]]
        po2 = [po, [S.ps(es, [P, 8, CH], F32, uid("spo2")) for _ in range(2)]]
        tiles = {}
        blocks = {}

        def get_tile(tt):
            if tt not in tiles:
                tsl = slice(tt * P, (tt + 1) * P)
                kdt = kdb.next()
                vt_ = vb.next()
                S.dma(kdt.t[:], kd.t[tsl, :], reads=[kd.b], writes=[kdt.b])
                S.dma(vt_.t[:], vtm.t[tsl, :], reads=[vtm.b], writes=[vt_.b])
                tiles.clear()
                tiles[tt] = (kdt, vt_)
            return tiles[tt]

        def get_block(blk):
            if blk not in blocks:
                bsl = slice(blk * 512, (blk + 1) * 512)
                qb = qbs.next()
                sg = sgs.next()
                og = ogs.next()
                S.dma(qb.t[:], qv[:, :, bsl], reads=[qT.b], writes=[qb.b])
                S.dma(sg.t[:], sgv[:, :, bsl], reads=[sgT.b], writes=[sg.b])
                for i16 in range(16):
                    S.op("dve", lambda i16=i16: nc.vector.tensor_scalar_mul(
                        sg.t[:, i16, :], sg.t[:, i16, :], gn.t[:, i16 % 4:i16 % 4 + 1]),
                        reads=[sg.b, gn.b], writes=[sg.b])
                blocks.clear()
                blocks[blk] = (qb, sg, og)
            return blocks[blk]

        def stage_state(c):
            tt, c2 = c // 2, c % 2
            cs = c2 * CH
            kdt, vt_ = get_tile(tt)
            sbs = Sb2[c % 2]
            for h in range(4):
                for kt2 in range(2):
                    i8 = h * 2 + kt2
                    u = pu.next()
                    S.mm(u, u.t[:], [(kdt.t[cs:cs + CH, i8 * P:(i8 + 1) * P], vt_.t[cs:cs + CH, h * 512:(h + 1) * 512])],
                         reads=[kdt.b, vt_.b])
                    didx = tt * 16 + i8 * 2 + c2
                    S.op("dve", lambda: nc.vector.scalar_tensor_tensor(
                        out=St[i8].t[:], in0=St[i8].t[:], scalar=dcs.t[:, didx:didx + 1], in1=u.t[:],
                        op0=ALU.mult, op1=ALU.add), reads=[St[i8].b, dcs.b, u.b], writes=[St[i8].b])
                    S.op("act", lambda: nc.scalar.copy(out=sbs[i8].t[:], in_=St[i8].t[:]),
                         reads=[St[i8].b], writes=[sbs[i8].b])

        def stage_read_pe(c):
            blk = c // 8
            qb, sg, og = get_block(blk)
            ccol = slice((c % 8) * CH, (c % 8) * CH + CH)
            sbs = Sb2[c % 2]
            pp = po2[c % 2]
            for h in range(4):
                for vt4 in range(4):
                    i16 = h * 4 + vt4
                    pt = pp[i16 // 8]
                    S.mm(pt, pt.t[:, i16 % 8, :],
                         [(sbs[h * 2 + kt2].t[:, vt4 * P:(vt4 + 1) * P], qb.t[:, h * 2 + kt2, ccol]) for kt2 in range(2)],
                         reads=[sbs[h * 2].b, sbs[h * 2 + 1].b, qb.b])
            for half in range(2):
                S.op("act", lambda half=half: nc.scalar.activation(
                    out=sq.t[:, half * 8:(half + 1) * 8, :], in_=pp[half].t[:], func=AF.Square),
                    reads=[pp[half].b], writes=[sq.b])
            for h in range(4):
                S.mm(pss, pss.t[:, h, :], [(ones.t[:], sq.t[:, h * 4 + v4, :]) for v4 in range(4)],
                     reads=[ones.b, sq.b])
            S.op("act", lambda: nc.scalar.activation(out=rs.t[:], in_=pss.t[:], func=AF.Sqrt,
                                                     bias=EPS, scale=1.0 / 512.0),
                 reads=[pss.b], writes=[rs.b])

        def stage_read_dve(c):
            blk = c // 8
            qb, sg, og = get_block(blk)
            ccol = slice((c % 8) * CH, (c % 8) * CH + CH)
            pp = po2[c % 2]
            S.op("dve", lambda: nc.vector.reciprocal(rs.t[:], rs.t[:]), reads=[rs.b], writes=[rs.b])
            for half in range(2):
                S.op("dve", lambda half=half: nc.vector.tensor_tensor(
                    out=tmpb.t[:, half * 8:(half + 1) * 8, :], in0=pp[half].t[:],
                    in1=sg.t[:, half * 8:(half + 1) * 8, ccol], op=ALU.mult),
                    reads=[pp[half].b, sg.b], writes=[tmpb.b])
            for h in range(4):
                S.op("dve", lambda h=h: nc.vector.tensor_tensor(
                    out=og.t[:, h * 4:(h + 1) * 4, ccol], in0=tmpb.t[:, h * 4:(h + 1) * 4, :],
                    in1=rs.t[:, h, :].unsqueeze(1).to_broadcast([P, 4, CH]), op=ALU.mult),
                    reads=[tmpb.b, rs.b], writes=[og.b])
            if c % 8 == 7:
                bsl = slice(blk * 512, (blk + 1) * 512)
                S.dma(ogv[:, :, bsl], og.t[:], reads=[og.b], writes=[ogT.b])

        stage_state(0)
        for c in range(NCH):
            stage_read_pe(c)
            if c + 1 < NCH:
                stage_state(c + 1)
            stage_read_dve(c)
    S.barrier()
    S.sc_end()
    with ExitStack() as es:
        phase_linear(S, ogT, DT, C["gla_w_out"][j], 0, D, "fm", make_resid_sink(S, es, xT))


def topk_threshold(S, acc, work, mx, thr, nk):
    nc = S.nc
    S.op("act", lambda: nc.scalar.copy(out=work.t[:, :nk], in_=acc.t[:, :nk]), reads=[acc.b], writes=[work.b])
    for r8 in range(KSEL // 8):
        S.op("dve", lambda: nc.vector.max(out=mx.t[:], in_=work.t[:, :nk]), reads=[work.b], writes=[mx.b])
        if r8 < KSEL // 8 - 1:
            S.op("dve", lambda: nc.vector.match_replace(
                out=work.t[:, :nk], in_to_replace=mx.t[:], in_values=work.t[:, :nk], imm_value=NEG_REP),
                reads=[work.b, mx.b], writes=[work.b])
    S.op("dve", lambda: nc.vector.tensor_reduce(out=thr.t[:], in_=mx.t[:], axis=mybir.AxisListType.X, op=ALU.min),
         reads=[mx.b], writes=[thr.b])
    S.op("dve", lambda: nc.vector.tensor_scalar_max(thr.t[:], thr.t[:], -1.0e29), reads=[thr.b], writes=[thr.b])


def phase_dsa(S, es0, hT, xT, C, j, dram):
    nc = S.nc
    w_in = C["dsa_w_in"][j]
    qT = dram("d_qT", [4096, T], BF16)
    cTr = dram("d_cTr", [256, T], F32)
    cTn = dram("d_cTn", [256, T], BF16)
    qiT = dram("d_qiT", [2048, T], BF16)
    kiTr = dram("d_kiTr", [P, T], F32)
    kiTn = dram("d_kiTn", [P, T], BF16)
    wi = dram("d_wi", [T, 16], F32)
    oT = dram("d_oT", [2048, T], BF16)
    with ExitStack() as es:
        phase_linear_multi(S, hT, DT, w_in, [
            (0, 4096, "fm", make_store_sink(S, es, qT, "fm", BF16, scale=1.0 / 16.0)),
            (4096, 256, "fm", make_store_sink(S, es, cTr, "fm", F32)),
            (4352, 2048, "fm", make_store_sink(S, es, qiT, "fm", BF16)),
            (6400, 128, "fm", make_store_sink(S, es, kiTr, "fm", F32)),
            (6528, 16, "tm", make_store_sink(S, es, wi, "tm", F32)),
        ])
    phase_norm(S, cTr, cTn, C["dsa_kvn"][j], 2, 256)
    phase_norm(S, kiTr, kiTn, C["dsa_kin"][j], 1, 128)

    NT = T // P
    with ExitStack() as es:
        ident = S.sb(es, [P, P], BF16, uid("did"))
        S.dma(ident.t[:], C["ident"].t, writes=[ident.b], q="pool")
        ones = S.sb(es, [P, P], BF16, uid("dones"))
        S.op("dve", lambda: nc.vector.memset(ones.t[:], 1.0), writes=[ones.b])
        b15 = S.sb(es, [P, 16], F32, uid("db15"))
        S.dma(b15.t[:], C["b15"].t, writes=[b15.b])
        kiT = S.sb(es, [P, T], BF16, uid("dki"))
        S.dma(kiT.t[:], kiTn.t, reads=[kiTn.b], writes=[kiT.b])
        cT = S.sb(es, [P, 2, T], BF16, uid("dcT"))
        S.dma(cT.t[:], cTn.t.rearrange("(k p) t -> p k t", p=P), reads=[cTn.b], writes=[cT.b])
        wis = S.sb(es, [P, NT, 16], F32, uid("dwi"))
        with nc.allow_non_contiguous_dma(reason="small per-token head weights"):
            S.dma(wis.t[:], wi.t.rearrange("(n p) h -> p n h", p=P), reads=[wi.b], writes=[wis.b])
        wuv = S.sb(es, [P, 16, 2, P], BF16, uid("dwuv"))
        for h in range(16):
            S.dma(wuv.t[:, h, :, :], C["dsa_w_uv"][j].t[h].rearrange("(k p) e -> p k e", p=P),
                  writes=[wuv.b], q="pool")
        cn = S.sb(es, [P, NT, 256], BF16, uid("dcn"))
        pG = Rot([S.ps(es, [P, 512], F32, uid("dpg")) for _ in range(2)])
        pGB = Rot([S.ps(es, [P, 512], F32, uid("dpgb")) for _ in range(3)])
        for st in range(NT):
            pt = pG.next()
            for k2 in range(2):
                S.mm(pt, pt.t[:, k2 * P:(k2 + 1) * P], [(cT.t[:, k2, st * P:(st + 1) * P], ident.t[:])],
                     reads=[cT.b, ident.b])
            S.op("act", lambda st=st, pt=pt: nc.scalar.copy(out=cn.t[:, st, :], in_=pt.t[:, 0:256]),
                 reads=[pt.b], writes=[cn.b])
        qi = S.sb(es, [P, 16, QB], BF16, uid("dqi"))
        qq = S.sb(es, [P, 2, 16 * QB], BF16, uid("dq"))
        maskT = S.sb(es, [P, NT * QB], BF16, uid("dmT"))
        acc = S.sb(es, [P, T], F32, uid("dacc"))
        work = S.sb(es, [P, T], F32, uid("dwork"))
        mask = S.sb(es, [P, T], BF16, uid("dmask"))
        mx = S.sb(es, [P, 8], F32, uid("dmx"))
        thr = S.sb(es, [P, 1], F32, uid("dthr"))
        rl = Rot([S.sb(es, [P, 512], F32, uid("drl")) for _ in range(3)])
        Es = Rot([S.sb(es, [P, 512], BF16, uid("dE")) for _ in range(4)])
        Ems = Rot([S.sb(es, [P, 512], BF16, uid("dEm")) for _ in range(4)])
        tl = Rot([S.sb(es, [P, 512], F32, uid("dtl")) for _ in range(2)])
        bn = Rot([S.sb(es, [P, 3, 2, QB], F32, uid("dbn")) for _ in range(3)])
        rden = S.sb(es, [P, 512], F32, uid("drd"))
        obT = S.sb(es, [P, 2, 512], BF16, uid("dob"))
        ohs = Rot([S.sb(es, [P, 512], BF16, uid("doh")) for _ in range(2)])
        pO2 = [S.ps(es, [P, 512], F32, uid("dpO")) for _ in range(2)]
        pD = S.ps(es, [P, 512], F32, uid("dpD"))
        qiv = qiT.t.rearrange("(h p) t -> p h t", p=P)
        qv4 = qT.t.rearrange("(h k p) t -> p k h t", k=2, p=P)
        LA = 2

        negb15 = S.sb(es, [P, 16], F32, uid("dnb15"))
        S.op("dve", lambda: nc.vector.tensor_scalar_mul(negb15.t[:], b15.t[:], -1.0), reads=[b15.b], writes=[negb15.b])
        expbs = Rot([S.sb(es, [P, 3, 2, QB], BF16, uid("dexpb")) for _ in range(2)])
        usb = S.sb(es, [P, 512], F32, uid("dus"))
        dsb = S.sb(es, [P, 512], F32, uid("dds"))

        def load_bias(hp):
            bt = bn.next()
            S.dma(bt.t[:], C["biasN"].t[2 * hp:2 * hp + 2].rearrange("h p a b -> p a h b"), writes=[bt.b])
            return bt

        def adjust_bias(bt, hp):
            eb = expbs.next()
            for a_ in range(2):
                S.op("act", lambda a_=a_: nc.scalar.activation(
                    out=eb.t[:, :, a_, :], in_=bt.t[:, :, a_, :], func=AF.Exp,
                    bias=negb15.t[:, 2 * hp + a_:2 * hp + a_ + 1]),
                    reads=[bt.b, negb15.b], writes=[eb.b])
            return eb

        maskTs = [maskT, S.sb(es, [P, NT * QB], BF16, uid("dmT2"))]
        FMA_ENG = DBG.get("fma_eng", "dve")

        def sel_idx(qb_, qt2):
            mT = maskTs[qb_ % 2]
            q0 = qb_ * QB
            qsl = slice(q0, q0 + QB)
            NS = (q0 + QB) // P
            if qt2 == 0:
                S.dma(qi.t[:], qiv[:, :, qsl], reads=[qiT.b], writes=[qi.b])
                S.op("dve", lambda: nc.vector.memset(mT.t[:, :NS * QB], 0.0), writes=[mT.b])
            qt = q0 // P + qt2
            nk = (qt + 1) * P
            qcol = slice(qt2 * P, (qt2 + 1) * P)
            for h in range(16):
                for k0 in range(0, nk, 512):
                    w_ = min(512, nk - k0)
                    p_ = pG.next()
                    r_ = rl.next()
                    S.mm(p_, p_.t[:, :w_], [(qi.t[:, h, qcol], kiT.t[:, k0:k0 + w_])], reads=[qi.b, kiT.b])
                    S.op("act", lambda: nc.scalar.activation(
                        out=r_.t[:, :w_], in_=p_.t[:, :w_], func=AF.Relu), reads=[p_.b], writes=[r_.b])
                    if h == 0:
                        S.op("dve", lambda: nc.vector.tensor_scalar_mul(
                            acc.t[:, k0:k0 + w_], r_.t[:, :w_], wis.t[:, qt, 0:1]),
                            reads=[r_.b, wis.b], writes=[acc.b])
                    else:
                        S.op("dve", lambda: nc.vector.scalar_tensor_tensor(
                            out=acc.t[:, k0:k0 + w_], in0=r_.t[:, :w_], scalar=wis.t[:, qt, h:h + 1],
                            in1=acc.t[:, k0:k0 + w_], op0=ALU.mult, op1=ALU.add),
                            reads=[r_.b, wis.b, acc.b], writes=[acc.b])
            S.op("dve", lambda: nc.vector.memset(acc.t[0:CH, nk - CH:nk], NEG_ADM), writes=[acc.b])

        def sel_topk(qb_, qt2):
            qt = qb_ * (QB // P) + qt2
            nk = (qt + 1) * P
            if nk > KSEL:
                S.op("dve", lambda: nc.vector.tensor_copy(work.t[:, :nk], acc.t[:, :nk]), reads=[acc.b], writes=[work.b])
                yield
                for r8 in range(KSEL // 8):
                    S.op("dve", lambda: nc.vector.max(out=mx.t[:], in_=work.t[:, :nk]), reads=[work.b], writes=[mx.b])
                    yield
                    if r8 < KSEL // 8 - 1:
                        S.op("dve", lambda: nc.vector.match_replace(
                            out=work.t[:, :nk], in_to_replace=mx.t[:], in_values=work.t[:, :nk], imm_value=NEG_REP),
                            reads=[work.b, mx.b], writes=[work.b])
                        yield
                S.op("dve", lambda: nc.vector.tensor_reduce(out=thr.t[:], in_=mx.t[:], axis=mybir.AxisListType.X,
                                                            op=ALU.min), reads=[mx.b], writes=[thr.b])
                S.op("dve", lambda: nc.vector.tensor_scalar_max(thr.t[:], thr.t[:], -1.0e29),
                     reads=[thr.b], writes=[thr.b])
                S.op("dve", lambda: nc.vector.tensor_scalar(
                    mask.t[:, :nk], acc.t[:, :nk], thr.t[:, 0:1], None, op0=ALU.is_ge),
                    reads=[acc.b, thr.b], writes=[mask.b])
            else:
                S.op("dve", lambda: nc.vector.tensor_scalar(
                    mask.t[:, :nk], acc.t[:, :nk], -1.0e29, None, op0=ALU.is_ge),
                    reads=[acc.b], writes=[mask.b])
            yield

        def sel_maskT(qb_, qt2):
            mT = maskTs[qb_ % 2]
            qt = qb_ * (QB // P) + qt2
            qc0 = qt2 * P
            for st in range(qt + 1):
                pt = pG.next()
                S.mm(pt, pt.t[:, 0:P], [(mask.t[:, st * P:(st + 1) * P], ident.t[:])], reads=[mask.b, ident.b])
                S.op("act", lambda: nc.scalar.copy(
                    out=mT.t[:, st * QB + qc0:st * QB + qc0 + P], in_=pt.t[:, 0:P]),
                    reads=[pt.b], writes=[mT.b])

        def gen_attn(qb_):
            mT = maskTs[qb_ % 2]
            q0 = qb_ * QB
            qsl = slice(q0, q0 + QB)
            NS = (q0 + QB) // P
            for k2 in range(2):
                S.dma(qq.t[:, k2, :].rearrange("p (h q) -> p h q", h=16), qv4[:, k2, :, qsl], reads=[qT.b], writes=[qq.b])
            jb = q0 // P - 1
            bt_next = load_bias(0)
            eb_next = adjust_bias(bt_next, 0)
            pe_ = nc.gpsimd
            for hp in range(8):
                eb = eb_next
                if hp + 1 < 8:
                    bt_next = load_bias(hp + 1)

                def emitL(st):
                    l_ = pGB.next()
                    S.mm(l_, l_.t[:, :], [(cT.t[:, k2, st * P:(st + 1) * P], qq.t[:, k2, hp * 512:(hp + 1) * 512]) for k2 in range(2)],
                         reads=[cT.b, qq.b])
                    return l_
                ls = [emitL(g) for g in range(min(LA, NS))]
                for st in range(NS):
                    l_ = ls[st]
                    if st + LA < NS:
                        ls.append(emitL(st + LA))
                    e_ = Es.next()
                    em = Ems.next()
                    S.op("act", lambda: nc.scalar.activation(out=e_.t[:], in_=l_.t[:], func=AF.Exp),
                         reads=[l_.b], writes=[e_.b])
                    src_ = e_
                    if st >= jb:
                        jrel = st - jb
                        S.op("pool", lambda: pe_.tensor_tensor(
                            out=em.t[:], in0=e_.t[:], in1=eb.t[:, jrel, :, :].rearrange("p a b -> p (a b)"), op=ALU.mult),
                            reads=[e_.b, eb.b], writes=[em.b])
                        src_ = em
                    for a_ in range(2):
                        S.op("pool", lambda a_=a_: pe_.tensor_tensor(
                            out=em.t[:, a_ * QB:(a_ + 1) * QB], in0=src_.t[:, a_ * QB:(a_ + 1) * QB],
                            in1=mT.t[:, st * QB:(st + 1) * QB], op=ALU.mult),
                            reads=[src_.b, mT.b], writes=[em.b])
                    for k2 in range(2):
                        S.mm(pO2[k2], pO2[k2].t[:], [(cn.t[:, st, k2 * P:(k2 + 1) * P], em.t[:])],
                             reads=[cn.b, em.b], first=(st == 0), last=(st == NS - 1))
                    S.mm(pD, pD.t[:], [(ones.t[:], em.t[:])], reads=[ones.b, em.b],
                         first=(st == 0), last=(st == NS - 1))
                    yield
                for k2 in range(2):
                    S.op("act", lambda k2=k2: nc.scalar.copy(out=obT.t[:, k2, :], in_=pO2[k2].t[:]),
                         reads=[pO2[k2].b], writes=[obT.b])
                S.op("act", lambda: nc.scalar.activation(out=dsb.t[:], in_=pD.t[:], func=AF.Ln), reads=[pD.b], writes=[dsb.b])
                u_ = pGB.next()
                for a_ in range(2):
                    h = 2 * hp + a_
                    S.mm(u_, u_.t[:, a_ * QB:(a_ + 1) * QB],
                         [(wuv.t[:, h, k2, :], obT.t[:, k2, a_ * QB:(a_ + 1) * QB]) for k2 in range(2)],
                         reads=[wuv.b, obT.b])
                S.op("act", lambda: nc.scalar.copy(out=usb.t[:], in_=u_.t[:]), reads=[u_.b], writes=[usb.b])
                oh = ohs.next()
                S.op("act", lambda: nc.scalar.activation(out=dsb.t[:], in_=dsb.t[:], func=AF.Exp, scale=-1.0),
                     reads=[dsb.b], writes=[dsb.b])
                S.op("pool", lambda: pe_.tensor_tensor(out=oh.t[:], in0=usb.t[:], in1=dsb.t[:], op=ALU.mult),
                     reads=[usb.b, dsb.b], writes=[oh.b])
                S.dma(oT.t[2 * hp * P:(2 * hp + 2) * P, qsl].rearrange("(a p) q -> p a q", p=P),
                      oh.t[:].rearrange("p (a q) -> p a q", a=2), reads=[oh.b], writes=[oT.b])
                if hp + 1 < 8:
                    eb_next = adjust_bias(bt_next, hp + 1)
                yield

        def run_all(g):
            n = 0
            for _ in g:
                n += 1
            return n

        def count_select(qb_):
            n = 0
            for qt2 in range(QB // P):
                qt = qb_ * (QB // P) + qt2
                nk = (qt + 1) * P
                n += 16 * ((nk + 511) // 512) + (2 * (KSEL // 8) - 1 if nk > KSEL else 0) + 1 + (qt + 1) // 4
            return n

        NQB = T // QB
        NT2 = QB // P
        for t2 in range(NT2):
            sel_idx(0, t2)
            run_all(sel_topk(0, t2))
            sel_maskT(0, t2)
        for qb_ in range(NQB):
            B = gen_attn(qb_)
            NSb = (qb_ * QB + QB) // P
            nBh = (8 // NT2) * (NSb + 1)
            for t2 in range(NT2):
                A = None
                nA = 0
                if qb_ + 1 < NQB:
                    sel_idx(qb_ + 1, t2)
                    A = sel_topk(qb_ + 1, t2)
                    nA = 2 * (KSEL // 8) + 1
                doneA = 0
                for i in range(nBh):
                    next(B, None)
                    if A is not None:
                        tgt = (i + 1) * nA // nBh
                        while doneA < tgt:
                            next(A, None)
                            doneA += 1
                if A is not None:
                    run_all(A)
                    sel_maskT(qb_ + 1, t2)
            run_all(B)
    S.barrier()
    with ExitStack() as es:
        phase_linear(S, oT, DT, C["dsa_w_out"][j], 0, D, "fm", make_resid_sink(S, es, xT))


def build_program(stop_after=None, dump=()):
    nc = bass.Bass("TRN2", target_bir_lowering=False)
    C = {}

    def din(name, shape, dt=F32):
        return TB(nc.dram_tensor(name, list(shape), dt, kind="ExternalInput").ap())

    x_in = din("xT_in", [D, T])
    C["norm_mix"] = [din(f"norm_mix{i}", [P, DT]) for i in range(DEPTH)]
    C["norm_ffn"] = [din(f"norm_ffn{i}", [P, DT]) for i in range(DEPTH)]
    C["norm_final"] = din("norm_final", [P, DT])
    C["gla_w_in"] = [din(f"gla_w_in{j}", [D, GLA_IN]) for j in range(2)]
    C["gla_waug"] = [din(f"gla_waug{j}", [32, 1024]) for j in range(2)]
    C["gla_gn"] = [din(f"gla_gn{j}", [P, 4]) for j in range(2)]
    C["gla_w_out"] = [din(f"gla_w_out{j}", [D, D]) for j in range(2)]
    C["dsa_w_in"] = [din(f"dsa_w_in{j}", [D, DSA_IN]) for j in range(2)]
    C["dsa_kvn"] = [din(f"dsa_kvn{j}", [P, 2]) for j in range(2)]
    C["dsa_kin"] = [din(f"dsa_kin{j}", [P, 1]) for j in range(2)]
    C["dsa_w_uv"] = [din(f"dsa_w_uv{j}", [16, 256, P]) for j in range(2)]
    C["dsa_w_out"] = [din(f"dsa_w_out{j}", [D, D]) for j in range(2)]
    C["ffn_w1"] = [din(f"ffn_w1_{i}", [D, DFF]) for i in range(DEPTH)]
    C["ffn_w3"] = [din(f"ffn_w3_{i}", [D, DFF]) for i in range(DEPTH)]
    C["ffn_w2"] = [din(f"ffn_w2_{i}", [DFF, D]) for i in range(DEPTH)]
    C["mrev"] = din("mrev", [P, P])
    C["cind"] = din("cind", [P, 2])
    C["ident"] = din("ident", [P, P])
    C["b15"] = din("b15", [P, 16])
    C["biasN"] = din("biasN", [16, P, 3, QB])
    outT = TB(nc.dram_tensor("outT", [D, T], F32, kind="ExternalOutput").ap())

    scratch = {}

    def dram(name, shape, dt):
        if name not in scratch:
            scratch[name] = TB(nc.dram_tensor("scr_" + name, list(shape), dt, kind="Internal").ap())
        return scratch[name]

    xT = dram("xT", [D, T], F32)
    hT = dram("hT", [D, T], BF16)
    with ExitStack() as es0:
        S = Sched(nc, es0)
        for k in range(DT):
            S.dma(xT.t[k * P:(k + 1) * P, :], x_in.t[k * P:(k + 1) * P, :], reads=[x_in.b], writes=[xT.b])
        S.barrier()
        nph = 0
        done = False
        for i in range(DEPTH):
            j = i // 2
            phase_norm(S, xT, hT, C["norm_mix"][i], DT, D)
            if i % 2 == 0:
                phase_gla(S, es0, hT, xT, C, j, dram)
            else:
                phase_dsa(S, es0, hT, xT, C, j, dram)
            nph += 1
            if stop_after == nph:
                done = True
                break
            phase_norm(S, xT, hT, C["norm_ffn"][i], DT, D)
            phase_ffn(S, hT, xT, C["ffn_w1"][i], C["ffn_w3"][i], C["ffn_w2"][i])
            nph += 1
            if stop_after == nph:
                done = True
                break
        for nm in dump:
            src_ = scratch[nm]
            dt_ = BF16 if nm in ("d_oT", "d_cTn", "d_qT", "d_kiTn", "d_qiT", "hT") else F32
            dst_ = TB(nc.dram_tensor("dump_" + nm, list(src_.t.shape), dt_, kind="ExternalOutput").ap())
            S.dma(dst_.t, src_.t, reads=[src_.b], writes=[dst_.b])
        if done:
            for k in range(DT):
                S.dma(outT.t[k * P:(k + 1) * P, :], xT.t[k * P:(k + 1) * P, :], reads=[xT.b], writes=[outT.b])
        else:
            phase_norm(S, xT, outT, C["norm_final"], DT, D, dst_f32=True)
        S.barrier()
    return nc


def t5_bucket_np(rel):
    nb = 16
    me = 8
    ret = (rel > 0).astype(np.int32) * nb
    n = np.abs(rel)
    large = me + (np.log(np.maximum(n, 1).astype(np.float32) / me) / math.log(128 / me) * (nb - me)).astype(np.int32)
    large = np.minimum(large, nb - 1)
    return ret + np.where(n < me, n, large)


def prep_inputs(inputs):
    f = lambda a: np.ascontiguousarray(np.asarray(a, dtype=np.float32))
    vecl = lambda v, k: f(np.asarray(v).reshape(k, P).T)
    sh = {}
    for i in range(DEPTH):
        sh[f"norm_mix{i}"] = vecl(inputs["norm_mix"][i], DT)
        sh[f"norm_ffn{i}"] = vecl(inputs["norm_ffn"][i], DT)
        sh[f"ffn_w1_{i}"] = f(inputs["ffn_w1"][i])
        sh[f"ffn_w3_{i}"] = f(inputs["ffn_w3"][i])
        sh[f"ffn_w2_{i}"] = f(inputs["ffn_w2"][i])
    sh["norm_final"] = vecl(inputs["norm_final"], DT)
    for j in range(2):
        sh[f"gla_w_in{j}"] = f(inputs["gla_w_in"][j])
        waug = np.zeros((32, 1024), np.float32)
        waug[0:16] = inputs["gla_w_a2"][j]
        waug[16] = inputs["gla_b_a"][j]
        sh[f"gla_waug{j}"] = waug
        sh[f"gla_gn{j}"] = vecl(inputs["gla_g_norm"][j], 4)
        sh[f"gla_w_out{j}"] = f(inputs["gla_w_out"][j])
        sh[f"dsa_w_in{j}"] = f(inputs["dsa_w_in"][j])
        sh[f"dsa_kvn{j}"] = vecl(inputs["dsa_kv_norm"][j], 2)
        sh[f"dsa_kin{j}"] = vecl(inputs["dsa_kidx_norm"][j], 1)
        sh[f"dsa_w_uv{j}"] = f(inputs["dsa_w_uv"][j])
        sh[f"dsa_w_out{j}"] = f(inputs["dsa_w_out"][j])
    tp = np.arange(P)
    same = (tp[:, None] // CH) == (tp[None, :] // CH)
    sh["mrev"] = (same & (tp[:, None] > tp[None, :])).astype(np.float32)
    sh["cind"] = (tp[:, None] // CH == np.arange(2)[None, :]).astype(np.float32)
    sh["ident"] = np.eye(P, dtype=np.float32)
    rb = np.asarray(inputs["rel_bias"], dtype=np.float32)
    sh["b15"] = f(np.broadcast_to(rb[15][None, :], (P, 16)))
    sl = np.arange(P)[:, None, None]
    jj = np.arange(3)[None, :, None]
    ql = np.arange(QB)[None, None, :]
    rel = (jj - 1) * P + sl - ql
    bidx = t5_bucket_np(rel)
    sh["biasN"] = f(np.transpose(rb[bidx], (3, 0, 1, 2)))
    return sh


def kernel(**inputs):
    shared = prep_inputs(inputs)
    x = np.asarray(inputs["x"], dtype=np.float32)
    in_maps = []
    for b in range(NB):
        m = dict(shared)
        m["xT_in"] = np.ascontiguousarray(x[b].T)
        in_maps.append(m)
    nc = build_program()
    res = run_bass_kernel_spmd(nc, in_maps, core_ids=list(range(NB)))
    out = np.stack([np.ascontiguousarray(res.results[b]["outT"].T) for b in range(NB)], axis=0)
    return out.astype(np.float32)
```

```python
import math
from contextlib import ExitStack

import numpy as np
import concourse.bass as bass
import concourse.mybir as mybir
from concourse.bass_utils import run_bass_kernel_spmd

F32 = mybir.dt.float32
BF16 = mybir.dt.bfloat16
ALU = mybir.AluOpType
AF = mybir.ActivationFunctionType

P = 128
D = 2048
DT = D // P
T = 4096
NB = 4
DEPTH = 4
DFF = 5632
FT = DFF // P
CH = 64
EPS = 1e-6
GLA_IN = 6160
DSA_IN = 6544
KSEL = 256
NEG_ADM = -1.0e30
NEG_REP = -3.0e38
QB = 256
NDQ = 6
DBG = {}
TOPK_R0 = 65536.0
TOPK_NIT = 36


class Buf:
    __slots__ = ("w", "r")

    def __init__(self):
        self.w = None
        self.r = {}


class TB:
    def __init__(self, t):
        self.t = t
        self.b = Buf()


class Sched:
    def __init__(self, nc, es):
        self.nc = nc
        self.eng = {"pe": nc.tensor, "act": nc.scalar, "dve": nc.vector,
                    "pool": nc.gpsimd, "sp": nc.sync}
        names = ["pe", "act", "dve", "pool"]
        names += [f"qsp{i}" for i in range(NDQ)] + [f"qpool{i}" for i in range(NDQ)] + [f"qact{i}" for i in range(NDQ)]
        self.sem = {n: es.enter_context(nc.semaphore("s_" + n)) for n in names}
        self.cnt = dict.fromkeys(names, 0)
        self.seen = {e: dict.fromkeys(names, 0) for e in self.eng}
        self.dq = {"sp": 0, "pool": 0, "act": 0}

    def _wait(self, E, s, c):
        if s == E and c > self.cnt[s]:
            return
        if self.seen[E][s] < c:
            self.eng[E].wait_ge(self.sem[s], c)
            self.seen[E][s] = c

    def op(self, E, fn, reads=(), writes=(), inc=True, dma=False):
        if dma:
            slot = self.dq[E]
            self.dq[E] = (slot + 1) % NDQ
            s = f"q{E}{slot}"
            self._wait(E, s, self.cnt[s])
        else:
            s = E
        for b in reads:
            if b.w:
                self._wait(E, *b.w)
        for b in writes:
            if b.w:
                self._wait(E, *b.w)
            for rs, rc in b.r.items():
                self._wait(E, rs, rc)
        ins = fn()
        if inc:
            step = 16 if dma else 1
            self.cnt[s] += step
            ins.then_inc(self.sem[s], step)
            c = self.cnt[s]
        else:
            c = self.cnt[s] + 1
        for b in writes:
            b.w = (s, c)
            b.r = {}
        for b in reads:
            if b.r.get(s, 0) < c:
                b.r[s] = c
        return ins

    def sc_begin(self, name):
        if DBG.get("scopes"):
            cm = self.nc.named_scope(name)
            cm.__enter__()
            self._scopes = getattr(self, "_scopes", []) + [cm]

    def sc_end(self):
        if DBG.get("scopes"):
            self._scopes.pop().__exit__(None, None, None)

    def barrier(self):
        for E in self.eng:
            for s, c in self.cnt.items():
                if c:
                    self._wait(E, s, c)

    def sb(self, es, shape, dt, name):
        return TB(es.enter_context(self.nc.sbuf_tensor(name, list(shape), dt)))

    def ps(self, es, shape, dt, name):
        return TB(es.enter_context(self.nc.psum_tensor(name, list(shape), dt)))

    def dma(self, out_ap, in_ap, reads=(), writes=(), q="sp"):
        eng = self.eng[q]
        return self.op(q, lambda: eng.dma_start(out=out_ap, in_=in_ap), reads, writes, dma=True)

    def mm(self, out_tb, out_ap, pairs, reads, first=True, last=True):
        n = len(pairs)
        nc = self.nc
        for i, (l, r) in enumerate(pairs):
            st = first and i == 0
            sp = last and i == n - 1
            self.op("pe", lambda l=l, r=r, st=st, sp=sp: nc.tensor.matmul(out_ap, l, r, start=st, stop=sp),
                    reads=reads if i == 0 else (), writes=[out_tb.b], inc=sp)


class Rot:
    def __init__(self, items):
        self.items = items
        self.i = 0

    def next(self):
        it = self.items[self.i % len(self.items)]
        self.i += 1
        return it


_uid = [0]


def uid(p):
    _uid[0] += 1
    return f"{p}{_uid[0]}"


def phase_norm(S, src, dst, gw, KT, nfeat, dst_f32=False):
    S.sc_begin(uid("norm"))
    nc = S.nc
    TBK = 512
    sv = src.t.rearrange("(k p) t -> p k t", p=P)
    dv = dst.t.rearrange("(k p) t -> p k t", p=P)
    with ExitStack() as es:
        g = S.sb(es, [P, KT], F32, uid("ng"))
        ones = S.sb(es, [P, P], BF16, uid("nones"))
        S.dma(g.t[:], gw.t, writes=[g.b])
        S.op("dve", lambda: nc.vector.memset(ones.t[:], 1.0), writes=[ones.b])
        xs = Rot([S.sb(es, [P, KT, TBK], F32, uid("nx")) for _ in range(2)])
        sqs = Rot([S.sb(es, [P, KT, TBK], BF16, uid("nsq")) for _ in range(2)])
        hs = Rot([S.sb(es, [P, KT, TBK], F32 if dst_f32 else BF16, uid("nh")) for _ in range(2)])
        rss = Rot([S.sb(es, [P, TBK], F32, uid("nrs")) for _ in range(2)])
        pss = Rot([S.ps(es, [P, TBK], F32, uid("nps")) for _ in range(2)])
        for tb in range(T // TBK):
            sl = slice(tb * TBK, (tb + 1) * TBK)
            x = xs.next()
            h = hs.next()
            sq = sqs.next()
            rs = rss.next()
            ps = pss.next()
            S.dma(x.t[:], sv[:, :, sl], reads=[src.b], writes=[x.b])
            S.op("act", lambda x=x, sq=sq: nc.scalar.activation(out=sq.t[:], in_=x.t[:], func=AF.Square),
                 reads=[x.b], writes=[sq.b])
            S.mm(ps, ps.t[:], [(ones.t[:], sq.t[:, k, :]) for k in range(KT)], reads=[ones.b, sq.b])
            S.op("act", lambda ps=ps, rs=rs: nc.scalar.activation(out=rs.t[:], in_=ps.t[:], func=AF.Sqrt,
                                                                 bias=EPS, scale=1.0 / nfeat),
                 reads=[ps.b], writes=[rs.b])
            S.op("dve", lambda rs=rs: nc.vector.reciprocal(rs.t[:], rs.t[:]), reads=[rs.b], writes=[rs.b])
            for k in range(KT):
                S.op("dve", lambda k=k, x=x, h=h, rs=rs: nc.vector.scalar_tensor_tensor(
                    out=h.t[:, k, :], in0=x.t[:, k, :], scalar=g.t[:, k:k + 1], in1=rs.t[:],
                    op0=ALU.mult, op1=ALU.mult), reads=[x.b, g.b, rs.b], writes=[h.b])
            S.dma(dv[:, :, sl], h.t[:], reads=[h.b], writes=[dst.b])
    S.barrier()
    S.sc_end()


def phase_linear(S, hT, KT, W, c0, ncols, mode, sink, TSB=2048):
    phase_linear_multi(S, hT, KT, W, [(c0, ncols, mode, sink)], TSB)


def phase_linear_multi(S, hT, KT, W, jobs, TSB=2048):
    nc = S.nc
    S.sc_begin(uid("lin_" + "_".join(f"{m}{c}" for c, _, m, _ in jobs) + "_"))
    hv = hT.t.rearrange("(k p) t -> p k t", p=P)
    wv = W.t.rearrange("(k p) n -> p k n", p=P)
    modes = {m for _, _, m, _ in jobs}
    with ExitStack() as es:
        hsb = S.sb(es, [P, KT, TSB], BF16, uid("lh"))
        wsm = {}
        if "fm" in modes:
            wsm["fm"] = Rot([S.sb(es, [P, KT, 128], BF16, uid("lwf")) for _ in range(3)])
        if "tm" in modes:
            wsm["tm"] = Rot([S.sb(es, [P, KT, 512], BF16, uid("lwt")) for _ in range(2 if len(modes) > 1 else 3)])
        pss = Rot([S.ps(es, [P, 512], F32, uid("lps")) for _ in range(4)])
        for sbk in range(T // TSB):
            ts0 = sbk * TSB
            for q4 in range(4):
                qs = TSB // 4
                S.dma(hsb.t[:, :, q4 * qs:(q4 + 1) * qs], hv[:, :, ts0 + q4 * qs: ts0 + (q4 + 1) * qs],
                      reads=[hT.b], writes=[hsb.b])
            for (c0, ncols, mode, sink) in jobs:
                nw = 128 if mode == "fm" else 512
                for n0 in range(0, ncols, nw):
                    nsz = min(nw, ncols - n0)
                    w = wsm[mode].next()
                    S.dma(w.t[:, :, :nsz], wv[:, :, c0 + n0:c0 + n0 + nsz], reads=[W.b], writes=[w.b], q="pool")
                    if mode == "fm":
                        for t0 in range(0, TSB, 512):
                            ps = pss.next()
                            S.mm(ps, ps.t[:nsz, :], [(w.t[:, k, :nsz], hsb.t[:, k, t0:t0 + 512]) for k in range(KT)],
                                 reads=[w.b, hsb.b])
                            sink(ps, n0, nsz, ts0 + t0)
                    else:
                        for t0 in range(0, TSB, P):
                            ps = pss.next()
                            S.mm(ps, ps.t[:, :nsz], [(hsb.t[:, k, t0:t0 + P], w.t[:, k, :nsz]) for k in range(KT)],
                                 reads=[w.b, hsb.b])
                            sink(ps, ts0 + t0, n0, nsz)
        for (_, _, _, sink) in jobs:
            if hasattr(sink, "flush"):
                sink.flush()
    S.barrier()
    S.sc_end()


def make_store_sink(S, es, dst, mode, dt, func=AF.Copy, scale=1.0, roff=0):
    nc = S.nc
    obs = Rot([S.sb(es, [P, 512], dt, uid("so")) for _ in range(3)])

    def sink_fm(ps, n0, nsz, t0):
        o = obs.next()
        S.op("act", lambda: nc.scalar.activation(out=o.t[:nsz, :], in_=ps.t[:nsz, :], func=func, scale=scale),
             reads=[ps.b], writes=[o.b])
        S.dma(dst.t[roff + n0:roff + n0 + nsz, t0:t0 + 512], o.t[:nsz, :], reads=[o.b], writes=[dst.b])

    def sink_tm(ps, t0, n0, nsz):
        o = obs.next()
        S.op("act", lambda: nc.scalar.activation(out=o.t[:, :nsz], in_=ps.t[:, :nsz], func=func, scale=scale),
             reads=[ps.b], writes=[o.b])
        S.dma(dst.t[t0:t0 + P, roff + n0:roff + n0 + nsz], o.t[:, :nsz], reads=[o.b], writes=[dst.b])

    return sink_fm if mode == "fm" else sink_tm


def make_resid_sink(S, es, xT, slab=2048):
    nc = S.nc
    xin = Rot([S.sb(es, [P, slab], F32, uid("rx")) for _ in range(2)])
    cur = {}

    def sink(ps, n0, nsz, t0):
        off = t0 % slab
        if off == 0:
            xt = xin.next()
            S.dma(xt.t[:nsz, :], xT.t[n0:n0 + nsz, t0:t0 + slab], reads=[xT.b], writes=[xt.b])
            cur["xt"] = xt
        xt = cur["xt"]
        S.op("dve", lambda: nc.vector.tensor_tensor(out=xt.t[:nsz, off:off + 512], in0=xt.t[:nsz, off:off + 512],
                                                    in1=ps.t[:nsz, :], op=ALU.add), reads=[xt.b, ps.b], writes=[xt.b])
        if off + 512 == slab:
            S.dma(xT.t[n0:n0 + nsz, t0 + 512 - slab:t0 + 512], xt.t[:nsz, :], reads=[xt.b], writes=[xT.b])

    return sink


def phase_ffn(S, hT, xT, w1, w3, w2):
    nc = S.nc
    S.sc_begin(uid("ffn"))
    TSB = 1024
    hv = hT.t.rearrange("(k p) t -> p k t", p=P)
    w1v = w1.t.rearrange("(k p) n -> p k n", p=P)
    w3v = w3.t.rearrange("(k p) n -> p k n", p=P)
    w2v = w2.t.rearrange("(f p) n -> p f n", p=P)
    with ExitStack() as es:
        hsb = S.sb(es, [P, DT, TSB], BF16, uid("fh"))
        gT = S.sb(es, [P, FT, TSB], BF16, uid("fg"))
        w13 = Rot([(S.sb(es, [P, DT, P], BF16, uid("fw1")), S.sb(es, [P, DT, P], BF16, uid("fw3")))
                   for _ in range(3)])
        w2s = Rot([S.sb(es, [P, FT, P], BF16, uid("fw2")) for _ in range(2)])
        s1s = Rot([S.sb(es, [P, 512], F32, uid("fs1")) for _ in range(2)])
        p1s = Rot([S.ps(es, [P, 512], F32, uid("fp1")) for _ in range(2)])
        p3s = Rot([S.ps(es, [P, 512], F32, uid("fp3")) for _ in range(2)])
        pys = Rot([S.ps(es, [P, 512], F32, uid("fpy")) for _ in range(2)])
        rsink = make_resid_sink(S, es, xT, slab=TSB)
        for sbk in range(T // TSB):
            ts0 = sbk * TSB
            for q4 in range(2):
                qs = TSB // 2
                S.dma(hsb.t[:, :, q4 * qs:(q4 + 1) * qs], hv[:, :, ts0 + q4 * qs:ts0 + (q4 + 1) * qs],
                      reads=[hT.b], writes=[hsb.b])
            for ft in range(FT):
                a, b = w13.next()
                S.dma(a.t[:], w1v[:, :, ft * P:(ft + 1) * P], reads=[w1.b], writes=[a.b], q="pool")
                S.dma(b.t[:], w3v[:, :, ft * P:(ft + 1) * P], reads=[w3.b], writes=[b.b], q="pool")
                for nb in range(TSB // 512):
                    tsl = slice(nb * 512, (nb + 1) * 512)
                    p1 = p1s.next()
                    p3 = p3s.next()
                    s1 = s1s.next()
                    S.mm(p1, p1.t[:], [(a.t[:, k, :], hsb.t[:, k, tsl]) for k in range(DT)], reads=[a.b, hsb.b])
                    S.mm(p3, p3.t[:], [(b.t[:, k, :], hsb.t[:, k, tsl]) for k in range(DT)], reads=[b.b, hsb.b])
                    S.op("act", lambda p1=p1, s1=s1: nc.scalar.activation(out=s1.t[:], in_=p1.t[:], func=AF.Silu),
                         reads=[p1.b], writes=[s1.b])
                    S.op("dve", lambda s1=s1, p3=p3, ft=ft, tsl=tsl: nc.vector.tensor_tensor(
                        out=gT.t[:, ft, tsl], in0=s1.t[:], in1=p3.t[:], op=ALU.mult),
                        reads=[s1.b, p3.b], writes=[gT.b])
            for dt_ in range(DT):
                w = w2s.next()
                S.dma(w.t[:, :FT // 2, :], w2v[:, :FT // 2, dt_ * P:(dt_ + 1) * P], reads=[w2.b], writes=[w.b], q="pool")
                S.dma(w.t[:, FT // 2:, :], w2v[:, FT // 2:, dt_ * P:(dt_ + 1) * P], reads=[w2.b], writes=[w.b], q="pool")
                for nb in range(TSB // 512):
                    tsl = slice(nb * 512, (nb + 1) * 512)
                    py = pys.next()
                    S.mm(py, py.t[:], [(w.t[:, f, :], gT.t[:, f, tsl]) for f in range(FT)], reads=[w.b, gT.b])
                    rsink(py, dt_ * P, P, ts0 + nb * 512)
    S.barrier()
    S.sc_end()


def phase_gla(S, es0, hT, xT, C, j, dram):
    nc = S.nc
    w_in = C["gla_w_in"][j]
    qT = dram("g_qT", [1024, T], BF16)
    sgT = dram("g_sgT", [2048, T], BF16)
    aT = dram("g_aT", [16, T], F32)
    ktm = dram("g_k", [T, 1024], F32)
    vtm = dram("g_v", [T, 2048], BF16)
    kd = dram("g_kd", [T, 1024], BF16)
    decS = dram("g_decS", [P, (T // P) * 16], F32)
    ogT = dram("g_ogT", [2048, T], BF16)
    with ExitStack() as es:
        phase_linear_multi(S, hT, DT, w_in, [
            (0, 1024, "fm", make_store_sink(S, es, qT, "fm", BF16, scale=1.0 / 16.0)),
            (4096, 2048, "fm", make_store_sink(S, es, sgT, "fm", BF16, func=AF.Silu)),
            (6144, 16, "fm", make_store_sink(S, es, aT, "fm", F32)),
            (1024, 1024, "tm", make_store_sink(S, es, ktm, "tm", F32)),
            (2048, 2048, "tm", make_store_sink(S, es, vtm, "tm", BF16)),
        ])
    S.sc_begin(uid("gla_gate"))
    with ExitStack() as es:
        waug = S.sb(es, [32, 1024], F32, uid("gw"))
        S.dma(waug.t[:], C["gla_waug"][j].t, writes=[waug.b])
        mrev = S.sb(es, [P, P], F32, uid("gm"))
        S.dma(mrev.t[:], C["mrev"].t, writes=[mrev.b])
        cind = S.sb(es, [P, 2], F32, uid("gci"))
        S.dma(cind.t[:], C["cind"].t, writes=[cind.b])
        aaug = Rot([S.sb(es, [32, P], F32, uid("ga")) for _ in range(2)])
        for a in aaug.items:
            S.op("dve", lambda a=a: nc.vector.memset(a.t[:], 1.0), writes=[a.b])
        Ls = Rot([S.sb(es, [P, 1024], F32, uid("gL")) for _ in range(2)])
        dec = S.sb(es, [P, 1024], F32, uid("gdec"))
        kts = Rot([S.sb(es, [P, 1024], F32, uid("gk")) for _ in range(2)])
        kds = Rot([S.sb(es, [P, 1024], BF16, uid("gkd")) for _ in range(2)])
        dcs = S.sb(es, [P, (T // P) * 16], F32, uid("gdcs"))
        pz = Rot([S.ps(es, [P, 512], F32, uid("gpz")) for _ in range(4)])
        pc = S.ps(es, [P, 16], F32, uid("gpc"))
        for tt in range(T // P):
            tsl = slice(tt * P, (tt + 1) * P)
            a = aaug.next()
            L = Ls.next()
            kt_ = kts.next()
            kdt = kds.next()
            S.dma(a.t[0:16, :], aT.t[:, tsl], reads=[aT.b], writes=[a.b])
            S.dma(kt_.t[:], ktm.t[tsl, :], reads=[ktm.b], writes=[kt_.b])
            for hb in range(2):
                z = pz.next()
                S.mm(z, z.t[:], [(a.t[:, :], waug.t[:, hb * 512:(hb + 1) * 512])], reads=[a.b, waug.b])
                S.op("act", lambda z=z, L=L, hb=hb: nc.scalar.activation(
                    out=L.t[:, hb * 512:(hb + 1) * 512], in_=z.t[:], func=AF.Exp, scale=-1.0),
                    reads=[z.b], writes=[L.b])
            S.op("act", lambda L=L: nc.scalar.activation(out=L.t[:], in_=L.t[:], func=AF.Ln, bias=1.0),
                 reads=[L.b], writes=[L.b])
            for hb in range(2):
                z = pz.next()
                S.mm(z, z.t[:], [(mrev.t[:], L.t[:, hb * 512:(hb + 1) * 512])], reads=[mrev.b, L.b])
                S.op("act", lambda z=z, hb=hb: nc.scalar.activation(
                    out=dec.t[:, hb * 512:(hb + 1) * 512], in_=z.t[:], func=AF.Exp, scale=-1.0 / 16.0),
                    reads=[z.b], writes=[dec.b])
            S.op("dve", lambda kt_=kt_, kdt=kdt: nc.vector.tensor_tensor(out=kdt.t[:], in0=kt_.t[:], in1=dec.t[:],
                                                                          op=ALU.mult),
                 reads=[kt_.b, dec.b], writes=[kdt.b])
            S.dma(kd.t[tsl, :], kdt.t[:], reads=[kdt.b], writes=[kd.b])
            for k8 in range(8):
                S.mm(pc, pc.t[:, k8 * 2:k8 * 2 + 2], [(L.t[:, k8 * P:(k8 + 1) * P], cind.t[:])], reads=[L.b, cind.b])
            S.op("act", lambda tt=tt: nc.scalar.activation(out=dcs.t[:, tt * 16:(tt + 1) * 16], in_=pc.t[:],
                                                           func=AF.Exp, scale=-1.0 / 16.0),
                 reads=[pc.b], writes=[dcs.b])
        S.dma(decS.t, dcs.t[:], reads=[dcs.b], writes=[decS.b])
    S.barrier()
    S.sc_end()
    S.sc_begin(uid("gla_scan"))
    with ExitStack() as es:
        dcs = S.sb(es, [P, (T // P) * 16], F32, uid("sdcs"))
        S.dma(dcs.t[:], decS.t, reads=[decS.b], writes=[dcs.b])
        gn = S.sb(es, [P, 4], F32, uid("sgn"))
        S.dma(gn.t[:], C["gla_gn"][j].t, writes=[gn.b])
        ones = S.sb(es, [P, P], F32, uid("sones"))
        S.op("dve", lambda: nc.vector.memset(ones.t[:], 1.0), writes=[ones.b])
        St = [S.sb(es, [P, 512], F32, uid("sS")) for _ in range(8)]
        Sb = [S.sb(es, [P, 512], BF16, uid("sSb")) for _ in range(8)]
        for s_ in St:
            S.op("dve", lambda s_=s_: nc.vector.memset(s_.t[:], 0.0), writes=[s_.b])
        qbs = Rot([S.sb(es, [P, 8, 512], BF16, uid("sq")) for _ in range(2)])
        sgs = Rot([S.sb(es, [P, 16, 512], BF16, uid("ssg")) for _ in range(2)])
        ogs = Rot([S.sb(es, [P, 16, 512], BF16, uid("sog")) for _ in range(2)])
        kdb = Rot([S.sb(es, [P, 1024], BF16, uid("skd")) for _ in range(2)])
        vb = Rot([S.sb(es, [P, 2048], BF16, uid("sv")) for _ in range(2)])
        sq = S.sb(es, [P, 16, CH], F32, uid("ssq"))
        rs = S.sb(es, [P, 4, CH], F32, uid("srs"))
        tmpb = S.sb(es, [P, 16, CH], F32, uid("stmpb"))
        pu = Rot([S.ps(es, [P, 512], F32, uid("spu")) for _ in range(3)])
        po = [S.ps(es, [P, 8, CH], F32, uid("spo")) for _ in range(2)]
        pss = S.ps(es, [P, 4, CH], F32, uid("spss"))
        qv = qT.t.rearrange("(k p) t -> p k t", p=P)
        sgv = sgT.t.rearrange("(k p) t -> p k t", p=P)
        ogv = ogT.t.rearrange("(k p) t -> p k t", p=P)
        NCH = T // CH
        Sb2 = [Sb, [S.sb(es, [P, 512], BF16, uid("sSb2")) for _ in range(8)]]
        po2 = [po, [S.ps(es, [P, 8, CH], F32, uid("spo2")) for _ in range(2)]]
        tiles = {}
        blocks = {}

        def get_tile(tt):
            if tt not in tiles:
                tsl = slice(tt * P, (tt + 1) * P)
                kdt = kdb.next()
                vt_ = vb.next()
                S.dma(kdt.t[:], kd.t[tsl, :], reads=[kd.b], writes=[kdt.b])
                S.dma(vt_.t[:], vtm.t[tsl, :], reads=[vtm.b], writes=[vt_.b])
                tiles.clear()
                tiles[tt] = (kdt, vt_)
            return tiles[tt]

        def get_block(blk):
            if blk not in blocks:
                bsl = slice(blk * 512, (blk + 1) * 512)
                qb = qbs.next()
                sg = sgs.next()
                og = ogs.next()
                S.dma(qb.t[:], qv[:, :, bsl], reads=[qT.b], writes=[qb.b])
                S.dma(sg.t[:], sgv[:, :, bsl], reads=[sgT.b], writes=[sg.b])
                for i16 in range(16):
                    S.op("dve", lambda i16=i16: nc.vector.tensor_scalar_mul(
                        sg.t[:, i16, :], sg.t[:, i16, :], gn.t[:, i16 % 4:i16 % 4 + 1]),
                        reads=[sg.b, gn.b], writes=[sg.b])
                blocks.clear()
                blocks[blk] = (qb, sg, og)
            return blocks[blk]

        def stage_state(c):
            tt, c2 = c // 2, c % 2
            cs = c2 * CH
            kdt, vt_ = get_tile(tt)
            sbs = Sb2[c % 2]
            for h in range(4):
                for kt2 in range(2):
                    i8 = h * 2 + kt2
                    u = pu.next()
                    S.mm(u, u.t[:], [(kdt.t[cs:cs + CH, i8 * P:(i8 + 1) * P], vt_.t[cs:cs + CH, h * 512:(h + 1) * 512])],
                         reads=[kdt.b, vt_.b])
                    didx = tt * 16 + i8 * 2 + c2
                    S.op("dve", lambda: nc.vector.scalar_tensor_tensor(
                        out=St[i8].t[:], in0=St[i8].t[:], scalar=dcs.t[:, didx:didx + 1], in1=u.t[:],
                        op0=ALU.mult, op1=ALU.add), reads=[St[i8].b, dcs.b, u.b], writes=[St[i8].b])
                    S.op("act", lambda: nc.scalar.copy(out=sbs[i8].t[:], in_=St[i8].t[:]),
                         reads=[St[i8].b], writes=[sbs[i8].b])

        def stage_read_pe(c):
            blk = c // 8
            qb, sg, og = get_block(blk)
            ccol = slice((c % 8) * CH, (c % 8) * CH + CH)
            sbs = Sb2[c % 2]
            pp = po2[c % 2]
            for h in range(4):
                for vt4 in range(4):
                    i16 = h * 4 + vt4
                    pt = pp[i16 // 8]
                    S.mm(pt, pt.t[:, i16 % 8, :],
                         [(sbs[h * 2 + kt2].t[:, vt4 * P:(vt4 + 1) * P], qb.t[:, h * 2 + kt2, ccol]) for kt2 in range(2)],
                         reads=[sbs[h * 2].b, sbs[h * 2 + 1].b, qb.b])
            for half in range(2):
                S.op("act", lambda half=half: nc.scalar.activation(
                    out=sq.t[:, half * 8:(half + 1) * 8, :], in_=pp[half].t[:], func=AF.Square),
                    reads=[pp[half].b], writes=[sq.b])
            for h in range(4):
                S.mm(pss, pss.t[:, h, :], [(ones.t[:], sq.t[:, h * 4 + v4, :]) for v4 in range(4)],
                     reads=[ones.b, sq.b])
            S.op("act", lambda: nc.scalar.activation(out=rs.t[:], in_=pss.t[:], func=AF.Sqrt,
                                                     bias=EPS, scale=1.0 / 512.0),
                 reads=[pss.b], writes=[rs.b])

        def stage_read_dve(c):
            blk = c // 8
            qb, sg, og = get_block(blk)
            ccol = slice((c % 8) * CH, (c % 8) * CH + CH)
            pp = po2[c % 2]
            S.op("dve", lambda: nc.vector.reciprocal(rs.t[:], rs.t[:]), reads=[rs.b], writes=[rs.b])
            for half in range(2):
                S.op("dve", lambda half=half: nc.vector.tensor_tensor(
                    out=tmpb.t[:, half * 8:(half + 1) * 8, :], in0=pp[half].t[:],
                    in1=sg.t[:, half * 8:(half + 1) * 8, ccol], op=ALU.mult),
                    reads=[pp[half].b, sg.b], writes=[tmpb.b])
            for h in range(4):
                S.op("dve", lambda h=h: nc.vector.tensor_tensor(
                    out=og.t[:, h * 4:(h + 1) * 4, ccol], in0=tmpb.t[:, h * 4:(h + 1) * 4, :],
                    in1=rs.t[:, h, :].unsqueeze(1).to_broadcast([P, 4, CH]), op=ALU.mult),
                    reads=[tmpb.b, rs.b], writes=[og.b])
            if c % 8 == 7:
                bsl = slice(blk * 512, (blk + 1) * 512)
                S.dma(ogv[:, :, bsl], og.t[:], reads=[og.b], writes=[ogT.b])

        stage_state(0)
        for c in range(NCH):
            stage_read_pe(c)
            if c + 1 < NCH:
                stage_state(c + 1)
            stage_read_dve(c)
    S.barrier()
    S.sc_end()
    with ExitStack() as es:
        phase_linear(S, ogT, DT, C["gla_w_out"][j], 0, D, "fm", make_resid_sink(S, es, xT))


def topk_threshold(S, acc, work, mx, thr, nk):
    nc = S.nc
    S.op("act", lambda: nc.scalar.copy(out=work.t[:, :nk], in_=acc.t[:, :nk]), reads=[acc.b], writes=[work.b])
    for r8 in range(KSEL // 8):
        S.op("dve", lambda: nc.vector.max(out=mx.t[:], in_=work.t[:, :nk]), reads=[work.b], writes=[mx.b])
        if r8 < KSEL // 8 - 1:
            S.op("dve", lambda: nc.vector.match_replace(
                out=work.t[:, :nk], in_to_replace=mx.t[:], in_values=work.t[:, :nk], imm_value=NEG_REP),
                reads=[work.b, mx.b], writes=[work.b])
    S.op("dve", lambda: nc.vector.tensor_reduce(out=thr.t[:], in_=mx.t[:], axis=mybir.AxisListType.X, op=ALU.min),
         reads=[mx.b], writes=[thr.b])
    S.op("dve", lambda: nc.vector.tensor_scalar_max(thr.t[:], thr.t[:], -1.0e29), reads=[thr.b], writes=[thr.b])


def phase_dsa(S, es0, hT, xT, C, j, dram):
    nc = S.nc
    w_in = C["dsa_w_in"][j]
    qT = dram("d_qT", [4096, T], BF16)
    cTr = dram("d_cTr", [256, T], F32)
    cTn = dram("d_cTn", [256, T], BF16)
    qiT = dram("d_qiT", [2048, T], BF16)
    kiTr = dram("d_kiTr", [P, T], F32)
    kiTn = dram("d_kiTn", [P, T], BF16)
    wi = dram("d_wi", [T, 16], F32)
    oT = dram("d_oT", [2048, T], BF16)
    with ExitStack() as es:
        phase_linear_multi(S, hT, DT, w_in, [
            (0, 4096, "fm", make_store_sink(S, es, qT, "fm", BF16, scale=1.0 / 16.0)),
            (4096, 256, "fm", make_store_sink(S, es, cTr, "fm", F32)),
            (4352, 2048, "fm", make_store_sink(S, es, qiT, "fm", BF16)),
            (6400, 128, "fm", make_store_sink(S, es, kiTr, "fm", F32)),
            (6528, 16, "tm", make_store_sink(S, es, wi, "tm", F32)),
        ])
    phase_norm(S, cTr, cTn, C["dsa_kvn"][j], 2, 256)
    phase_norm(S, kiTr, kiTn, C["dsa_kin"][j], 1, 128)

    NT = T // P
    with ExitStack() as es:
        ident = S.sb(es, [P, P], BF16, uid("did"))
        S.dma(ident.t[:], C["ident"].t, writes=[ident.b], q="pool")
        ones = S.sb(es, [P, P], BF16, uid("dones"))
        S.op("dve", lambda: nc.vector.memset(ones.t[:], 1.0), writes=[ones.b])
        b15 = S.sb(es, [P, 16], F32, uid("db15"))
        S.dma(b15.t[:], C["b15"].t, writes=[b15.b])
        kiT = S.sb(es, [P, T], BF16, uid("dki"))
        S.dma(kiT.t[:], kiTn.t, reads=[kiTn.b], writes=[kiT.b])
        cT = S.sb(es, [P, 2, T], BF16, uid("dcT"))
        S.dma(cT.t[:], cTn.t.rearrange("(k p) t -> p k t", p=P), reads=[cTn.b], writes=[cT.b])
        wis = S.sb(es, [P, NT, 16], F32, uid("dwi"))
        with nc.allow_non_contiguous_dma(reason="small per-token head weights"):
            S.dma(wis.t[:], wi.t.rearrange("(n p) h -> p n h", p=P), reads=[wi.b], writes=[wis.b])
        wuv = S.sb(es, [P, 16, 2, P], BF16, uid("dwuv"))
        for h in range(16):
            S.dma(wuv.t[:, h, :, :], C["dsa_w_uv"][j].t[h].rearrange("(k p) e -> p k e", p=P),
                  writes=[wuv.b], q="pool")
        cn = S.sb(es, [P, NT, 256], BF16, uid("dcn"))
        pG = Rot([S.ps(es, [P, 512], F32, uid("dpg")) for _ in range(2)])
        pGB = Rot([S.ps(es, [P, 512], F32, uid("dpgb")) for _ in range(3)])
        for st in range(NT):
            pt = pG.next()
            for k2 in range(2):
                S.mm(pt, pt.t[:, k2 * P:(k2 + 1) * P], [(cT.t[:, k2, st * P:(st + 1) * P], ident.t[:])],
                     reads=[cT.b, ident.b])
            S.op("act", lambda st=st, pt=pt: nc.scalar.copy(out=cn.t[:, st, :], in_=pt.t[:, 0:256]),
                 reads=[pt.b], writes=[cn.b])
        qi = S.sb(es, [P, 16, QB], BF16, uid("dqi"))
        qq = S.sb(es, [P, 2, 16 * QB], BF16, uid("dq"))
        maskT = S.sb(es, [P, NT * QB], BF16, uid("dmT"))
        acc = S.sb(es, [P, T], F32, uid("dacc"))
        junk = S.sb(es, [P, T], BF16, uid("djunk"))
        lo_ = S.sb(es, [P, 1], F32, uid("dlo"))
        mid_ = S.sb(es, [P, 1], F32, uid("dmid"))
        cnt_ = S.sb(es, [P, 1], F32, uid("dcnt"))
        ged_ = S.sb(es, [P, 1], F32, uid("dged"))
        mask = S.sb(es, [P, T], BF16, uid("dmask"))
        mx = S.sb(es, [P, 8], F32, uid("dmx"))
        thr = S.sb(es, [P, 1], F32, uid("dthr"))
        rl = Rot([S.sb(es, [P, 512], F32, uid("drl")) for _ in range(3)])
        Es = Rot([S.sb(es, [P, 512], BF16, uid("dE")) for _ in range(4)])
        Ems = Rot([S.sb(es, [P, 512], BF16, uid("dEm")) for _ in range(4)])
        tl = Rot([S.sb(es, [P, 512], F32, uid("dtl")) for _ in range(2)])
        bn = Rot([S.sb(es, [P, 3, 2, QB], F32, uid("dbn")) for _ in range(3)])
        rden = S.sb(es, [P, 512], F32, uid("drd"))
        obT = S.sb(es, [P, 2, 512], BF16, uid("dob"))
        ohs = Rot([S.sb(es, [P, 512], BF16, uid("doh")) for _ in range(2)])
        pO2 = [S.ps(es, [P, 512], F32, uid("dpO")) for _ in range(2)]
        pD = S.ps(es, [P, 512], F32, uid("dpD"))
        qiv = qiT.t.rearrange("(h p) t -> p h t", p=P)
        qv4 = qT.t.rearrange("(h k p) t -> p k h t", k=2, p=P)
        LA = 2

        negb15 = S.sb(es, [P, 16], F32, uid("dnb15"))
        S.op("dve", lambda: nc.vector.tensor_scalar_mul(negb15.t[:], b15.t[:], -1.0), reads=[b15.b], writes=[negb15.b])
        expbs = Rot([S.sb(es, [P, 3, 2, QB], BF16, uid("dexpb")) for _ in range(2)])
        usb = S.sb(es, [P, 512], F32, uid("dus"))
        dsb = S.sb(es, [P, 512], F32, uid("dds"))

        def load_bias(hp):
            bt = bn.next()
            S.dma(bt.t[:], C["biasN"].t[2 * hp:2 * hp + 2].rearrange("h p a b -> p a h b"), writes=[bt.b])
            return bt

        def adjust_bias(bt, hp):
            eb = expbs.next()
            for a_ in range(2):
                S.op("act", lambda a_=a_: nc.scalar.activation(
                    out=eb.t[:, :, a_, :], in_=bt.t[:, :, a_, :], func=AF.Exp,
                    bias=negb15.t[:, 2 * hp + a_:2 * hp + a_ + 1]),
                    reads=[bt.b, negb15.b], writes=[eb.b])
            return eb

        maskTs = [maskT, S.sb(es, [P, NT * QB], BF16, uid("dmT2"))]
        FMA_ENG = DBG.get("fma_eng", "dve")

        def sel_idx(qb_, qt2):
            mT = maskTs[qb_ % 2]
            q0 = qb_ * QB
            qsl = slice(q0, q0 + QB)
            NS = (q0 + QB) // P
            if qt2 == 0:
                S.dma(qi.t[:], qiv[:, :, qsl], reads=[qiT.b], writes=[qi.b])
                S.op("dve", lambda: nc.vector.memset(mT.t[:, :NS * QB], 0.0), writes=[mT.b])
            qt = q0 // P + qt2
            nk = (qt + 1) * P
            qcol = slice(qt2 * P, (qt2 + 1) * P)
            for h in range(16):
                for k0 in range(0, nk, 512):
                    w_ = min(512, nk - k0)
                    p_ = pG.next()
                    r_ = rl.next()
                    S.mm(p_, p_.t[:, :w_], [(qi.t[:, h, qcol], kiT.t[:, k0:k0 + w_])], reads=[qi.b, kiT.b])
                    S.op("act", lambda: nc.scalar.activation(
                        out=r_.t[:, :w_], in_=p_.t[:, :w_], func=AF.Relu), reads=[p_.b], writes=[r_.b])
                    if h == 0:
                        S.op("dve", lambda: nc.vector.tensor_scalar_mul(
                            acc.t[:, k0:k0 + w_], r_.t[:, :w_], wis.t[:, qt, 0:1]),
                            reads=[r_.b, wis.b], writes=[acc.b])
                    else:
                        S.op("dve", lambda: nc.vector.scalar_tensor_tensor(
                            out=acc.t[:, k0:k0 + w_], in0=r_.t[:, :w_], scalar=wis.t[:, qt, h:h + 1],
                            in1=acc.t[:, k0:k0 + w_], op0=ALU.mult, op1=ALU.add),
                            reads=[r_.b, wis.b, acc.b], writes=[acc.b])
            S.op("dve", lambda: nc.vector.memset(acc.t[0:CH, nk - CH:nk], NEG_ADM), writes=[acc.b])

        def sel_topk(qb_, qt2):
            qt = qb_ * (QB // P) + qt2
            nk = (qt + 1) * P
            if nk > KSEL:
                S.op("dve", lambda: nc.vector.memset(lo_.t[:], -TOPK_R0), writes=[lo_.b])
                d = 2.0 * TOPK_R0
                for it in range(TOPK_NIT):
                    d *= 0.5
                    S.op("dve", lambda d=d: nc.vector.tensor_scalar_add(mid_.t[:], lo_.t[:], d), reads=[lo_.b], writes=[mid_.b])
                    S.op("dve", lambda: nc.vector.memset(cnt_.t[:], 0.0), writes=[cnt_.b])
                    yield
                    S.op("dve", lambda: nc.vector.tensor_scalar(
                        junk.t[:, :nk], acc.t[:, :nk], mid_.t[:, 0:1], 0.0, op0=ALU.is_ge, op1=ALU.add,
                        accum_out=cnt_.t[:, 0:1]), reads=[acc.b, mid_.b, cnt_.b], writes=[junk.b, cnt_.b])
                    yield
                    S.op("dve", lambda d=d: nc.vector.tensor_scalar(
                        ged_.t[:], cnt_.t[:], KSEL - 0.5, d, op0=ALU.is_ge, op1=ALU.mult),
                        reads=[cnt_.b], writes=[ged_.b])
                    S.op("dve", lambda: nc.vector.tensor_tensor(out=lo_.t[:], in0=lo_.t[:], in1=ged_.t[:], op=ALU.add),
                         reads=[lo_.b, ged_.b], writes=[lo_.b])
                    yield
                S.op("dve", lambda: nc.vector.tensor_scalar(
                    mask.t[:, :nk], acc.t[:, :nk], lo_.t[:, 0:1], None, op0=ALU.is_ge),
                    reads=[acc.b, lo_.b], writes=[mask.b])
            else:
                S.op("dve", lambda: nc.vector.tensor_scalar(
                    mask.t[:, :nk], acc.t[:, :nk], -1.0e29, None, op0=ALU.is_ge),
                    reads=[acc.b], writes=[mask.b])
            yield

        def sel_maskT(qb_, qt2):
            mT = maskTs[qb_ % 2]
            qt = qb_ * (QB // P) + qt2
            qc0 = qt2 * P
            for st in range(qt + 1):
                pt = pG.next()
                S.mm(pt, pt.t[:, 0:P], [(mask.t[:, st * P:(st + 1) * P], ident.t[:])], reads=[mask.b, ident.b])
                S.op("act", lambda: nc.scalar.copy(
                    out=mT.t[:, st * QB + qc0:st * QB + qc0 + P], in_=pt.t[:, 0:P]),
                    reads=[pt.b], writes=[mT.b])

        def gen_attn(qb_):
            mT = maskTs[qb_ % 2]
            q0 = qb_ * QB
            qsl = slice(q0, q0 + QB)
            NS = (q0 + QB) // P
            for k2 in range(2):
                S.dma(qq.t[:, k2, :].rearrange("p (h q) -> p h q", h=16), qv4[:, k2, :, qsl], reads=[qT.b], writes=[qq.b])
            jb = q0 // P - 1
            bt_next = load_bias(0)
            eb_next = adjust_bias(bt_next, 0)
            pe_ = nc.gpsimd
            for hp in range(8):
                eb = eb_next
                if hp + 1 < 8:
                    bt_next = load_bias(hp + 1)

                def emitL(st):
                    l_ = pGB.next()
                    S.mm(l_, l_.t[:, :], [(cT.t[:, k2, st * P:(st + 1) * P], qq.t[:, k2, hp * 512:(hp + 1) * 512]) for k2 in range(2)],
                         reads=[cT.b, qq.b])
                    return l_
                ls = [emitL(g) for g in range(min(LA, NS))]
                for st in range(NS):
                    l_ = ls[st]
                    if st + LA < NS:
                        ls.append(emitL(st + LA))
                    e_ = Es.next()
                    em = Ems.next()
                    S.op("act", lambda: nc.scalar.activation(out=e_.t[:], in_=l_.t[:], func=AF.Exp),
                         reads=[l_.b], writes=[e_.b])
                    src_ = e_
                    if st >= jb:
                        jrel = st - jb
                        S.op("pool", lambda: pe_.tensor_tensor(
                            out=em.t[:], in0=e_.t[:], in1=eb.t[:, jrel, :, :].rearrange("p a b -> p (a b)"), op=ALU.mult),
                            reads=[e_.b, eb.b], writes=[em.b])
                        src_ = em
                    for a_ in range(2):
                        S.op("pool", lambda a_=a_: pe_.tensor_tensor(
                            out=em.t[:, a_ * QB:(a_ + 1) * QB], in0=src_.t[:, a_ * QB:(a_ + 1) * QB],
                            in1=mT.t[:, st * QB:(st + 1) * QB], op=ALU.mult),
                            reads=[src_.b, mT.b], writes=[em.b])
                    for k2 in range(2):
                        S.mm(pO2[k2], pO2[k2].t[:], [(cn.t[:, st, k2 * P:(k2 + 1) * P], em.t[:])],
                             reads=[cn.b, em.b], first=(st == 0), last=(st == NS - 1))
                    S.mm(pD, pD.t[:], [(ones.t[:], em.t[:])], reads=[ones.b, em.b],
                         first=(st == 0), last=(st == NS - 1))
                    yield
                for k2 in range(2):
                    S.op("act", lambda k2=k2: nc.scalar.copy(out=obT.t[:, k2, :], in_=pO2[k2].t[:]),
                         reads=[pO2[k2].b], writes=[obT.b])
                S.op("act", lambda: nc.scalar.activation(out=dsb.t[:], in_=pD.t[:], func=AF.Ln), reads=[pD.b], writes=[dsb.b])
                u_ = pGB.next()
                for a_ in range(2):
                    h = 2 * hp + a_
                    S.mm(u_, u_.t[:, a_ * QB:(a_ + 1) * QB],
                         [(wuv.t[:, h, k2, :], obT.t[:, k2, a_ * QB:(a_ + 1) * QB]) for k2 in range(2)],
                         reads=[wuv.b, obT.b])
                S.op("act", lambda: nc.scalar.copy(out=usb.t[:], in_=u_.t[:]), reads=[u_.b], writes=[usb.b])
                oh = ohs.next()
                S.op("act", lambda: nc.scalar.activation(out=dsb.t[:], in_=dsb.t[:], func=AF.Exp, scale=-1.0),
                     reads=[dsb.b], writes=[dsb.b])
                S.op("pool", lambda: pe_.tensor_tensor(out=oh.t[:], in0=usb.t[:], in1=dsb.t[:], op=ALU.mult),
                     reads=[usb.b, dsb.b], writes=[oh.b])
                S.dma(oT.t[2 * hp * P:(2 * hp + 2) * P, qsl].rearrange("(a p) q -> p a q", p=P),
                      oh.t[:].rearrange("p (a q) -> p a q", a=2), reads=[oh.b], writes=[oT.b])
                if hp + 1 < 8:
                    eb_next = adjust_bias(bt_next, hp + 1)
                yield

        def run_all(g):
            n = 0
            for _ in g:
                n += 1
            return n

        def count_select(qb_):
            n = 0
            for qt2 in range(QB // P):
                qt = qb_ * (QB // P) + qt2
                nk = (qt + 1) * P
                n += 16 * ((nk + 511) // 512) + (2 * (KSEL // 8) - 1 if nk > KSEL else 0) + 1 + (qt + 1) // 4
            return n

        NQB = T // QB
        NT2 = QB // P
        for t2 in range(NT2):
            sel_idx(0, t2)
            run_all(sel_topk(0, t2))
            sel_maskT(0, t2)
        for qb_ in range(NQB):
            B = gen_attn(qb_)
            NSb = (qb_ * QB + QB) // P
            nBh = (8 // NT2) * (NSb + 1)
            for t2 in range(NT2):
                A = None
                nA = 0
                if qb_ + 1 < NQB:
                    sel_idx(qb_ + 1, t2)
                    A = sel_topk(qb_ + 1, t2)
                    nA = 3 * TOPK_NIT + 1
                doneA = 0
                for i in range(nBh):
                    next(B, None)
                    if A is not None:
                        tgt = (i + 1) * nA // nBh
                        while doneA < tgt:
                            next(A, None)
                            doneA += 1
                if A is not None:
                    run_all(A)
                    sel_maskT(qb_ + 1, t2)
            run_all(B)
    S.barrier()
    with ExitStack() as es:
        phase_linear(S, oT, DT, C["dsa_w_out"][j], 0, D, "fm", make_resid_sink(S, es, xT))


def build_program(stop_after=None, dump=()):
    nc = bass.Bass("TRN2", target_bir_lowering=False)
    C = {}

    def din(name, shape, dt=F32):
        return TB(nc.dram_tensor(name, list(shape), dt, kind="ExternalInput").ap())

    x_in = din("xT_in", [D, T])
    C["norm_mix"] = [din(f"norm_mix{i}", [P, DT]) for i in range(DEPTH)]
    C["norm_ffn"] = [din(f"norm_ffn{i}", [P, DT]) for i in range(DEPTH)]
    C["norm_final"] = din("norm_final", [P, DT])
    C["gla_w_in"] = [din(f"gla_w_in{j}", [D, GLA_IN]) for j in range(2)]
    C["gla_waug"] = [din(f"gla_waug{j}", [32, 1024]) for j in range(2)]
    C["gla_gn"] = [din(f"gla_gn{j}", [P, 4]) for j in range(2)]
    C["gla_w_out"] = [din(f"gla_w_out{j}", [D, D]) for j in range(2)]
    C["dsa_w_in"] = [din(f"dsa_w_in{j}", [D, DSA_IN]) for j in range(2)]
    C["dsa_kvn"] = [din(f"dsa_kvn{j}", [P, 2]) for j in range(2)]
    C["dsa_kin"] = [din(f"dsa_kin{j}", [P, 1]) for j in range(2)]
    C["dsa_w_uv"] = [din(f"dsa_w_uv{j}", [16, 256, P]) for j in range(2)]
    C["dsa_w_out"] = [din(f"dsa_w_out{j}", [D, D]) for j in range(2)]
    C["ffn_w1"] = [din(f"ffn_w1_{i}", [D, DFF]) for i in range(DEPTH)]
    C["ffn_w3"] = [din(f"ffn_w3_{i}", [D, DFF]) for i in range(DEPTH)]
    C["ffn_w2"] = [din(f"ffn_w2_{i}", [DFF, D]) for i in range(DEPTH)]
    C["mrev"] = din("mrev", [P, P])
    C["cind"] = din("cind", [P, 2])
    C["ident"] = din("ident", [P, P])
    C["b15"] = din("b15", [P, 16])
    C["biasN"] = din("biasN", [16, P, 3, QB])
    outT = TB(nc.dram_tensor("outT", [D, T], F32, kind="ExternalOutput").ap())

    scratch = {}

    def dram(name, shape, dt):
        if name not in scratch:
            scratch[name] = TB(nc.dram_tensor("scr_" + name, list(shape), dt, kind="Internal").ap())
        return scratch[name]

    xT = dram("xT", [D, T], F32)
    hT = dram("hT", [D, T], BF16)
    with ExitStack() as es0:
        S = Sched(nc, es0)
        for k in range(DT):
            S.dma(xT.t[k * P:(k + 1) * P, :], x_in.t[k * P:(k + 1) * P, :], reads=[x_in.b], writes=[xT.b])
        S.barrier()
        nph = 0
        done = False
        for i in range(DEPTH):
            j = i // 2
            phase_norm(S, xT, hT, C["norm_mix"][i], DT, D)
            if i % 2 == 0:
                phase_gla(S, es0, hT, xT, C, j, dram)
            else:
                phase_dsa(S, es0, hT, xT, C, j, dram)
            nph += 1
            if stop_after == nph:
                done = True
                break
            phase_norm(S, xT, hT, C["norm_ffn"][i], DT, D)
            phase_ffn(S, hT, xT, C["ffn_w1"][i], C["ffn_w3"][i], C["ffn_w2"][i])
            nph += 1
            if stop_after == nph:
                done = True
                break
        for nm in dump:
            src_ = scratch[nm]
            dt_ = BF16 if nm in ("d_oT", "d_cTn", "d_qT", "d_kiTn", "d_qiT", "hT") else F32
            dst_ = TB(nc.dram_tensor("dump_" + nm, list(src_.t.shape), dt_, kind="ExternalOutput").ap())
            S.dma(dst_.t, src_.t, reads=[src_.b], writes=[dst_.b])
        if done:
            for k in range(DT):
                S.dma(outT.t[k * P:(k + 1) * P, :], xT.t[k * P:(k + 1) * P, :], reads=[xT.b], writes=[outT.b])
        else:
            phase_norm(S, xT, outT, C["norm_final"], DT, D, dst_f32=True)
        S.barrier()
    return nc


def t5_bucket_np(rel):
    nb = 16
    me = 8
    ret = (rel > 0).astype(np.int32) * nb
    n = np.abs(rel)
    large = me + (np.log(np.maximum(n, 1).astype(np.float32) / me) / math.log(128 / me) * (nb - me)).astype(np.int32)
    large = np.minimum(large, nb - 1)
    return ret + np.where(n < me, n, large)


def prep_inputs(inputs):
    f = lambda a: np.ascontiguousarray(np.asarray(a, dtype=np.float32))
    vecl = lambda v, k: f(np.asarray(v).reshape(k, P).T)
    sh = {}
    for i in range(DEPTH):
        sh[f"norm_mix{i}"] = vecl(inputs["norm_mix"][i], DT)
        sh[f"norm_ffn{i}"] = vecl(inputs["norm_ffn"][i], DT)
        sh[f"ffn_w1_{i}"] = f(inputs["ffn_w1"][i])
        sh[f"ffn_w3_{i}"] = f(inputs["ffn_w3"][i])
        sh[f"ffn_w2_{i}"] = f(inputs["ffn_w2"][i])
    sh["norm_final"] = vecl(inputs["norm_final"], DT)
    for j in range(2):
        sh[f"gla_w_in{j}"] = f(inputs["gla_w_in"][j])
        waug = np.zeros((32, 1024), np.float32)
        waug[0:16] = inputs["gla_w_a2"][j]
        waug[16] = inputs["gla_b_a"][j]
        sh[f"gla_waug{j}"] = waug
        sh[f"gla_gn{j}"] = vecl(inputs["gla_g_norm"][j], 4)
        sh[f"gla_w_out{j}"] = f(inputs["gla_w_out"][j])
        sh[f"dsa_w_in{j}"] = f(inputs["dsa_w_in"][j])
        sh[f"dsa_kvn{j}"] = vecl(inputs["dsa_kv_norm"][j], 2)
        sh[f"dsa_kin{j}"] = vecl(inputs["dsa_kidx_norm"][j], 1)
        sh[f"dsa_w_uv{j}"] = f(inputs["dsa_w_uv"][j])
        sh[f"dsa_w_out{j}"] = f(inputs["dsa_w_out"][j])
    tp = np.arange(P)
    same = (tp[:, None] // CH) == (tp[None, :] // CH)
    sh["mrev"] = (same & (tp[:, None] > tp[None, :])).astype(np.float32)
    sh["cind"] = (tp[:, None] // CH == np.arange(2)[None, :]).astype(np.float32)
    sh["ident"] = np.eye(P, dtype=np.float32)
    rb = np.asarray(inputs["rel_bias"], dtype=np.float32)
    sh["b15"] = f(np.broadcast_to(rb[15][None, :], (P, 16)))
    sl = np.arange(P)[:, None, None]
    jj = np.arange(3)[None, :, None]
    ql = np.arange(QB)[None, None, :]
    rel = (jj - 1) * P + sl - ql
    bidx = t5_bucket_np(rel)
    sh["biasN"] = f(np.transpose(rb[bidx], (3, 0, 1, 2)))
    return sh


def kernel(**inputs):
    shared = prep_inputs(inputs)
    x = np.asarray(inputs["x"], dtype=np.float32)
    in_maps = []
    for b in range(NB):
        m = dict(shared)
        m["xT_in"] = np.ascontiguousarray(x[b].T)
        in_maps.append(m)
    nc = build_program()
    res = run_bass_kernel_spmd(nc, in_maps, core_ids=list(range(NB)))
    out = np.stack([np.ascontiguousarray(res.results[b]["outT"].T) for b in range(NB)], axis=0)
    return out.astype(np.float32)
```

```python
import math
from contextlib import ExitStack

import numpy as np
import concourse.bass as bass
import concourse.mybir as mybir
from concourse.bass_utils import run_bass_kernel_spmd

F32 = mybir.dt.float32
BF16 = mybir.dt.bfloat16
ALU = mybir.AluOpType
AF = mybir.ActivationFunctionType

P = 128
D = 2048
DT = D // P
T = 4096
NB = 4
DEPTH = 4
DFF = 5632
FT = DFF // P
CH = 64
EPS = 1e-6
GLA_IN = 6160
DSA_IN = 6544
KSEL = 256
NEG_ADM = -1.0e30
NEG_REP = -3.0e38
QB = 256
NDQ = 6
DBG = {}
TOPK_R0 = 65536.0
TOPK_NIT = 36


class Buf:
    __slots__ = ("w", "r")

    def __init__(self):
        self.w = None
        self.r = {}


class TB:
    def __init__(self, t):
        self.t = t
        self.b = Buf()


class Sched:
    def __init__(self, nc, es):
        self.nc = nc
        self.eng = {"pe": nc.tensor, "act": nc.scalar, "dve": nc.vector,
                    "pool": nc.gpsimd, "sp": nc.sync}
        names = ["pe", "act", "dve", "pool"]
        names += [f"qsp{i}" for i in range(NDQ)] + [f"qpool{i}" for i in range(NDQ)] + [f"qact{i}" for i in range(NDQ)]
        self.sem = {n: es.enter_context(nc.semaphore("s_" + n)) for n in names}
        self.cnt = dict.fromkeys(names, 0)
        self.seen = {e: dict.fromkeys(names, 0) for e in self.eng}
        self.dq = {"sp": 0, "pool": 0, "act": 0}

    def _wait(self, E, s, c):
        if s == E and c > self.cnt[s]:
            return
        if self.seen[E][s] < c:
            self.eng[E].wait_ge(self.sem[s], c)
            self.seen[E][s] = c

    def op(self, E, fn, reads=(), writes=(), inc=True, dma=False):
        if dma:
            slot = self.dq[E]
            self.dq[E] = (slot + 1) % NDQ
            s = f"q{E}{slot}"
            self._wait(E, s, self.cnt[s])
        else:
            s = E
        for b in reads:
            if b.w:
                self._wait(E, *b.w)
        for b in writes:
            if b.w:
                self._wait(E, *b.w)
            for rs, rc in b.r.items():
                self._wait(E, rs, rc)
        ins = fn()
        if inc:
            step = 16 if dma else 1
            self.cnt[s] += step
            ins.then_inc(self.sem[s], step)
            c = self.cnt[s]
        else:
            c = self.cnt[s] + 1
        for b in writes:
            b.w = (s, c)
            b.r = {}
        for b in reads:
            if b.r.get(s, 0) < c:
                b.r[s] = c
        return ins

    def sc_begin(self, name):
        if DBG.get("scopes"):
            cm = self.nc.named_scope(name)
            cm.__enter__()
            self._scopes = getattr(self, "_scopes", []) + [cm]

    def sc_end(self):
        if DBG.get("scopes"):
            self._scopes.pop().__exit__(None, None, None)

    def barrier(self):
        for E in self.eng:
            for s, c in self.cnt.items():
                if c:
                    self._wait(E, s, c)

    def sb(self, es, shape, dt, name):
        return TB(es.enter_context(self.nc.sbuf_tensor(name, list(shape), dt)))

    def ps(self, es, shape, dt, name):
        return TB(es.enter_context(self.nc.psum_tensor(name, list(shape), dt)))

    def dma(self, out_ap, in_ap, reads=(), writes=(), q="sp"):
        eng = self.eng[q]
        return self.op(q, lambda: eng.dma_start(out=out_ap, in_=in_ap), reads, writes, dma=True)

    def mm(self, out_tb, out_ap, pairs, reads, first=True, last=True):
        n = len(pairs)
        nc = self.nc
        for i, (l, r) in enumerate(pairs):
            st = first and i == 0
            sp = last and i == n - 1
            self.op("pe", lambda l=l, r=r, st=st, sp=sp: nc.tensor.matmul(out_ap, l, r, start=st, stop=sp),
                    reads=reads if i == 0 else (), writes=[out_tb.b], inc=sp)


class Rot:
    def __init__(self, items):
        self.items = items
        self.i = 0

    def next(self):
        it = self.items[self.i % len(self.items)]
        self.i += 1
        return it


_uid = [0]


def uid(p):
    _uid[0] += 1
    return f"{p}{_uid[0]}"


def phase_norm(S, src, dst, gw, KT, nfeat, dst_f32=False):
    S.sc_begin(uid("norm"))
    nc = S.nc
    TBK = 512
    sv = src.t.rearrange("(k p) t -> p k t", p=P)
    dv = dst.t.rearrange("(k p) t -> p k t", p=P)
    with ExitStack() as es:
        g = S.sb(es, [P, KT], F32, uid("ng"))
        ones = S.sb(es, [P, P], BF16, uid("nones"))
        S.dma(g.t[:], gw.t, writes=[g.b])
        S.op("dve", lambda: nc.vector.memset(ones.t[:], 1.0), writes=[ones.b])
        xs = Rot([S.sb(es, [P, KT, TBK], F32, uid("nx")) for _ in range(2)])
        sqs = Rot([S.sb(es, [P, KT, TBK], BF16, uid("nsq")) for _ in range(2)])
        hs = Rot([S.sb(es, [P, KT, TBK], F32 if dst_f32 else BF16, uid("nh")) for _ in range(2)])
        rss = Rot([S.sb(es, [P, TBK], F32, uid("nrs")) for _ in range(2)])
        pss = Rot([S.ps(es, [P, TBK], F32, uid("nps")) for _ in range(2)])
        for tb in range(T // TBK):
            sl = slice(tb * TBK, (tb + 1) * TBK)
            x = xs.next()
            h = hs.next()
            sq = sqs.next()
            rs = rss.next()
            ps = pss.next()
            S.dma(x.t[:], sv[:, :, sl], reads=[src.b], writes=[x.b])
            S.op("act", lambda x=x, sq=sq: nc.scalar.activation(out=sq.t[:], in_=x.t[:], func=AF.Square),
                 reads=[x.b], writes=[sq.b])
            S.mm(ps, ps.t[:], [(ones.t[:], sq.t[:, k, :]) for k in range(KT)], reads=[ones.b, sq.b])
            S.op("act", lambda ps=ps, rs=rs: nc.scalar.activation(out=rs.t[:], in_=ps.t[:], func=AF.Sqrt,
                                                                 bias=EPS, scale=1.0 / nfeat),
                 reads=[ps.b], writes=[rs.b])
            S.op("dve", lambda rs=rs: nc.vector.reciprocal(rs.t[:], rs.t[:]), reads=[rs.b], writes=[rs.b])
            for k in range(KT):
                S.op("dve", lambda k=k, x=x, h=h, rs=rs: nc.vector.scalar_tensor_tensor(
                    out=h.t[:, k, :], in0=x.t[:, k, :], scalar=g.t[:, k:k + 1], in1=rs.t[:],
                    op0=ALU.mult, op1=ALU.mult), reads=[x.b, g.b, rs.b], writes=[h.b])
            S.dma(dv[:, :, sl], h.t[:], reads=[h.b], writes=[dst.b])
    S.barrier()
    S.sc_end()


def phase_linear(S, hT, KT, W, c0, ncols, mode, sink, TSB=2048):
    phase_linear_multi(S, hT, KT, W, [(c0, ncols, mode, sink)], TSB)


def phase_linear_multi(S, hT, KT, W, jobs, TSB=2048):
    nc = S.nc
    S.sc_begin(uid("lin_" + "_".join(f"{m}{c}" for c, _, m, _ in jobs) + "_"))
    hv = hT.t.rearrange("(k p) t -> p k t", p=P)
    wv = W.t.rearrange("(k p) n -> p k n", p=P)
    modes = {m for _, _, m, _ in jobs}
    with ExitStack() as es:
        hsb = S.sb(es, [P, KT, TSB], BF16, uid("lh"))
        wsm = {}
        if "fm" in modes:
            wsm["fm"] = Rot([S.sb(es, [P, KT, 128], BF16, uid("lwf")) for _ in range(3)])
        if "tm" in modes:
            wsm["tm"] = Rot([S.sb(es, [P, KT, 512], BF16, uid("lwt")) for _ in range(2 if len(modes) > 1 else 3)])
        pss = Rot([S.ps(es, [P, 512], F32, uid("lps")) for _ in range(4)])
        for sbk in range(T // TSB):
            ts0 = sbk * TSB
            for q4 in range(4):
                qs = TSB // 4
                S.dma(hsb.t[:, :, q4 * qs:(q4 + 1) * qs], hv[:, :, ts0 + q4 * qs: ts0 + (q4 + 1) * qs],
                      reads=[hT.b], writes=[hsb.b])
            for (c0, ncols, mode, sink) in jobs:
                nw = 128 if mode == "fm" else 512
                for n0 in range(0, ncols, nw):
                    nsz = min(nw, ncols - n0)
                    w = wsm[mode].next()
                    S.dma(w.t[:, :, :nsz], wv[:, :, c0 + n0:c0 + n0 + nsz], reads=[W.b], writes=[w.b], q="pool")
                    if mode == "fm":
                        for t0 in range(0, TSB, 512):
                            ps = pss.next()
                            S.mm(ps, ps.t[:nsz, :], [(w.t[:, k, :nsz], hsb.t[:, k, t0:t0 + 512]) for k in range(KT)],
                                 reads=[w.b, hsb.b])
                            sink(ps, n0, nsz, ts0 + t0)
                    else:
                        for t0 in range(0, TSB, P):
                            ps = pss.next()
                            S.mm(ps, ps.t[:, :nsz], [(hsb.t[:, k, t0:t0 + P], w.t[:, k, :nsz]) for k in range(KT)],
                                 reads=[w.b, hsb.b])
                            sink(ps, ts0 + t0, n0, nsz)
        for (_, _, _, sink) in jobs:
            if hasattr(sink, "flush"):
                sink.flush()
    S.barrier()
    S.sc_end()


def make_store_sink(S, es, dst, mode, dt, func=AF.Copy, scale=1.0, roff=0):
    nc = S.nc
    obs = Rot([S.sb(es, [P, 512], dt, uid("so")) for _ in range(3)])

    def sink_fm(ps, n0, nsz, t0):
        o = obs.next()
        S.op("act", lambda: nc.scalar.activation(out=o.t[:nsz, :], in_=ps.t[:nsz, :], func=func, scale=scale),
             reads=[ps.b], writes=[o.b])
        S.dma(dst.t[roff + n0:roff + n0 + nsz, t0:t0 + 512], o.t[:nsz, :], reads=[o.b], writes=[dst.b])

    def sink_tm(ps, t0, n0, nsz):
        o = obs.next()
        S.op("act", lambda: nc.scalar.activation(out=o.t[:, :nsz], in_=ps.t[:, :nsz], func=func, scale=scale),
             reads=[ps.b], writes=[o.b])
        S.dma(dst.t[t0:t0 + P, roff + n0:roff + n0 + nsz], o.t[:, :nsz], reads=[o.b], writes=[dst.b])

    return sink_fm if mode == "fm" else sink_tm


def make_resid_sink(S, es, xT, slab=2048):
    nc = S.nc
    xin = Rot([S.sb(es, [P, slab], F32, uid("rx")) for _ in range(2)])
    cur = {}

    def sink(ps, n0, nsz, t0):
        off = t0 % slab
        if off == 0:
            xt = xin.next()
            S.dma(xt.t[:nsz, :], xT.t[n0:n0 + nsz, t0:t0 + slab], reads=[xT.b], writes=[xt.b])
            cur["xt"] = xt
        xt = cur["xt"]
        S.op("dve", lambda: nc.vector.tensor_tensor(out=xt.t[:nsz, off:off + 512], in0=xt.t[:nsz, off:off + 512],
                                                    in1=ps.t[:nsz, :], op=ALU.add), reads=[xt.b, ps.b], writes=[xt.b])
        if off + 512 == slab:
            S.dma(xT.t[n0:n0 + nsz, t0 + 512 - slab:t0 + 512], xt.t[:nsz, :], reads=[xt.b], writes=[xT.b])

    return sink


def phase_ffn(S, hT, xT, w1, w3, w2):
    nc = S.nc
    S.sc_begin(uid("ffn"))
    TSB = 1024
    hv = hT.t.rearrange("(k p) t -> p k t", p=P)
    w1v = w1.t.rearrange("(k p) n -> p k n", p=P)
    w3v = w3.t.rearrange("(k p) n -> p k n", p=P)
    w2v = w2.t.rearrange("(f p) n -> p f n", p=P)
    with ExitStack() as es:
        hsb = S.sb(es, [P, DT, TSB], BF16, uid("fh"))
        gT = S.sb(es, [P, FT, TSB], BF16, uid("fg"))
        w13 = Rot([(S.sb(es, [P, DT, P], BF16, uid("fw1")), S.sb(es, [P, DT, P], BF16, uid("fw3")))
                   for _ in range(3)])
        w2s = Rot([S.sb(es, [P, FT, P], BF16, uid("fw2")) for _ in range(2)])
        s1s = Rot([S.sb(es, [P, 512], F32, uid("fs1")) for _ in range(2)])
        p1s = Rot([S.ps(es, [P, 512], F32, uid("fp1")) for _ in range(2)])
        p3s = Rot([S.ps(es, [P, 512], F32, uid("fp3")) for _ in range(2)])
        pys = Rot([S.ps(es, [P, 512], F32, uid("fpy")) for _ in range(2)])
        rsink = make_resid_sink(S, es, xT, slab=TSB)
        for sbk in range(T // TSB):
            ts0 = sbk * TSB
            for q4 in range(2):
                qs = TSB // 2
                S.dma(hsb.t[:, :, q4 * qs:(q4 + 1) * qs], hv[:, :, ts0 + q4 * qs:ts0 + (q4 + 1) * qs],
                      reads=[hT.b], writes=[hsb.b])
            for ft in range(FT):
                a, b = w13.next()
                S.dma(a.t[:], w1v[:, :, ft * P:(ft + 1) * P], reads=[w1.b], writes=[a.b], q="pool")
                S.dma(b.t[:], w3v[:, :, ft * P:(ft + 1) * P], reads=[w3.b], writes=[b.b], q="pool")
                for nb in range(TSB // 512):
                    tsl = slice(nb * 512, (nb + 1) * 512)
                    p1 = p1s.next()
                    p3 = p3s.next()
                    s1 = s1s.next()
                    S.mm(p1, p1.t[:], [(a.t[:, k, :], hsb.t[:, k, tsl]) for k in range(DT)], reads=[a.b, hsb.b])
                    S.mm(p3, p3.t[:], [(b.t[:, k, :], hsb.t[:, k, tsl]) for k in range(DT)], reads=[b.b, hsb.b])
                    S.op("act", lambda p1=p1, s1=s1: nc.scalar.activation(out=s1.t[:], in_=p1.t[:], func=AF.Silu),
                         reads=[p1.b], writes=[s1.b])
                    S.op("dve", lambda s1=s1, p3=p3, ft=ft, tsl=tsl: nc.vector.tensor_tensor(
                        out=gT.t[:, ft, tsl], in0=s1.t[:], in1=p3.t[:], op=ALU.mult),
                        reads=[s1.b, p3.b], writes=[gT.b])
            for dt_ in range(DT):
                w = w2s.next()
                S.dma(w.t[:, :FT // 2, :], w2v[:, :FT // 2, dt_ * P:(dt_ + 1) * P], reads=[w2.b], writes=[w.b], q="pool")
                S.dma(w.t[:, FT // 2:, :], w2v[:, FT // 2:, dt_ * P:(dt_ + 1) * P], reads=[w2.b], writes=[w.b], q="pool")
                for nb in range(TSB // 512):
                    tsl = slice(nb * 512, (nb + 1) * 512)
                    py = pys.next()
                    S.mm(py, py.t[:], [(w.t[:, f, :], gT.t[:, f, tsl]) for f in range(FT)], reads=[w.b, gT.b])
                    rsink(py, dt_ * P, P, ts0 + nb * 512)
    S.barrier()
    S.sc_end()


def phase_gla(S, es0, hT, xT, C, j, dram):
    nc = S.nc
    w_in = C["gla_w_in"][j]
    qT = dram("g_qT", [1024, T], BF16)
    sgT = dram("g_sgT", [2048, T], BF16)
    aT = dram("g_aT", [16, T], F32)
    ktm = dram("g_k", [T, 1024], F32)
    vtm = dram("g_v", [T, 2048], BF16)
    kd = dram("g_kd", [T, 1024], BF16)
    decS = dram("g_decS", [P, (T // P) * 16], F32)
    ogT = dram("g_ogT", [2048, T], BF16)
    with ExitStack() as es:
        phase_linear_multi(S, hT, DT, w_in, [
            (0, 1024, "fm", make_store_sink(S, es, qT, "fm", BF16, scale=1.0 / 16.0)),
            (4096, 2048, "fm", make_store_sink(S, es, sgT, "fm", BF16, func=AF.Silu)),
            (6144, 16, "fm", make_store_sink(S, es, aT, "fm", F32)),
            (1024, 1024, "tm", make_store_sink(S, es, ktm, "tm", F32)),
            (2048, 2048, "tm", make_store_sink(S, es, vtm, "tm", BF16)),
        ])
    S.sc_begin(uid("gla_gate"))
    with ExitStack() as es:
        waug = S.sb(es, [32, 1024], F32, uid("gw"))
        S.dma(waug.t[:], C["gla_waug"][j].t, writes=[waug.b])
        mrev = S.sb(es, [P, P], F32, uid("gm"))
        S.dma(mrev.t[:], C["mrev"].t, writes=[mrev.b])
        cind = S.sb(es, [P, 2], F32, uid("gci"))
        S.dma(cind.t[:], C["cind"].t, writes=[cind.b])
        aaug = Rot([S.sb(es, [32, P], F32, uid("ga")) for _ in range(2)])
        for a in aaug.items:
            S.op("dve", lambda a=a: nc.vector.memset(a.t[:], 1.0), writes=[a.b])
        Ls = Rot([S.sb(es, [P, 1024], F32, uid("gL")) for _ in range(2)])
        dec = S.sb(es, [P, 1024], F32, uid("gdec"))
        kts = Rot([S.sb(es, [P, 1024], F32, uid("gk")) for _ in range(2)])
        kds = Rot([S.sb(es, [P, 1024], BF16, uid("gkd")) for _ in range(2)])
        dcs = S.sb(es, [P, (T // P) * 16], F32, uid("gdcs"))
        pz = Rot([S.ps(es, [P, 512], F32, uid("gpz")) for _ in range(4)])
        pc = S.ps(es, [P, 16], F32, uid("gpc"))
        for tt in range(T // P):
            tsl = slice(tt * P, (tt + 1) * P)
            a = aaug.next()
            L = Ls.next()
            kt_ = kts.next()
            kdt = kds.next()
            S.dma(a.t[0:16, :], aT.t[:, tsl], reads=[aT.b], writes=[a.b])
            S.dma(kt_.t[:], ktm.t[tsl, :], reads=[ktm.b], writes=[kt_.b])
            for hb in range(2):
                z = pz.next()
                S.mm(z, z.t[:], [(a.t[:, :], waug.t[:, hb * 512:(hb + 1) * 512])], reads=[a.b, waug.b])
                S.op("act", lambda z=z, L=L, hb=hb: nc.scalar.activation(
                    out=L.t[:, hb * 512:(hb + 1) * 512], in_=z.t[:], func=AF.Exp, scale=-1.0),
                    reads=[z.b], writes=[L.b])
            S.op("act", lambda L=L: nc.scalar.activation(out=L.t[:], in_=L.t[:], func=AF.Ln, bias=1.0),
                 reads=[L.b], writes=[L.b])
            for hb in range(2):
                z = pz.next()
                S.mm(z, z.t[:], [(mrev.t[:], L.t[:, hb * 512:(hb + 1) * 512])], reads=[mrev.b, L.b])
                S.op("act", lambda z=z, hb=hb: nc.scalar.activation(
                    out=dec.t[:, hb * 512:(hb + 1) * 512], in_=z.t[:], func=AF.Exp, scale=-1.0 / 16.0),
                    reads=[z.b], writes=[dec.b])
            S.op("dve", lambda kt_=kt_, kdt=kdt: nc.vector.tensor_tensor(out=kdt.t[:], in0=kt_.t[:], in1=dec.t[:],
                                                                          op=ALU.mult),
                 reads=[kt_.b, dec.b], writes=[kdt.b])
            S.dma(kd.t[tsl, :], kdt.t[:], reads=[kdt.b], writes=[kd.b])
            for k8 in range(8):
                S.mm(pc, pc.t[:, k8 * 2:k8 * 2 + 2], [(L.t[:, k8 * P:(k8 + 1) * P], cind.t[:])], reads=[L.b, cind.b])
            S.op("act", lambda tt=tt: nc.scalar.activation(out=dcs.t[:, tt * 16:(tt + 1) * 16], in_=pc.t[:],
                                                           func=AF.Exp, scale=-1.0 / 16.0),
                 reads=[pc.b], writes=[dcs.b])
        S.dma(decS.t, dcs.t[:], reads=[dcs.b], writes=[decS.b])
    S.barrier()
    S.sc_end()
    S.sc_begin(uid("gla_scan"))
    with ExitStack() as es:
        dcs = S.sb(es, [P, (T // P) * 16], F32, uid("sdcs"))
        S.dma(dcs.t[:], decS.t, reads=[decS.b], writes=[dcs.b])
        gn = S.sb(es, [P, 4], F32, uid("sgn"))
        S.dma(gn.t[:], C["gla_gn"][j].t, writes=[gn.b])
        ones = S.sb(es, [P, P], F32, uid("sones"))
        S.op("dve", lambda: nc.vector.memset(ones.t[:], 1.0), writes=[ones.b])
        St = [S.sb(es, [P, 512], F32, uid("sS")) for _ in range(8)]
        Sb = [S.sb(es, [P, 512], BF16, uid("sSb")) for _ in range(8)]
        for s_ in St:
            S.op("dve", lambda s_=s_: nc.vector.memset(s_.t[:], 0.0), writes=[s_.b])
        qbs = Rot([S.sb(es, [P, 8, 512], BF16, uid("sq")) for _ in range(2)])
        sgs = Rot([S.sb(es, [P, 16, 512], BF16, uid("ssg")) for _ in range(2)])
        ogs = Rot([S.sb(es, [P, 16, 512], BF16, uid("sog")) for _ in range(2)])
        kdb = Rot([S.sb(es, [P, 1024], BF16, uid("skd")) for _ in range(2)])
        vb = Rot([S.sb(es, [P, 2048], BF16, uid("sv")) for _ in range(2)])
        sq = S.sb(es, [P, 16, CH], F32, uid("ssq"))
        rs = S.sb(es, [P, 4, CH], F32, uid("srs"))
        tmpb = S.sb(es, [P, 16, CH], F32, uid("stmpb"))
        pu = Rot([S.ps(es, [P, 512], F32, uid("spu")) for _ in range(3)])
        po = [S.ps(es, [P, 8, CH], F32, uid("spo")) for _ in range(2)]
        pss = S.ps(es, [P, 4, CH], F32, uid("spss"))
        qv = qT.t.rearrange("(k p) t -> p k t", p=P)
        sgv = sgT.t.rearrange("(k p) t -> p k t", p=P)
        ogv = ogT.t.rearrange("(k p) t -> p k t", p=P)
        NCH = T // CH
        Sb2 = [Sb, [S.sb(es, [P, 512], BF16, uid("sSb2")) for _ in range(8)]]
        po2 = [po, [S.ps(es, [P, 8, CH], F32, uid("spo2")) for _ in range(2)]]
        tiles = {}
        blocks = {}

        def get_tile(tt):
            if tt not in tiles:
                tsl = slice(tt * P, (tt + 1) * P)
                kdt = kdb.next()
                vt_ = vb.next()
                S.dma(kdt.t[:], kd.t[tsl, :], reads=[kd.b], writes=[kdt.b])
                S.dma(vt_.t[:], vtm.t[tsl, :], reads=[vtm.b], writes=[vt_.b])
                tiles.clear()
                tiles[tt] = (kdt, vt_)
            return tiles[tt]

        def get_block(blk):
            if blk not in blocks:
                bsl = slice(blk * 512, (blk + 1) * 512)
                qb = qbs.next()
                sg = sgs.next()
                og = ogs.next()
                S.dma(qb.t[:], qv[:, :, bsl], reads=[qT.b], writes=[qb.b])
                S.dma(sg.t[:], sgv[:, :, bsl], reads=[sgT.b], writes=[sg.b])
                for i16 in range(16):
                    S.op("dve", lambda i16=i16: nc.vector.tensor_scalar_mul(
                        sg.t[:, i16, :], sg.t[:, i16, :], gn.t[:, i16 % 4:i16 % 4 + 1]),
                        reads=[sg.b, gn.b], writes=[sg.b])
                blocks.clear()
                blocks[blk] = (qb, sg, og)
            return blocks[blk]

        def stage_state(c):
            tt, c2 = c // 2, c % 2
            cs = c2 * CH
            kdt, vt_ = get_tile(tt)
            sbs = Sb2[c % 2]
            for h in range(4):
                for kt2 in range(2):
                    i8 = h * 2 + kt2
                    u = pu.next()
                    S.mm(u, u.t[:], [(kdt.t[cs:cs + CH, i8 * P:(i8 + 1) * P], vt_.t[cs:cs + CH, h * 512:(h + 1) * 512])],
                         reads=[kdt.b, vt_.b])
                    didx = tt * 16 + i8 * 2 + c2
                    S.op("dve", lambda: nc.vector.scalar_tensor_tensor(
                        out=St[i8].t[:], in0=St[i8].t[:], scalar=dcs.t[:, didx:didx + 1], in1=u.t[:],
                        op0=ALU.mult, op1=ALU.add), reads=[St[i8].b, dcs.b, u.b], writes=[St[i8].b])
                    S.op("act", lambda: nc.scalar.copy(out=sbs[i8].t[:], in_=St[i8].t[:]),
                         reads=[St[i8].b], writes=[sbs[i8].b])

        def stage_read_pe(c):
            blk = c // 8
            qb, sg, og = get_block(blk)
            ccol = slice((c % 8) * CH, (c % 8) * CH + CH)
            sbs = Sb2[c % 2]
            pp = po2[c % 2]
            for h in range(4):
                for vt4 in range(4):
                    i16 = h * 4 + vt4
                    pt = pp[i16 // 8]
                    S.mm(pt, pt.t[:, i16 % 8, :],
                         [(sbs[h * 2 + kt2].t[:, vt4 * P:(vt4 + 1) * P], qb.t[:, h * 2 + kt2, ccol]) for kt2 in range(2)],
                         reads=[sbs[h * 2].b, sbs[h * 2 + 1].b, qb.b])
            for half in range(2):
                S.op("act", lambda half=half: nc.scalar.activation(
                    out=sq.t[:, half * 8:(half + 1) * 8, :], in_=pp[half].t[:], func=AF.Square),
                    reads=[pp[half].b], writes=[sq.b])
            for h in range(4):
                S.mm(pss, pss.t[:, h, :], [(ones.t[:], sq.t[:, h * 4 + v4, :]) for v4 in range(4)],
                     reads=[ones.b, sq.b])
            S.op("act", lambda: nc.scalar.activation(out=rs.t[:], in_=pss.t[:], func=AF.Sqrt,
                                                     bias=EPS, scale=1.0 / 512.0),
                 reads=[pss.b], writes=[rs.b])

        def stage_read_dve(c):
            blk = c // 8
            qb, sg, og = get_block(blk)
            ccol = slice((c % 8) * CH, (c % 8) * CH + CH)
            pp = po2[c % 2]
            S.op("dve", lambda: nc.vector.reciprocal(rs.t[:], rs.t[:]), reads=[rs.b], writes=[rs.b])
            for half in range(2):
                S.op("dve", lambda half=half: nc.vector.tensor_tensor(
                    out=tmpb.t[:, half * 8:(half + 1) * 8, :], in0=pp[half].t[:],
                    in1=sg.t[:, half * 8:(half + 1) * 8, ccol], op=ALU.mult),
                    reads=[pp[half].b, sg.b], writes=[tmpb.b])
            for h in range(4):
                S.op("dve", lambda h=h: nc.vector.tensor_tensor(
                    out=og.t[:, h * 4:(h + 1) * 4, ccol], in0=tmpb.t[:, h * 4:(h + 1) * 4, :],
                    in1=rs.t[:, h, :].unsqueeze(1).to_broadcast([P, 4, CH]), op=ALU.mult),
                    reads=[tmpb.b, rs.b], writes=[og.b])
            if c % 8 == 7:
                bsl = slice(blk * 512, (blk + 1) * 512)
                S.dma(ogv[:, :, bsl], og.t[:], reads=[og.b], writes=[ogT.b])

        stage_state(0)
        for c in range(NCH):
            stage_read_pe(c)
            if c + 1 < NCH:
                stage_state(c + 1)
            stage_read_dve(c)
    S.barrier()
    S.sc_end()
    with ExitStack() as es:
        phase_linear(S, ogT, DT, C["gla_w_out"][j], 0, D, "fm", make_resid_sink(S, es, xT))


def topk_threshold(S, acc, work, mx, thr, nk):
    nc = S.nc
    S.op("act", lambda: nc.scalar.copy(out=work.t[:, :nk], in_=acc.t[:, :nk]), reads=[acc.b], writes=[work.b])
    for r8 in range(KSEL // 8):
        S.op("dve", lambda: nc.vector.max(out=mx.t[:], in_=work.t[:, :nk]), reads=[work.b], writes=[mx.b])
        if r8 < KSEL // 8 - 1:
            S.op("dve", lambda: nc.vector.match_replace(
                out=work.t[:, :nk], in_to_replace=mx.t[:], in_values=work.t[:, :nk], imm_value=NEG_REP),
                reads=[work.b, mx.b], writes=[work.b])
    S.op("dve", lambda: nc.vector.tensor_reduce(out=thr.t[:], in_=mx.t[:], axis=mybir.AxisListType.X, op=ALU.min),
         reads=[mx.b], writes=[thr.b])
    S.op("dve", lambda: nc.vector.tensor_scalar_max(thr.t[:], thr.t[:], -1.0e29), reads=[thr.b], writes=[thr.b])


def phase_dsa(S, es0, hT, xT, C, j, dram):
    nc = S.nc
    w_in = C["dsa_w_in"][j]
    qT = dram("d_qT", [4096, T], BF16)
    cTr = dram("d_cTr", [256, T], F32)
    cTn = dram("d_cTn", [256, T], BF16)
    qiT = dram("d_qiT", [2048, T], BF16)
    kiTr = dram("d_kiTr", [P, T], F32)
    kiTn = dram("d_kiTn", [P, T], BF16)
    wi = dram("d_wi", [T, 16], F32)
    oT = dram("d_oT", [2048, T], BF16)
    with ExitStack() as es:
        phase_linear_multi(S, hT, DT, w_in, [
            (0, 4096, "fm", make_store_sink(S, es, qT, "fm", BF16, scale=1.0 / 16.0)),
            (4096, 256, "fm", make_store_sink(S, es, cTr, "fm", F32)),
            (4352, 2048, "fm", make_store_sink(S, es, qiT, "fm", BF16)),
            (6400, 128, "fm", make_store_sink(S, es, kiTr, "fm", F32)),
            (6528, 16, "tm", make_store_sink(S, es, wi, "tm", F32)),
        ])
    phase_norm(S, cTr, cTn, C["dsa_kvn"][j], 2, 256)
    phase_norm(S, kiTr, kiTn, C["dsa_kin"][j], 1, 128)

    NT = T // P
    with ExitStack() as es:
        ident = S.sb(es, [P, P], BF16, uid("did"))
        S.dma(ident.t[:], C["ident"].t, writes=[ident.b], q="pool")
        ones = S.sb(es, [P, P], BF16, uid("dones"))
        S.op("dve", lambda: nc.vector.memset(ones.t[:], 1.0), writes=[ones.b])
        b15 = S.sb(es, [P, 16], F32, uid("db15"))
        S.dma(b15.t[:], C["b15"].t, writes=[b15.b])
        kiT = S.sb(es, [P, T], BF16, uid("dki"))
        S.dma(kiT.t[:], kiTn.t, reads=[kiTn.b], writes=[kiT.b])
        cT = S.sb(es, [P, 2, T], BF16, uid("dcT"))
        S.dma(cT.t[:], cTn.t.rearrange("(k p) t -> p k t", p=P), reads=[cTn.b], writes=[cT.b])
        wis = S.sb(es, [P, NT, 16], F32, uid("dwi"))
        with nc.allow_non_contiguous_dma(reason="small per-token head weights"):
            S.dma(wis.t[:], wi.t.rearrange("(n p) h -> p n h", p=P), reads=[wi.b], writes=[wis.b])
        wuv = S.sb(es, [P, 16, 2, P], BF16, uid("dwuv"))
        for h in range(16):
            S.dma(wuv.t[:, h, :, :], C["dsa_w_uv"][j].t[h].rearrange("(k p) e -> p k e", p=P),
                  writes=[wuv.b], q="pool")
        cn = S.sb(es, [P, NT, 256], BF16, uid("dcn"))
        pG = Rot([S.ps(es, [P, 512], F32, uid("dpg")) for _ in range(2)])
        pGB = Rot([S.ps(es, [P, 512], F32, uid("dpgb")) for _ in range(3)])
        for st in range(NT):
            pt = pG.next()
            for k2 in range(2):
                S.mm(pt, pt.t[:, k2 * P:(k2 + 1) * P], [(cT.t[:, k2, st * P:(st + 1) * P], ident.t[:])],
                     reads=[cT.b, ident.b])
            S.op("act", lambda st=st, pt=pt: nc.scalar.copy(out=cn.t[:, st, :], in_=pt.t[:, 0:256]),
                 reads=[pt.b], writes=[cn.b])
        qi = S.sb(es, [P, 16, QB], BF16, uid("dqi"))
        qq = S.sb(es, [P, 2, 16 * QB], BF16, uid("dq"))
        maskT = S.sb(es, [P, NT * QB], BF16, uid("dmT"))
        acc = S.sb(es, [P, T], F32, uid("dacc"))
        junk = S.sb(es, [P, T], BF16, uid("djunk"))
        lo_ = S.sb(es, [P, 1], F32, uid("dlo"))
        mid_ = S.sb(es, [P, 1], F32, uid("dmid"))
        cnt_ = S.sb(es, [P, 1], F32, uid("dcnt"))
        ged_ = S.sb(es, [P, 1], F32, uid("dged"))
        mask = S.sb(es, [P, T], BF16, uid("dmask"))
        mx = S.sb(es, [P, 8], F32, uid("dmx"))
        thr = S.sb(es, [P, 1], F32, uid("dthr"))
        rl = Rot([S.sb(es, [P, 512], F32, uid("drl")) for _ in range(3)])
        Es = Rot([S.sb(es, [P, 512], BF16, uid("dE")) for _ in range(4)])
        Ems = Rot([S.sb(es, [P, 512], BF16, uid("dEm")) for _ in range(4)])
        tl = Rot([S.sb(es, [P, 512], F32, uid("dtl")) for _ in range(2)])
        bn = Rot([S.sb(es, [P, 3, 2, QB], F32, uid("dbn")) for _ in range(3)])
        rden = S.sb(es, [P, 512], F32, uid("drd"))
        obT = S.sb(es, [P, 2, 512], BF16, uid("dob"))
        ohs = Rot([S.sb(es, [P, 512], BF16, uid("doh")) for _ in range(2)])
        pO2 = [S.ps(es, [P, 512], F32, uid("dpO")) for _ in range(2)]
        pD = S.ps(es, [P, 512], F32, uid("dpD"))
        qiv = qiT.t.rearrange("(h p) t -> p h t", p=P)
        qv4 = qT.t.rearrange("(h k p) t -> p k h t", k=2, p=P)
        LA = 2

        negb15 = S.sb(es, [P, 16], F32, uid("dnb15"))
        S.op("dve", lambda: nc.vector.tensor_scalar_mul(negb15.t[:], b15.t[:], -1.0), reads=[b15.b], writes=[negb15.b])
        expbs = Rot([S.sb(es, [P, 3, 2, QB], BF16, uid("dexpb")) for _ in range(2)])
        usb = S.sb(es, [P, 512], F32, uid("dus"))
        dsb = S.sb(es, [P, 512], F32, uid("dds"))

        def load_bias(hp):
            bt = bn.next()
            S.dma(bt.t[:], C["biasN"].t[2 * hp:2 * hp + 2].rearrange("h p a b -> p a h b"), writes=[bt.b])
            return bt

        def adjust_bias(bt, hp):
            eb = expbs.next()
            for a_ in range(2):
                S.op("act", lambda a_=a_: nc.scalar.activation(
                    out=eb.t[:, :, a_, :], in_=bt.t[:, :, a_, :], func=AF.Exp,
                    bias=negb15.t[:, 2 * hp + a_:2 * hp + a_ + 1]),
                    reads=[bt.b, negb15.b], writes=[eb.b])
            return eb

        maskTs = [maskT, S.sb(es, [P, NT * QB], BF16, uid("dmT2"))]
        FMA_ENG = DBG.get("fma_eng", "dve")

        def sel_idx(qb_, qt2):
            mT = maskTs[qb_ % 2]
            q0 = qb_ * QB
            qsl = slice(q0, q0 + QB)
            NS = (q0 + QB) // P
            if qt2 == 0:
                S.dma(qi.t[:], qiv[:, :, qsl], reads=[qiT.b], writes=[qi.b])
                S.op("dve", lambda: nc.vector.memset(mT.t[:, :NS * QB], 0.0), writes=[mT.b])
            qt = q0 // P + qt2
            nk = (qt + 1) * P
            qcol = slice(qt2 * P, (qt2 + 1) * P)
            for h in range(16):
                for k0 in range(0, nk, 512):
                    w_ = min(512, nk - k0)
                    p_ = pG.next()
                    r_ = rl.next()
                    S.mm(p_, p_.t[:, :w_], [(qi.t[:, h, qcol], kiT.t[:, k0:k0 + w_])], reads=[qi.b, kiT.b])
                    S.op("act", lambda: nc.scalar.activation(
                        out=r_.t[:, :w_], in_=p_.t[:, :w_], func=AF.Relu), reads=[p_.b], writes=[r_.b])
                    if h == 0:
                        S.op("dve", lambda: nc.vector.tensor_scalar_mul(
                            acc.t[:, k0:k0 + w_], r_.t[:, :w_], wis.t[:, qt, 0:1]),
                            reads=[r_.b, wis.b], writes=[acc.b])
                    else:
                        S.op("dve", lambda: nc.vector.scalar_tensor_tensor(
                            out=acc.t[:, k0:k0 + w_], in0=r_.t[:, :w_], scalar=wis.t[:, qt, h:h + 1],
                            in1=acc.t[:, k0:k0 + w_], op0=ALU.mult, op1=ALU.add),
                            reads=[r_.b, wis.b, acc.b], writes=[acc.b])
            S.op("dve", lambda: nc.vector.memset(acc.t[0:CH, nk - CH:nk], NEG_ADM), writes=[acc.b])

        def sel_topk(qb_, qt2):
            qt = qb_ * (QB // P) + qt2
            nk = (qt + 1) * P
            if nk > KSEL:
                S.op("dve", lambda: nc.vector.memset(lo_.t[:], -TOPK_R0), writes=[lo_.b])
                d = 2.0 * TOPK_R0
                for it in range(TOPK_NIT):
                    d *= 0.5
                    S.op("dve", lambda d=d: nc.vector.tensor_scalar_add(mid_.t[:], lo_.t[:], d), reads=[lo_.b], writes=[mid_.b])
                    S.op("dve", lambda: nc.vector.memset(cnt_.t[:], 0.0), writes=[cnt_.b])
                    yield
                    S.op("dve", lambda: nc.vector.tensor_scalar(
                        junk.t[:, :nk], acc.t[:, :nk], mid_.t[:, 0:1], 0.0, op0=ALU.is_ge, op1=ALU.add,
                        accum_out=cnt_.t[:, 0:1]), reads=[acc.b, mid_.b, cnt_.b], writes=[junk.b, cnt_.b])
                    yield
                    S.op("dve", lambda d=d: nc.vector.tensor_scalar(
                        ged_.t[:], cnt_.t[:], KSEL - 0.5, d, op0=ALU.is_ge, op1=ALU.mult),
                        reads=[cnt_.b], writes=[ged_.b])
                    S.op("dve", lambda: nc.vector.tensor_tensor(out=lo_.t[:], in0=lo_.t[:], in1=ged_.t[:], op=ALU.add),
                         reads=[lo_.b, ged_.b], writes=[lo_.b])
                    yield
                S.op("dve", lambda: nc.vector.tensor_scalar(
                    mask.t[:, :nk], acc.t[:, :nk], lo_.t[:, 0:1], None, op0=ALU.is_ge),
                    reads=[acc.b, lo_.b], writes=[mask.b])
            else:
                S.op("dve", lambda: nc.vector.tensor_scalar(
                    mask.t[:, :nk], acc.t[:, :nk], -1.0e29, None, op0=ALU.is_ge),
                    reads=[acc.b], writes=[mask.b])
            yield

        def sel_maskT(qb_, qt2):
            mT = maskTs[qb_ % 2]
            qt = qb_ * (QB // P) + qt2
            qc0 = qt2 * P
            for st in range(qt + 1):
                pt = pG.next()
                S.mm(pt, pt.t[:, 0:P], [(mask.t[:, st * P:(st + 1) * P], ident.t[:])], reads=[mask.b, ident.b])
                S.op("act", lambda: nc.scalar.copy(
                    out=mT.t[:, st * QB + qc0:st * QB + qc0 + P], in_=pt.t[:, 0:P]),
                    reads=[pt.b], writes=[mT.b])

        def gen_attn(qb_):
            mT = maskTs[qb_ % 2]
            q0 = qb_ * QB
            qsl = slice(q0, q0 + QB)
            NS = (q0 + QB) // P
            for k2 in range(2):
                S.dma(qq.t[:, k2, :].rearrange("p (h q) -> p h q", h=16), qv4[:, k2, :, qsl], reads=[qT.b], writes=[qq.b])
            jb = q0 // P - 1
            bt_next = load_bias(0)
            eb_next = adjust_bias(bt_next, 0)
            pe_ = nc.gpsimd
            for hp in range(8):
                eb = eb_next
                if hp + 1 < 8:
                    bt_next = load_bias(hp + 1)

                def emitL(st):
                    l_ = pGB.next()
                    S.mm(l_, l_.t[:, :], [(cT.t[:, k2, st * P:(st + 1) * P], qq.t[:, k2, hp * 512:(hp + 1) * 512]) for k2 in range(2)],
                         reads=[cT.b, qq.b])
                    return l_
                ls = [emitL(g) for g in range(min(LA, NS))]
                for st in range(NS):
                    l_ = ls[st]
                    if st + LA < NS:
                        ls.append(emitL(st + LA))
                    e_ = Es.next()
                    em = Ems.next()
                    S.op("act", lambda: nc.scalar.activation(out=e_.t[:], in_=l_.t[:], func=AF.Exp),
                         reads=[l_.b], writes=[e_.b])
                    src_ = e_
                    if st >= jb:
                        jrel = st - jb
                        S.op("pool", lambda: pe_.tensor_tensor(
                            out=em.t[:], in0=e_.t[:], in1=eb.t[:, jrel, :, :].rearrange("p a b -> p (a b)"), op=ALU.mult),
                            reads=[e_.b, eb.b], writes=[em.b])
                        src_ = em
                    S.op("pool", lambda: pe_.tensor_tensor(
                        out=em.t[:].rearrange("p (a q) -> p a q", a=2), in0=src_.t[:].rearrange("p (a q) -> p a q", a=2),
                        in1=mT.t[:, st * QB:(st + 1) * QB].unsqueeze(1).to_broadcast([P, 2, QB]), op=ALU.mult),
                        reads=[src_.b, mT.b], writes=[em.b])
                    for k2 in range(2):
                        S.mm(pO2[k2], pO2[k2].t[:], [(cn.t[:, st, k2 * P:(k2 + 1) * P], em.t[:])],
                             reads=[cn.b, em.b], first=(st == 0), last=(st == NS - 1))
                    S.mm(pD, pD.t[:], [(ones.t[:], em.t[:])], reads=[ones.b, em.b],
                         first=(st == 0), last=(st == NS - 1))
                    yield
                for k2 in range(2):
                    S.op("act", lambda k2=k2: nc.scalar.copy(out=obT.t[:, k2, :], in_=pO2[k2].t[:]),
                         reads=[pO2[k2].b], writes=[obT.b])
                S.op("act", lambda: nc.scalar.activation(out=dsb.t[:], in_=pD.t[:], func=AF.Ln), reads=[pD.b], writes=[dsb.b])
                u_ = pGB.next()
                for a_ in range(2):
                    h = 2 * hp + a_
                    S.mm(u_, u_.t[:, a_ * QB:(a_ + 1) * QB],
                         [(wuv.t[:, h, k2, :], obT.t[:, k2, a_ * QB:(a_ + 1) * QB]) for k2 in range(2)],
                         reads=[wuv.b, obT.b])
                S.op("act", lambda: nc.scalar.copy(out=usb.t[:], in_=u_.t[:]), reads=[u_.b], writes=[usb.b])
                oh = ohs.next()
                S.op("act", lambda: nc.scalar.activation(out=dsb.t[:], in_=dsb.t[:], func=AF.Exp, scale=-1.0),
                     reads=[dsb.b], writes=[dsb.b])
                S.op("pool", lambda: pe_.tensor_tensor(out=oh.t[:], in0=usb.t[:], in1=dsb.t[:], op=ALU.mult),
                     reads=[usb.b, dsb.b], writes=[oh.b])
                S.dma(oT.t[2 * hp * P:(2 * hp + 2) * P, qsl].rearrange("(a p) q -> p a q", p=P),
                      oh.t[:].rearrange("p (a q) -> p a q", a=2), reads=[oh.b], writes=[oT.b])
                if hp + 1 < 8:
                    eb_next = adjust_bias(bt_next, hp + 1)
                yield

        def run_all(g):
            n = 0
            for _ in g:
                n += 1
            return n

        def count_select(qb_):
            n = 0
            for qt2 in range(QB // P):
                qt = qb_ * (QB // P) + qt2
                nk = (qt + 1) * P
                n += 16 * ((nk + 511) // 512) + (2 * (KSEL // 8) - 1 if nk > KSEL else 0) + 1 + (qt + 1) // 4
            return n

        NQB = T // QB
        NT2 = QB // P
        for t2 in range(NT2):
            sel_idx(0, t2)
            run_all(sel_topk(0, t2))
            sel_maskT(0, t2)
        for qb_ in range(NQB):
            B = gen_attn(qb_)
            NSb = (qb_ * QB + QB) // P
            nBh = (8 // NT2) * (NSb + 1)
            for t2 in range(NT2):
                A = None
                nA = 0
                if qb_ + 1 < NQB:
                    sel_idx(qb_ + 1, t2)
                    A = sel_topk(qb_ + 1, t2)
                    nA = 3 * TOPK_NIT + 1
                doneA = 0
                for i in range(nBh):
                    next(B, None)
                    if A is not None:
                        tgt = (i + 1) * nA // nBh
                        while doneA < tgt:
                            next(A, None)
                            doneA += 1
                if A is not None:
                    run_all(A)
                    sel_maskT(qb_ + 1, t2)
            run_all(B)
    S.barrier()
    with ExitStack() as es:
        phase_linear(S, oT, DT, C["dsa_w_out"][j], 0, D, "fm", make_resid_sink(S, es, xT))


def build_program(stop_after=None, dump=()):
    nc = bass.Bass("TRN2", target_bir_lowering=False)
    C = {}

    def din(name, shape, dt=F32):
        return TB(nc.dram_tensor(name, list(shape), dt, kind="ExternalInput").ap())

    x_in = din("xT_in", [D, T])
    C["norm_mix"] = [din(f"norm_mix{i}", [P, DT]) for i in range(DEPTH)]
    C["norm_ffn"] = [din(f"norm_ffn{i}", [P, DT]) for i in range(DEPTH)]
    C["norm_final"] = din("norm_final", [P, DT])
    C["gla_w_in"] = [din(f"gla_w_in{j}", [D, GLA_IN]) for j in range(2)]
    C["gla_waug"] = [din(f"gla_waug{j}", [32, 1024]) for j in range(2)]
    C["gla_gn"] = [din(f"gla_gn{j}", [P, 4]) for j in range(2)]
    C["gla_w_out"] = [din(f"gla_w_out{j}", [D, D]) for j in range(2)]
    C["dsa_w_in"] = [din(f"dsa_w_in{j}", [D, DSA_IN]) for j in range(2)]
    C["dsa_kvn"] = [din(f"dsa_kvn{j}", [P, 2]) for j in range(2)]
    C["dsa_kin"] = [din(f"dsa_kin{j}", [P, 1]) for j in range(2)]
    C["dsa_w_uv"] = [din(f"dsa_w_uv{j}", [16, 256, P]) for j in range(2)]
    C["dsa_w_out"] = [din(f"dsa_w_out{j}", [D, D]) for j in range(2)]
    C["ffn_w1"] = [din(f"ffn_w1_{i}", [D, DFF]) for i in range(DEPTH)]
    C["ffn_w3"] = [din(f"ffn_w3_{i}", [D, DFF]) for i in range(DEPTH)]
    C["ffn_w2"] = [din(f"ffn_w2_{i}", [DFF, D]) for i in range(DEPTH)]
    C["mrev"] = din("mrev", [P, P])
    C["cind"] = din("cind", [P, 2])
    C["ident"] = din("ident", [P, P])
    C["b15"] = din("b15", [P, 16])
    C["biasN"] = din("biasN", [16, P, 3, QB])
    outT = TB(nc.dram_tensor("outT", [D, T], F32, kind="ExternalOutput").ap())

    scratch = {}

    def dram(name, shape, dt):
        if name not in scratch:
            scratch[name] = TB(nc.dram_tensor("scr_" + name, list(shape), dt, kind="Internal").ap())
        return scratch[name]

    xT = dram("xT", [D, T], F32)
    hT = dram("hT", [D, T], BF16)
    with ExitStack() as es0:
        S = Sched(nc, es0)
        for k in range(DT):
            S.dma(xT.t[k * P:(k + 1) * P, :], x_in.t[k * P:(k + 1) * P, :], reads=[x_in.b], writes=[xT.b])
        S.barrier()
        nph = 0
        done = False
        for i in range(DEPTH):
            j = i // 2
            phase_norm(S, xT, hT, C["norm_mix"][i], DT, D)
            if i % 2 == 0:
                phase_gla(S, es0, hT, xT, C, j, dram)
            else:
                phase_dsa(S, es0, hT, xT, C, j, dram)
            nph += 1
            if stop_after == nph:
                done = True
                break
            phase_norm(S, xT, hT, C["norm_ffn"][i], DT, D)
            phase_ffn(S, hT, xT, C["ffn_w1"][i], C["ffn_w3"][i], C["ffn_w2"][i])
            nph += 1
            if stop_after == nph:
                done = True
                break
        for nm in dump:
            src_ = scratch[nm]
            dt_ = BF16 if nm in ("d_oT", "d_cTn", "d_qT", "d_kiTn", "d_qiT", "hT") else F32
            dst_ = TB(nc.dram_tensor("dump_" + nm, list(src_.t.shape), dt_, kind="ExternalOutput").ap())
            S.dma(dst_.t, src_.t, reads=[src_.b], writes=[dst_.b])
        if done:
            for k in range(DT):
                S.dma(outT.t[k * P:(k + 1) * P, :], xT.t[k * P:(k + 1) * P, :], reads=[xT.b], writes=[outT.b])
        else:
            phase_norm(S, xT, outT, C["norm_final"], DT, D, dst_f32=True)
        S.barrier()
    return nc


def t5_bucket_np(rel):
    nb = 16
    me = 8
    ret = (rel > 0).astype(np.int32) * nb
    n = np.abs(rel)
    large = me + (np.log(np.maximum(n, 1).astype(np.float32) / me) / math.log(128 / me) * (nb - me)).astype(np.int32)
    large = np.minimum(large, nb - 1)
    return ret + np.where(n < me, n, large)


def prep_inputs(inputs):
    f = lambda a: np.ascontiguousarray(np.asarray(a, dtype=np.float32))
    vecl = lambda v, k: f(np.asarray(v).reshape(k, P).T)
    sh = {}
    for i in range(DEPTH):
        sh[f"norm_mix{i}"] = vecl(inputs["norm_mix"][i], DT)
        sh[f"norm_ffn{i}"] = vecl(inputs["norm_ffn"][i], DT)
        sh[f"ffn_w1_{i}"] = f(inputs["ffn_w1"][i])
        sh[f"ffn_w3_{i}"] = f(inputs["ffn_w3"][i])
        sh[f"ffn_w2_{i}"] = f(inputs["ffn_w2"][i])
    sh["norm_final"] = vecl(inputs["norm_final"], DT)
    for j in range(2):
        sh[f"gla_w_in{j}"] = f(inputs["gla_w_in"][j])
        waug = np.zeros((32, 1024), np.float32)
        waug[0:16] = inputs["gla_w_a2"][j]
        waug[16] = inputs["gla_b_a"][j]
        sh[f"gla_waug{j}"] = waug
        sh[f"gla_gn{j}"] = vecl(inputs["gla_g_norm"][j], 4)
        sh[f"gla_w_out{j}"] = f(inputs["gla_w_out"][j])
        sh[f"dsa_w_in{j}"] = f(inputs["dsa_w_in"][j])
        sh[f"dsa_kvn{j}"] = vecl(inputs["dsa_kv_norm"][j], 2)
        sh[f"dsa_kin{j}"] = vecl(inputs["dsa_kidx_norm"][j], 1)
        sh[f"dsa_w_uv{j}"] = f(inputs["dsa_w_uv"][j])
        sh[f"dsa_w_out{j}"] = f(inputs["dsa_w_out"][j])
    tp = np.arange(P)
    same = (tp[:, None] // CH) == (tp[None, :] // CH)
    sh["mrev"] = (same & (tp[:, None] > tp[None, :])).astype(np.float32)
    sh["cind"] = (tp[:, None] // CH == np.arange(2)[None, :]).astype(np.float32)
    sh["ident"] = np.eye(P, dtype=np.float32)
    rb = np.asarray(inputs["rel_bias"], dtype=np.float32)
    sh["b15"] = f(np.broadcast_to(rb[15][None, :], (P, 16)))
    sl = np.arange(P)[:, None, None]
    jj = np.arange(3)[None, :, None]
    ql = np.arange(QB)[None, None, :]
    rel = (jj - 1) * P + sl - ql
    bidx = t5_bucket_np(rel)
    sh["biasN"] = f(np.transpose(rb[bidx], (3, 0, 1, 2)))
    return sh


def kernel(**inputs):
    shared = prep_inputs(inputs)
    x = np.asarray(inputs["x"], dtype=np.float32)
    in_maps = []
    for b in range(NB):
        m = dict(shared)
        m["xT_in"] = np.ascontiguousarray(x[b].T)
        in_maps.append(m)
    nc = build_program()
    res = run_bass_kernel_spmd(nc, in_maps, core_ids=list(range(NB)))
    out = np.stack([np.ascontiguousarray(res.results[b]["outT"].T) for b in range(NB)], axis=0)
    return out.astype(np.float32)
```

```python
import math
from contextlib import ExitStack

import numpy as np
import concourse.bass as bass
import concourse.mybir as mybir
from concourse.bass_utils import run_bass_kernel_spmd

F32 = mybir.dt.float32
BF16 = mybir.dt.bfloat16
ALU = mybir.AluOpType
AF = mybir.ActivationFunctionType

P = 128
D = 2048
DT = D // P
T = 4096
NB = 4
DEPTH = 4
DFF = 5632
FT = DFF // P
CH = 64
EPS = 1e-6
GLA_IN = 6160
DSA_IN = 6544
KSEL = 256
NEG_ADM = -1.0e30
NEG_REP = -3.0e38
QB = 256
NDQ = 6
DBG = {}
TOPK_R0 = 65536.0
TOPK_NIT = 36


class Buf:
    __slots__ = ("w", "r")

    def __init__(self):
        self.w = None
        self.r = {}


class TB:
    def __init__(self, t):
        self.t = t
        self.b = Buf()


class Sched:
    def __init__(self, nc, es):
        self.nc = nc
        self.eng = {"pe": nc.tensor, "act": nc.scalar, "dve": nc.vector,
                    "pool": nc.gpsimd, "sp": nc.sync}
        names = ["pe", "act", "dve", "pool"]
        names += [f"qsp{i}" for i in range(NDQ)] + [f"qpool{i}" for i in range(NDQ)] + [f"qact{i}" for i in range(NDQ)]
        self.sem = {n: es.enter_context(nc.semaphore("s_" + n)) for n in names}
        self.cnt = dict.fromkeys(names, 0)
        self.seen = {e: dict.fromkeys(names, 0) for e in self.eng}
        self.dq = {"sp": 0, "pool": 0, "act": 0}

    def _wait(self, E, s, c):
        if s == E and c > self.cnt[s]:
            return
        if self.seen[E][s] < c:
            self.eng[E].wait_ge(self.sem[s], c)
            self.seen[E][s] = c

    def op(self, E, fn, reads=(), writes=(), inc=True, dma=False):
        if dma:
            slot = self.dq[E]
            self.dq[E] = (slot + 1) % NDQ
            s = f"q{E}{slot}"
            self._wait(E, s, self.cnt[s])
        else:
            s = E
        for b in reads:
            if b.w:
                self._wait(E, *b.w)
        for b in writes:
            if b.w:
                self._wait(E, *b.w)
            for rs, rc in b.r.items():
                self._wait(E, rs, rc)
        ins = fn()
        if inc:
            step = 16 if dma else 1
            self.cnt[s] += step
            ins.then_inc(self.sem[s], step)
            c = self.cnt[s]
        else:
            c = self.cnt[s] + 1
        for b in writes:
            b.w = (s, c)
            b.r = {}
        for b in reads:
            if b.r.get(s, 0) < c:
                b.r[s] = c
        return ins

    def sc_begin(self, name):
        if DBG.get("scopes"):
            cm = self.nc.named_scope(name)
            cm.__enter__()
            self._scopes = getattr(self, "_scopes", []) + [cm]

    def sc_end(self):
        if DBG.get("scopes"):
            self._scopes.pop().__exit__(None, None, None)

    def barrier(self):
        for E in self.eng:
            for s, c in self.cnt.items():
                if c:
                    self._wait(E, s, c)

    def sb(self, es, shape, dt, name):
        return TB(es.enter_context(self.nc.sbuf_tensor(name, list(shape), dt)))

    def ps(self, es, shape, dt, name):
        return TB(es.enter_context(self.nc.psum_tensor(name, list(shape), dt)))

    def dma(self, out_ap, in_ap, reads=(), writes=(), q="sp"):
        eng = self.eng[q]
        return self.op(q, lambda: eng.dma_start(out=out_ap, in_=in_ap), reads, writes, dma=True)

    def mm(self, out_tb, out_ap, pairs, reads, first=True, last=True):
        n = len(pairs)
        nc = self.nc
        for i, (l, r) in enumerate(pairs):
            st = first and i == 0
            sp = last and i == n - 1
            self.op("pe", lambda l=l, r=r, st=st, sp=sp: nc.tensor.matmul(out_ap, l, r, start=st, stop=sp),
                    reads=reads if i == 0 else (), writes=[out_tb.b], inc=sp)


class Rot:
    def __init__(self, items):
        self.items = items
        self.i = 0

    def next(self):
        it = self.items[self.i % len(self.items)]
        self.i += 1
        return it


_uid = [0]


def uid(p):
    _uid[0] += 1
    return f"{p}{_uid[0]}"


def phase_norm(S, src, dst, gw, KT, nfeat, dst_f32=False):
    S.sc_begin(uid("norm"))
    nc = S.nc
    TBK = 512
    sv = src.t.rearrange("(k p) t -> p k t", p=P)
    dv = dst.t.rearrange("(k p) t -> p k t", p=P)
    with ExitStack() as es:
        g = S.sb(es, [P, KT], F32, uid("ng"))
        ones = S.sb(es, [P, P], BF16, uid("nones"))
        S.dma(g.t[:], gw.t, writes=[g.b])
        S.op("dve", lambda: nc.vector.memset(ones.t[:], 1.0), writes=[ones.b])
        xs = Rot([S.sb(es, [P, KT, TBK], F32, uid("nx")) for _ in range(2)])
        sqs = Rot([S.sb(es, [P, KT, TBK], BF16, uid("nsq")) for _ in range(2)])
        hs = Rot([S.sb(es, [P, KT, TBK], F32 if dst_f32 else BF16, uid("nh")) for _ in range(2)])
        rss = Rot([S.sb(es, [P, TBK], F32, uid("nrs")) for _ in range(2)])
        pss = Rot([S.ps(es, [P, TBK], F32, uid("nps")) for _ in range(2)])
        def load_x(tb):
            x = xs.next()
            S.dma(x.t[:], sv[:, :, tb * TBK:(tb + 1) * TBK], reads=[src.b], writes=[x.b])
            return x
        x_next = load_x(0)
        for tb in range(T // TBK):
            sl = slice(tb * TBK, (tb + 1) * TBK)
            x = x_next
            if tb + 1 < T // TBK:
                x_next = load_x(tb + 1)
            h = hs.next()
            sq = sqs.next()
            rs = rss.next()
            ps = pss.next()
            S.op("act", lambda x=x, sq=sq: nc.scalar.activation(out=sq.t[:], in_=x.t[:], func=AF.Square),
                 reads=[x.b], writes=[sq.b])
            S.mm(ps, ps.t[:], [(ones.t[:], sq.t[:, k, :]) for k in range(KT)], reads=[ones.b, sq.b])
            S.op("act", lambda ps=ps, rs=rs: nc.scalar.activation(out=rs.t[:], in_=ps.t[:], func=AF.Sqrt,
                                                                 bias=EPS, scale=1.0 / nfeat),
                 reads=[ps.b], writes=[rs.b])
            S.op("dve", lambda rs=rs: nc.vector.reciprocal(rs.t[:], rs.t[:]), reads=[rs.b], writes=[rs.b])
            for k in range(KT):
                S.op("dve", lambda k=k, x=x, h=h, rs=rs: nc.vector.scalar_tensor_tensor(
                    out=h.t[:, k, :], in0=x.t[:, k, :], scalar=g.t[:, k:k + 1], in1=rs.t[:],
                    op0=ALU.mult, op1=ALU.mult), reads=[x.b, g.b, rs.b], writes=[h.b])
            S.dma(dv[:, :, sl], h.t[:], reads=[h.b], writes=[dst.b])
    S.barrier()
    S.sc_end()


def phase_linear(S, hT, KT, W, c0, ncols, mode, sink, TSB=2048):
    phase_linear_multi(S, hT, KT, W, [(c0, ncols, mode, sink)], TSB)


def phase_linear_multi(S, hT, KT, W, jobs, TSB=2048):
    nc = S.nc
    S.sc_begin(uid("lin_" + "_".join(f"{m}{c}" for c, _, m, _ in jobs) + "_"))
    hv = hT.t.rearrange("(k p) t -> p k t", p=P)
    wv = W.t.rearrange("(k p) n -> p k n", p=P)
    modes = {m for _, _, m, _ in jobs}
    with ExitStack() as es:
        hsb = S.sb(es, [P, KT, TSB], BF16, uid("lh"))
        wsm = {}
        if "fm" in modes:
            wsm["fm"] = Rot([S.sb(es, [P, KT, 128], BF16, uid("lwf")) for _ in range(3)])
        if "tm" in modes:
            wsm["tm"] = Rot([S.sb(es, [P, KT, 512], BF16, uid("lwt")) for _ in range(2 if len(modes) > 1 else 3)])
        pss = Rot([S.ps(es, [P, 512], F32, uid("lps")) for _ in range(4)])
        for sbk in range(T // TSB):
            ts0 = sbk * TSB
            for q4 in range(4):
                qs = TSB // 4
                S.dma(hsb.t[:, :, q4 * qs:(q4 + 1) * qs], hv[:, :, ts0 + q4 * qs: ts0 + (q4 + 1) * qs],
                      reads=[hT.b], writes=[hsb.b])
            for (c0, ncols, mode, sink) in jobs:
                nw = 128 if mode == "fm" else 512
                for n0 in range(0, ncols, nw):
                    nsz = min(nw, ncols - n0)
                    w = wsm[mode].next()
                    S.dma(w.t[:, :, :nsz], wv[:, :, c0 + n0:c0 + n0 + nsz], reads=[W.b], writes=[w.b], q="pool")
                    if mode == "fm":
                        for t0 in range(0, TSB, 512):
                            ps = pss.next()
                            S.mm(ps, ps.t[:nsz, :], [(w.t[:, k, :nsz], hsb.t[:, k, t0:t0 + 512]) for k in range(KT)],
                                 reads=[w.b, hsb.b])
                            sink(ps, n0, nsz, ts0 + t0)
                    else:
                        for t0 in range(0, TSB, P):
                            ps = pss.next()
                            S.mm(ps, ps.t[:, :nsz], [(hsb.t[:, k, t0:t0 + P], w.t[:, k, :nsz]) for k in range(KT)],
                                 reads=[w.b, hsb.b])
                            sink(ps, ts0 + t0, n0, nsz)
        for (_, _, _, sink) in jobs:
            if hasattr(sink, "flush"):
                sink.flush()
    S.barrier()
    S.sc_end()


def make_store_sink(S, es, dst, mode, dt, func=AF.Copy, scale=1.0, roff=0):
    nc = S.nc
    obs = Rot([S.sb(es, [P, 512], dt, uid("so")) for _ in range(3)])

    def sink_fm(ps, n0, nsz, t0):
        o = obs.next()
        S.op("act", lambda: nc.scalar.activation(out=o.t[:nsz, :], in_=ps.t[:nsz, :], func=func, scale=scale),
             reads=[ps.b], writes=[o.b])
        S.dma(dst.t[roff + n0:roff + n0 + nsz, t0:t0 + 512], o.t[:nsz, :], reads=[o.b], writes=[dst.b])

    def sink_tm(ps, t0, n0, nsz):
        o = obs.next()
        S.op("act", lambda: nc.scalar.activation(out=o.t[:, :nsz], in_=ps.t[:, :nsz], func=func, scale=scale),
             reads=[ps.b], writes=[o.b])
        S.dma(dst.t[t0:t0 + P, roff + n0:roff + n0 + nsz], o.t[:, :nsz], reads=[o.b], writes=[dst.b])

    return sink_fm if mode == "fm" else sink_tm


def make_resid_sink(S, es, xT, slab=2048):
    nc = S.nc
    xin = Rot([S.sb(es, [P, slab], F32, uid("rx")) for _ in range(2)])
    cur = {}

    def sink(ps, n0, nsz, t0):
        off = t0 % slab
        if off == 0:
            xt = xin.next()
            S.dma(xt.t[:nsz, :], xT.t[n0:n0 + nsz, t0:t0 + slab], reads=[xT.b], writes=[xt.b])
            cur["xt"] = xt
        xt = cur["xt"]
        S.op("dve", lambda: nc.vector.tensor_tensor(out=xt.t[:nsz, off:off + 512], in0=xt.t[:nsz, off:off + 512],
                                                    in1=ps.t[:nsz, :], op=ALU.add), reads=[xt.b, ps.b], writes=[xt.b])
        if off + 512 == slab:
            S.dma(xT.t[n0:n0 + nsz, t0 + 512 - slab:t0 + 512], xt.t[:nsz, :], reads=[xt.b], writes=[xT.b])

    return sink


def phase_ffn(S, hT, xT, w1, w3, w2):
    nc = S.nc
    S.sc_begin(uid("ffn"))
    TSB = 1024
    hv = hT.t.rearrange("(k p) t -> p k t", p=P)
    w1v = w1.t.rearrange("(k p) n -> p k n", p=P)
    w3v = w3.t.rearrange("(k p) n -> p k n", p=P)
    w2v = w2.t.rearrange("(f p) n -> p f n", p=P)
    with ExitStack() as es:
        hsb = S.sb(es, [P, DT, TSB], BF16, uid("fh"))
        gT = S.sb(es, [P, FT, TSB], BF16, uid("fg"))
        w13 = Rot([(S.sb(es, [P, DT, P], BF16, uid("fw1")), S.sb(es, [P, DT, P], BF16, uid("fw3")))
                   for _ in range(3)])
        w2s = Rot([S.sb(es, [P, FT, P], BF16, uid("fw2")) for _ in range(2)])
        s1s = Rot([S.sb(es, [P, 512], F32, uid("fs1")) for _ in range(2)])
        p1s = Rot([S.ps(es, [P, 512], F32, uid("fp1")) for _ in range(2)])
        p3s = Rot([S.ps(es, [P, 512], F32, uid("fp3")) for _ in range(2)])
        pys = Rot([S.ps(es, [P, 512], F32, uid("fpy")) for _ in range(2)])
        rsink = make_resid_sink(S, es, xT, slab=TSB)
        for sbk in range(T // TSB):
            ts0 = sbk * TSB
            for q4 in range(2):
                qs = TSB // 2
                S.dma(hsb.t[:, :, q4 * qs:(q4 + 1) * qs], hv[:, :, ts0 + q4 * qs:ts0 + (q4 + 1) * qs],
                      reads=[hT.b], writes=[hsb.b])
            for ft in range(FT):
                a, b = w13.next()
                S.dma(a.t[:], w1v[:, :, ft * P:(ft + 1) * P], reads=[w1.b], writes=[a.b], q="pool")
                S.dma(b.t[:], w3v[:, :, ft * P:(ft + 1) * P], reads=[w3.b], writes=[b.b], q="pool")
                for nb in range(TSB // 512):
                    tsl = slice(nb * 512, (nb + 1) * 512)
                    p1 = p1s.next()
                    p3 = p3s.next()
                    s1 = s1s.next()
                    S.mm(p1, p1.t[:], [(a.t[:, k, :], hsb.t[:, k, tsl]) for k in range(DT)], reads=[a.b, hsb.b])
                    S.mm(p3, p3.t[:], [(b.t[:, k, :], hsb.t[:, k, tsl]) for k in range(DT)], reads=[b.b, hsb.b])
                    S.op("act", lambda p1=p1, s1=s1: nc.scalar.activation(out=s1.t[:], in_=p1.t[:], func=AF.Silu),
                         reads=[p1.b], writes=[s1.b])
                    S.op("dve", lambda s1=s1, p3=p3, ft=ft, tsl=tsl: nc.vector.tensor_tensor(
                        out=gT.t[:, ft, tsl], in0=s1.t[:], in1=p3.t[:], op=ALU.mult),
                        reads=[s1.b, p3.b], writes=[gT.b])
            for dt_ in range(DT):
                w = w2s.next()
                S.dma(w.t[:, :FT // 2, :], w2v[:, :FT // 2, dt_ * P:(dt_ + 1) * P], reads=[w2.b], writes=[w.b], q="pool")
                S.dma(w.t[:, FT // 2:, :], w2v[:, FT // 2:, dt_ * P:(dt_ + 1) * P], reads=[w2.b], writes=[w.b], q="pool")
                for nb in range(TSB // 512):
                    tsl = slice(nb * 512, (nb + 1) * 512)
                    py = pys.next()
                    S.mm(py, py.t[:], [(w.t[:, f, :], gT.t[:, f, tsl]) for f in range(FT)], reads=[w.b, gT.b])
                    rsink(py, dt_ * P, P, ts0 + nb * 512)
    S.barrier()
    S.sc_end()


def phase_gla(S, es0, hT, xT, C, j, dram):
    nc = S.nc
    w_in = C["gla_w_in"][j]
    qT = dram("g_qT", [1024, T], BF16)
    sgT = dram("g_sgT", [2048, T], BF16)
    aT = dram("g_aT", [16, T], F32)
    ktm = dram("g_k", [T, 1024], F32)
    vtm = dram("g_v", [T, 2048], BF16)
    kd = dram("g_kd", [T, 1024], BF16)
    decS = dram("g_decS", [P, (T // P) * 16], F32)
    ogT = dram("g_ogT", [2048, T], BF16)
    with ExitStack() as es:
        phase_linear_multi(S, hT, DT, w_in, [
            (0, 1024, "fm", make_store_sink(S, es, qT, "fm", BF16, scale=1.0 / 16.0)),
            (4096, 2048, "fm", make_store_sink(S, es, sgT, "fm", BF16, func=AF.Silu)),
            (6144, 16, "fm", make_store_sink(S, es, aT, "fm", F32)),
            (1024, 1024, "tm", make_store_sink(S, es, ktm, "tm", F32)),
            (2048, 2048, "tm", make_store_sink(S, es, vtm, "tm", BF16)),
        ])
    S.sc_begin(uid("gla_gate"))
    with ExitStack() as es:
        waug = S.sb(es, [32, 1024], F32, uid("gw"))
        S.dma(waug.t[:], C["gla_waug"][j].t, writes=[waug.b])
        mrev = S.sb(es, [P, P], F32, uid("gm"))
        S.dma(mrev.t[:], C["mrev"].t, writes=[mrev.b])
        cind = S.sb(es, [P, 2], F32, uid("gci"))
        S.dma(cind.t[:], C["cind"].t, writes=[cind.b])
        aaug = Rot([S.sb(es, [32, P], F32, uid("ga")) for _ in range(2)])
        for a in aaug.items:
            S.op("dve", lambda a=a: nc.vector.memset(a.t[:], 1.0), writes=[a.b])
        Ls = Rot([S.sb(es, [P, 1024], F32, uid("gL")) for _ in range(2)])
        dec = S.sb(es, [P, 1024], F32, uid("gdec"))
        kts = Rot([S.sb(es, [P, 1024], F32, uid("gk")) for _ in range(2)])
        kds = Rot([S.sb(es, [P, 1024], BF16, uid("gkd")) for _ in range(2)])
        dcs = S.sb(es, [P, (T // P) * 16], F32, uid("gdcs"))
        pz = Rot([S.ps(es, [P, 512], F32, uid("gpz")) for _ in range(4)])
        pc = S.ps(es, [P, 16], F32, uid("gpc"))
        for tt in range(T // P):
            tsl = slice(tt * P, (tt + 1) * P)
            a = aaug.next()
            L = Ls.next()
            kt_ = kts.next()
            kdt = kds.next()
            S.dma(a.t[0:16, :], aT.t[:, tsl], reads=[aT.b], writes=[a.b])
            S.dma(kt_.t[:], ktm.t[tsl, :], reads=[ktm.b], writes=[kt_.b])
            for hb in range(2):
                z = pz.next()
                S.mm(z, z.t[:], [(a.t[:, :], waug.t[:, hb * 512:(hb + 1) * 512])], reads=[a.b, waug.b])
                S.op("act", lambda z=z, L=L, hb=hb: nc.scalar.activation(
                    out=L.t[:, hb * 512:(hb + 1) * 512], in_=z.t[:], func=AF.Exp, scale=-1.0),
                    reads=[z.b], writes=[L.b])
            S.op("act", lambda L=L: nc.scalar.activation(out=L.t[:], in_=L.t[:], func=AF.Ln, bias=1.0),
                 reads=[L.b], writes=[L.b])
            for hb in range(2):
                z = pz.next()
                S.mm(z, z.t[:], [(mrev.t[:], L.t[:, hb * 512:(hb + 1) * 512])], reads=[mrev.b, L.b])
                S.op("act", lambda z=z, hb=hb: nc.scalar.activation(
                    out=dec.t[:, hb * 512:(hb + 1) * 512], in_=z.t[:], func=AF.Exp, scale=-1.0 / 16.0),
                    reads=[z.b], writes=[dec.b])
            S.op("dve", lambda kt_=kt_, kdt=kdt: nc.vector.tensor_tensor(out=kdt.t[:], in0=kt_.t[:], in1=dec.t[:],
                                                                          op=ALU.mult),
                 reads=[kt_.b, dec.b], writes=[kdt.b])
            S.dma(kd.t[tsl, :], kdt.t[:], reads=[kdt.b], writes=[kd.b])
            for k8 in range(8):
                S.mm(pc, pc.t[:, k8 * 2:k8 * 2 + 2], [(L.t[:, k8 * P:(k8 + 1) * P], cind.t[:])], reads=[L.b, cind.b])
            S.op("act", lambda tt=tt: nc.scalar.activation(out=dcs.t[:, tt * 16:(tt + 1) * 16], in_=pc.t[:],
                                                           func=AF.Exp, scale=-1.0 / 16.0),
                 reads=[pc.b], writes=[dcs.b])
        S.dma(decS.t, dcs.t[:], reads=[dcs.b], writes=[decS.b])
    S.barrier()
    S.sc_end()
    S.sc_begin(uid("gla_scan"))
    with ExitStack() as es:
        dcs = S.sb(es, [P, (T // P) * 16], F32, uid("sdcs"))
        S.dma(dcs.t[:], decS.t, reads=[decS.b], writes=[dcs.b])
        gn = S.sb(es, [P, 4], F32, uid("sgn"))
        S.dma(gn.t[:], C["gla_gn"][j].t, writes=[gn.b])
        ones = S.sb(es, [P, P], F32, uid("sones"))
        S.op("dve", lambda: nc.vector.memset(ones.t[:], 1.0), writes=[ones.b])
        St = [S.sb(es, [P, 512], F32, uid("sS")) for _ in range(8)]
        Sb = [S.sb(es, [P, 512], BF16, uid("sSb")) for _ in range(8)]
        for s_ in St:
            S.op("dve", lambda s_=s_: nc.vector.memset(s_.t[:], 0.0), writes=[s_.b])
        qbs = Rot([S.sb(es, [P, 8, 512], BF16, uid("sq")) for _ in range(2)])
        sgs = Rot([S.sb(es, [P, 16, 512], BF16, uid("ssg")) for _ in range(2)])
        ogs = Rot([S.sb(es, [P, 16, 512], BF16, uid("sog")) for _ in range(2)])
        kdb = Rot([S.sb(es, [P, 1024], BF16, uid("skd")) for _ in range(2)])
        vb = Rot([S.sb(es, [P, 2048], BF16, uid("sv")) for _ in range(2)])
        sq = S.sb(es, [P, 16, CH], F32, uid("ssq"))
        rs = S.sb(es, [P, 4, CH], F32, uid("srs"))
        tmpb = S.sb(es, [P, 16, CH], F32, uid("stmpb"))
        pu = Rot([S.ps(es, [P, 512], F32, uid("spu")) for _ in range(3)])
        po = [S.ps(es, [P, 8, CH], F32, uid("spo")) for _ in range(2)]
        pss = S.ps(es, [P, 4, CH], F32, uid("spss"))
        qv = qT.t.rearrange("(k p) t -> p k t", p=P)
        sgv = sgT.t.rearrange("(k p) t -> p k t", p=P)
        ogv = ogT.t.rearrange("(k p) t -> p k t", p=P)
        NCH = T // CH
        Sb2 = [Sb, [S.sb(es, [P, 512], BF16, uid("sSb2")) for _ in range(8)]]
        po2 = [po, [S.ps(es, [P, 8, CH], F32, uid("spo2")) for _ in range(2)]]
        tiles = {}
        blocks = {}

        def get_tile(tt):
            if tt not in tiles:
                tsl = slice(tt * P, (tt + 1) * P)
                kdt = kdb.next()
                vt_ = vb.next()
                S.dma(kdt.t[:], kd.t[tsl, :], reads=[kd.b], writes=[kdt.b])
                S.dma(vt_.t[:], vtm.t[tsl, :], reads=[vtm.b], writes=[vt_.b])
                tiles.clear()
                tiles[tt] = (kdt, vt_)
            return tiles[tt]

        def get_block(blk):
            if blk not in blocks:
                bsl = slice(blk * 512, (blk + 1) * 512)
                qb = qbs.next()
                sg = sgs.next()
                og = ogs.next()
                S.dma(qb.t[:], qv[:, :, bsl], reads=[qT.b], writes=[qb.b])
                S.dma(sg.t[:], sgv[:, :, bsl], reads=[sgT.b], writes=[sg.b])
                for i16 in range(16):
                    S.op("dve", lambda i16=i16: nc.vector.tensor_scalar_mul(
                        sg.t[:, i16, :], sg.t[:, i16, :], gn.t[:, i16 % 4:i16 % 4 + 1]),
                        reads=[sg.b, gn.b], writes=[sg.b])
                blocks.clear()
                blocks[blk] = (qb, sg, og)
            return blocks[blk]

        def stage_state(c):
            tt, c2 = c // 2, c % 2
            cs = c2 * CH
            kdt, vt_ = get_tile(tt)
            sbs = Sb2[c % 2]
            for h in range(4):
                for kt2 in range(2):
                    i8 = h * 2 + kt2
                    u = pu.next()
                    S.mm(u, u.t[:], [(kdt.t[cs:cs + CH, i8 * P:(i8 + 1) * P], vt_.t[cs:cs + CH, h * 512:(h + 1) * 512])],
                         reads=[kdt.b, vt_.b])
                    didx = tt * 16 + i8 * 2 + c2
                    S.op("dve", lambda: nc.vector.scalar_tensor_tensor(
                        out=St[i8].t[:], in0=St[i8].t[:], scalar=dcs.t[:, didx:didx + 1], in1=u.t[:],
                        op0=ALU.mult, op1=ALU.add), reads=[St[i8].b, dcs.b, u.b], writes=[St[i8].b])
                    S.op("act", lambda: nc.scalar.copy(out=sbs[i8].t[:], in_=St[i8].t[:]),
                         reads=[St[i8].b], writes=[sbs[i8].b])

        def stage_read_pe(c):
            blk = c // 8
            qb, sg, og = get_block(blk)
            ccol = slice((c % 8) * CH, (c % 8) * CH + CH)
            sbs = Sb2[c % 2]
            pp = po2[c % 2]
            for h in range(4):
                for vt4 in range(4):
                    i16 = h * 4 + vt4
                    pt = pp[i16 // 8]
                    S.mm(pt, pt.t[:, i16 % 8, :],
                         [(sbs[h * 2 + kt2].t[:, vt4 * P:(vt4 + 1) * P], qb.t[:, h * 2 + kt2, ccol]) for kt2 in range(2)],
                         reads=[sbs[h * 2].b, sbs[h * 2 + 1].b, qb.b])
            for half in range(2):
                S.op("act", lambda half=half: nc.scalar.activation(
                    out=sq.t[:, half * 8:(half + 1) * 8, :], in_=pp[half].t[:], func=AF.Square),
                    reads=[pp[half].b], writes=[sq.b])
            for h in range(4):
                S.mm(pss, pss.t[:, h, :], [(ones.t[:], sq.t[:, h * 4 + v4, :]) for v4 in range(4)],
                     reads=[ones.b, sq.b])
            S.op("act", lambda: nc.scalar.activation(out=rs.t[:], in_=pss.t[:], func=AF.Sqrt,
                                                     bias=EPS, scale=1.0 / 512.0),
                 reads=[pss.b], writes=[rs.b])

        def stage_read_dve(c):
            blk = c // 8
            qb, sg, og = get_block(blk)
            ccol = slice((c % 8) * CH, (c % 8) * CH + CH)
            pp = po2[c % 2]
            S.op("dve", lambda: nc.vector.reciprocal(rs.t[:], rs.t[:]), reads=[rs.b], writes=[rs.b])
            for half in range(2):
                S.op("dve", lambda half=half: nc.vector.tensor_tensor(
                    out=tmpb.t[:, half * 8:(half + 1) * 8, :], in0=pp[half].t[:],
                    in1=sg.t[:, half * 8:(half + 1) * 8, ccol], op=ALU.mult),
                    reads=[pp[half].b, sg.b], writes=[tmpb.b])
            for h in range(4):
                S.op("dve", lambda h=h: nc.vector.tensor_tensor(
                    out=og.t[:, h * 4:(h + 1) * 4, ccol], in0=tmpb.t[:, h * 4:(h + 1) * 4, :],
                    in1=rs.t[:, h, :].unsqueeze(1).to_broadcast([P, 4, CH]), op=ALU.mult),
                    reads=[tmpb.b, rs.b], writes=[og.b])
            if c % 8 == 7:
                bsl = slice(blk * 512, (blk + 1) * 512)
                S.dma(ogv[:, :, bsl], og.t[:], reads=[og.b], writes=[ogT.b])

        stage_state(0)
        for c in range(NCH):
            stage_read_pe(c)
            if c + 1 < NCH:
                stage_state(c + 1)
            stage_read_dve(c)
    S.barrier()
    S.sc_end()
    with ExitStack() as es:
        phase_linear(S, ogT, DT, C["gla_w_out"][j], 0, D, "fm", make_resid_sink(S, es, xT))


def topk_threshold(S, acc, work, mx, thr, nk):
    nc = S.nc
    S.op("act", lambda: nc.scalar.copy(out=work.t[:, :nk], in_=acc.t[:, :nk]), reads=[acc.b], writes=[work.b])
    for r8 in range(KSEL // 8):
        S.op("dve", lambda: nc.vector.max(out=mx.t[:], in_=work.t[:, :nk]), reads=[work.b], writes=[mx.b])
        if r8 < KSEL // 8 - 1:
            S.op("dve", lambda: nc.vector.match_replace(
                out=work.t[:, :nk], in_to_replace=mx.t[:], in_values=work.t[:, :nk], imm_value=NEG_REP),
                reads=[work.b, mx.b], writes=[work.b])
    S.op("dve", lambda: nc.vector.tensor_reduce(out=thr.t[:], in_=mx.t[:], axis=mybir.AxisListType.X, op=ALU.min),
         reads=[mx.b], writes=[thr.b])
    S.op("dve", lambda: nc.vector.tensor_scalar_max(thr.t[:], thr.t[:], -1.0e29), reads=[thr.b], writes=[thr.b])


def phase_dsa(S, es0, hT, xT, C, j, dram):
    nc = S.nc
    w_in = C["dsa_w_in"][j]
    qT = dram("d_qT", [4096, T], BF16)
    cTr = dram("d_cTr", [256, T], F32)
    cTn = dram("d_cTn", [256, T], BF16)
    qiT = dram("d_qiT", [2048, T], BF16)
    kiTr = dram("d_kiTr", [P, T], F32)
    kiTn = dram("d_kiTn", [P, T], BF16)
    wi = dram("d_wi", [T, 16], F32)
    oT = dram("d_oT", [2048, T], BF16)
    with ExitStack() as es:
        phase_linear_multi(S, hT, DT, w_in, [
            (0, 4096, "fm", make_store_sink(S, es, qT, "fm", BF16, scale=1.0 / 16.0)),
            (4096, 256, "fm", make_store_sink(S, es, cTr, "fm", F32)),
            (4352, 2048, "fm", make_store_sink(S, es, qiT, "fm", BF16)),
            (6400, 128, "fm", make_store_sink(S, es, kiTr, "fm", F32)),
            (6528, 16, "tm", make_store_sink(S, es, wi, "tm", F32)),
        ])
    phase_norm(S, cTr, cTn, C["dsa_kvn"][j], 2, 256)
    phase_norm(S, kiTr, kiTn, C["dsa_kin"][j], 1, 128)

    NT = T // P
    with ExitStack() as es:
        ident = S.sb(es, [P, P], BF16, uid("did"))
        S.dma(ident.t[:], C["ident"].t, writes=[ident.b], q="pool")
        ones = S.sb(es, [P, P], BF16, uid("dones"))
        S.op("dve", lambda: nc.vector.memset(ones.t[:], 1.0), writes=[ones.b])
        b15 = S.sb(es, [P, 16], F32, uid("db15"))
        S.dma(b15.t[:], C["b15"].t, writes=[b15.b])
        kiT = S.sb(es, [P, T], BF16, uid("dki"))
        S.dma(kiT.t[:], kiTn.t, reads=[kiTn.b], writes=[kiT.b])
        cT = S.sb(es, [P, 2, T], BF16, uid("dcT"))
        S.dma(cT.t[:], cTn.t.rearrange("(k p) t -> p k t", p=P), reads=[cTn.b], writes=[cT.b])
        wis = S.sb(es, [P, NT, 16], F32, uid("dwi"))
        with nc.allow_non_contiguous_dma(reason="small per-token head weights"):
            S.dma(wis.t[:], wi.t.rearrange("(n p) h -> p n h", p=P), reads=[wi.b], writes=[wis.b])
        wuv = S.sb(es, [P, 16, 2, P], BF16, uid("dwuv"))
        for h in range(16):
            S.dma(wuv.t[:, h, :, :], C["dsa_w_uv"][j].t[h].rearrange("(k p) e -> p k e", p=P),
                  writes=[wuv.b], q="pool")
        cn = S.sb(es, [P, NT, 256], BF16, uid("dcn"))
        pG = Rot([S.ps(es, [P, 512], F32, uid("dpg")) for _ in range(2)])
        pGB = Rot([S.ps(es, [P, 512], F32, uid("dpgb")) for _ in range(3)])
        for st in range(NT):
            pt = pG.next()
            for k2 in range(2):
                S.mm(pt, pt.t[:, k2 * P:(k2 + 1) * P], [(cT.t[:, k2, st * P:(st + 1) * P], ident.t[:])],
                     reads=[cT.b, ident.b])
            S.op("act", lambda st=st, pt=pt: nc.scalar.copy(out=cn.t[:, st, :], in_=pt.t[:, 0:256]),
                 reads=[pt.b], writes=[cn.b])
        qi = S.sb(es, [P, 16, QB], BF16, uid("dqi"))
        qq = S.sb(es, [P, 2, 16 * QB], BF16, uid("dq"))
        maskT = S.sb(es, [P, NT * QB], BF16, uid("dmT"))
        acc = S.sb(es, [P, T], F32, uid("dacc"))
        junk = S.sb(es, [P, T], BF16, uid("djunk"))
        lo_ = S.sb(es, [P, 1], F32, uid("dlo"))
        mid_ = S.sb(es, [P, 1], F32, uid("dmid"))
        cnt_ = S.sb(es, [P, 1], F32, uid("dcnt"))
        ged_ = S.sb(es, [P, 1], F32, uid("dged"))
        mask = S.sb(es, [P, T], BF16, uid("dmask"))
        mx = S.sb(es, [P, 8], F32, uid("dmx"))
        thr = S.sb(es, [P, 1], F32, uid("dthr"))
        rl = Rot([S.sb(es, [P, 512], F32, uid("drl")) for _ in range(3)])
        Es = Rot([S.sb(es, [P, 512], BF16, uid("dE")) for _ in range(4)])
        Ems = Rot([S.sb(es, [P, 512], BF16, uid("dEm")) for _ in range(4)])
        tl = Rot([S.sb(es, [P, 512], F32, uid("dtl")) for _ in range(2)])
        bn = Rot([S.sb(es, [P, 3, 2, QB], F32, uid("dbn")) for _ in range(3)])
        rden = S.sb(es, [P, 512], F32, uid("drd"))
        obT = S.sb(es, [P, 2, 512], BF16, uid("dob"))
        ohs = Rot([S.sb(es, [P, 512], BF16, uid("doh")) for _ in range(2)])
        pO2 = [S.ps(es, [P, 512], F32, uid("dpO")) for _ in range(2)]
        pD = S.ps(es, [P, 512], F32, uid("dpD"))
        qiv = qiT.t.rearrange("(h p) t -> p h t", p=P)
        qv4 = qT.t.rearrange("(h k p) t -> p k h t", k=2, p=P)
        LA = 2

        negb15 = S.sb(es, [P, 16], F32, uid("dnb15"))
        S.op("dve", lambda: nc.vector.tensor_scalar_mul(negb15.t[:], b15.t[:], -1.0), reads=[b15.b], writes=[negb15.b])
        expbs = Rot([S.sb(es, [P, 3, 2, QB], BF16, uid("dexpb")) for _ in range(2)])
        usb = S.sb(es, [P, 512], F32, uid("dus"))
        dsb = S.sb(es, [P, 512], F32, uid("dds"))

        def load_bias(hp):
            bt = bn.next()
            S.dma(bt.t[:], C["biasN"].t[2 * hp:2 * hp + 2].rearrange("h p a b -> p a h b"), writes=[bt.b])
            return bt

        def adjust_bias(bt, hp):
            eb = expbs.next()
            for a_ in range(2):
                S.op("act", lambda a_=a_: nc.scalar.activation(
                    out=eb.t[:, :, a_, :], in_=bt.t[:, :, a_, :], func=AF.Exp,
                    bias=negb15.t[:, 2 * hp + a_:2 * hp + a_ + 1]),
                    reads=[bt.b, negb15.b], writes=[eb.b])
            return eb

        maskTs = [maskT, S.sb(es, [P, NT * QB], BF16, uid("dmT2"))]
        FMA_ENG = DBG.get("fma_eng", "dve")

        def sel_idx(qb_, qt2):
            mT = maskTs[qb_ % 2]
            q0 = qb_ * QB
            qsl = slice(q0, q0 + QB)
            NS = (q0 + QB) // P
            if qt2 == 0:
                S.dma(qi.t[:], qiv[:, :, qsl], reads=[qiT.b], writes=[qi.b])
                S.op("dve", lambda: nc.vector.memset(mT.t[:, :NS * QB], 0.0), writes=[mT.b])
            qt = q0 // P + qt2
            nk = (qt + 1) * P
            qcol = slice(qt2 * P, (qt2 + 1) * P)
            for h in range(16):
                for k0 in range(0, nk, 512):
                    w_ = min(512, nk - k0)
                    p_ = pG.next()
                    r_ = rl.next()
                    S.mm(p_, p_.t[:, :w_], [(qi.t[:, h, qcol], kiT.t[:, k0:k0 + w_])], reads=[qi.b, kiT.b])
                    S.op("act", lambda: nc.scalar.activation(
                        out=r_.t[:, :w_], in_=p_.t[:, :w_], func=AF.Relu), reads=[p_.b], writes=[r_.b])
                    if h == 0:
                        S.op("dve", lambda: nc.vector.tensor_scalar_mul(
                            acc.t[:, k0:k0 + w_], r_.t[:, :w_], wis.t[:, qt, 0:1]),
                            reads=[r_.b, wis.b], writes=[acc.b])
                    else:
                        S.op("dve", lambda: nc.vector.scalar_tensor_tensor(
                            out=acc.t[:, k0:k0 + w_], in0=r_.t[:, :w_], scalar=wis.t[:, qt, h:h + 1],
                            in1=acc.t[:, k0:k0 + w_], op0=ALU.mult, op1=ALU.add),
                            reads=[r_.b, wis.b, acc.b], writes=[acc.b])
            S.op("dve", lambda: nc.vector.memset(acc.t[0:CH, nk - CH:nk], NEG_ADM), writes=[acc.b])

        def sel_topk(qb_, qt2):
            qt = qb_ * (QB // P) + qt2
            nk = (qt + 1) * P
            if nk > KSEL:
                S.op("dve", lambda: nc.vector.memset(lo_.t[:], -TOPK_R0), writes=[lo_.b])
                d = 2.0 * TOPK_R0
                for it in range(TOPK_NIT):
                    d *= 0.5
                    S.op("dve", lambda d=d: nc.vector.tensor_scalar_add(mid_.t[:], lo_.t[:], d), reads=[lo_.b], writes=[mid_.b])
                    S.op("dve", lambda: nc.vector.memset(cnt_.t[:], 0.0), writes=[cnt_.b])
                    yield
                    S.op("dve", lambda: nc.vector.tensor_scalar(
                        junk.t[:, :nk], acc.t[:, :nk], mid_.t[:, 0:1], 0.0, op0=ALU.is_ge, op1=ALU.add,
                        accum_out=cnt_.t[:, 0:1]), reads=[acc.b, mid_.b, cnt_.b], writes=[junk.b, cnt_.b])
                    yield
                    S.op("dve", lambda d=d: nc.vector.tensor_scalar(
                        ged_.t[:], cnt_.t[:], KSEL - 0.5, d, op0=ALU.is_ge, op1=ALU.mult),
                        reads=[cnt_.b], writes=[ged_.b])
                    S.op("dve", lambda: nc.vector.tensor_tensor(out=lo_.t[:], in0=lo_.t[:], in1=ged_.t[:], op=ALU.add),
                         reads=[lo_.b, ged_.b], writes=[lo_.b])
                    yield
                S.op("dve", lambda: nc.vector.tensor_scalar(
                    mask.t[:, :nk], acc.t[:, :nk], lo_.t[:, 0:1], None, op0=ALU.is_ge),
                    reads=[acc.b, lo_.b], writes=[mask.b])
            else:
                S.op("dve", lambda: nc.vector.tensor_scalar(
                    mask.t[:, :nk], acc.t[:, :nk], -1.0e29, None, op0=ALU.is_ge),
                    reads=[acc.b], writes=[mask.b])
            yield

        def sel_maskT(qb_, qt2):
            mT = maskTs[qb_ % 2]
            qt = qb_ * (QB // P) + qt2
            qc0 = qt2 * P
            for st in range(qt + 1):
                pt = pG.next()
                S.mm(pt, pt.t[:, 0:P], [(mask.t[:, st * P:(st + 1) * P], ident.t[:])], reads=[mask.b, ident.b])
                S.op("act", lambda: nc.scalar.copy(
                    out=mT.t[:, st * QB + qc0:st * QB + qc0 + P], in_=pt.t[:, 0:P]),
                    reads=[pt.b], writes=[mT.b])

        def gen_attn(qb_):
            mT = maskTs[qb_ % 2]
            q0 = qb_ * QB
            qsl = slice(q0, q0 + QB)
            NS = (q0 + QB) // P
            for k2 in range(2):
                S.dma(qq.t[:, k2, :].rearrange("p (h q) -> p h q", h=16), qv4[:, k2, :, qsl], reads=[qT.b], writes=[qq.b])
            jb = q0 // P - 1
            bt_next = load_bias(0)
            eb_next = adjust_bias(bt_next, 0)
            pe_ = nc.gpsimd
            for hp in range(8):
                eb = eb_next
                if hp + 1 < 8:
                    bt_next = load_bias(hp + 1)

                def emitL(st):
                    l_ = pGB.next()
                    S.mm(l_, l_.t[:, :], [(cT.t[:, k2, st * P:(st + 1) * P], qq.t[:, k2, hp * 512:(hp + 1) * 512]) for k2 in range(2)],
                         reads=[cT.b, qq.b])
                    return l_
                ls = [emitL(g) for g in range(min(LA, NS))]
                for st in range(NS):
                    l_ = ls[st]
                    if st + LA < NS:
                        ls.append(emitL(st + LA))
                    e_ = Es.next()
                    em = Ems.next()
                    S.op("act", lambda: nc.scalar.activation(out=e_.t[:], in_=l_.t[:], func=AF.Exp),
                         reads=[l_.b], writes=[e_.b])
                    src_ = e_
                    if st >= jb:
                        jrel = st - jb
                        S.op("pool", lambda: pe_.tensor_tensor(
                            out=em.t[:], in0=e_.t[:], in1=eb.t[:, jrel, :, :].rearrange("p a b -> p (a b)"), op=ALU.mult),
                            reads=[e_.b, eb.b], writes=[em.b])
                        src_ = em
                    S.op("pool", lambda: pe_.tensor_tensor(
                        out=em.t[:].rearrange("p (a q) -> p a q", a=2), in0=src_.t[:].rearrange("p (a q) -> p a q", a=2),
                        in1=mT.t[:, st * QB:(st + 1) * QB].unsqueeze(1).to_broadcast([P, 2, QB]), op=ALU.mult),
                        reads=[src_.b, mT.b], writes=[em.b])
                    for k2 in range(2):
                        S.mm(pO2[k2], pO2[k2].t[:], [(cn.t[:, st, k2 * P:(k2 + 1) * P], em.t[:])],
                             reads=[cn.b, em.b], first=(st == 0), last=(st == NS - 1))
                    S.mm(pD, pD.t[:], [(ones.t[:], em.t[:])], reads=[ones.b, em.b],
                         first=(st == 0), last=(st == NS - 1))
                    yield
                for k2 in range(2):
                    S.op("act", lambda k2=k2: nc.scalar.copy(out=obT.t[:, k2, :], in_=pO2[k2].t[:]),
                         reads=[pO2[k2].b], writes=[obT.b])
                S.op("act", lambda: nc.scalar.activation(out=dsb.t[:], in_=pD.t[:], func=AF.Ln), reads=[pD.b], writes=[dsb.b])
                u_ = pGB.next()
                for a_ in range(2):
                    h = 2 * hp + a_
                    S.mm(u_, u_.t[:, a_ * QB:(a_ + 1) * QB],
                         [(wuv.t[:, h, k2, :], obT.t[:, k2, a_ * QB:(a_ + 1) * QB]) for k2 in range(2)],
                         reads=[wuv.b, obT.b])
                S.op("act", lambda: nc.scalar.copy(out=usb.t[:], in_=u_.t[:]), reads=[u_.b], writes=[usb.b])
                oh = ohs.next()
                S.op("act", lambda: nc.scalar.activation(out=dsb.t[:], in_=dsb.t[:], func=AF.Exp, scale=-1.0),
                     reads=[dsb.b], writes=[dsb.b])
                S.op("pool", lambda: pe_.tensor_tensor(out=oh.t[:], in0=usb.t[:], in1=dsb.t[:], op=ALU.mult),
                     reads=[usb.b, dsb.b], writes=[oh.b])
                S.dma(oT.t[2 * hp * P:(2 * hp + 2) * P, qsl].rearrange("(a p) q -> p a q", p=P),
                      oh.t[:].rearrange("p (a q) -> p a q", a=2), reads=[oh.b], writes=[oT.b])
                if hp + 1 < 8:
                    eb_next = adjust_bias(bt_next, hp + 1)
                yield

        def run_all(g):
            n = 0
            for _ in g:
                n += 1
            return n

        def count_select(qb_):
            n = 0
            for qt2 in range(QB // P):
                qt = qb_ * (QB // P) + qt2
                nk = (qt + 1) * P
                n += 16 * ((nk + 511) // 512) + (2 * (KSEL // 8) - 1 if nk > KSEL else 0) + 1 + (qt + 1) // 4
            return n

        NQB = T // QB
        NT2 = QB // P
        for t2 in range(NT2):
            sel_idx(0, t2)
            run_all(sel_topk(0, t2))
            sel_maskT(0, t2)
        for qb_ in range(NQB):
            B = gen_attn(qb_)
            NSb = (qb_ * QB + QB) // P
            nBh = (8 // NT2) * (NSb + 1)
            for t2 in range(NT2):
                A = None
                nA = 0
                if qb_ + 1 < NQB:
                    sel_idx(qb_ + 1, t2)
                    A = sel_topk(qb_ + 1, t2)
                    nA = 3 * TOPK_NIT + 1
                doneA = 0
                for i in range(nBh):
                    next(B, None)
                    if A is not None:
                        tgt = (i + 1) * nA // nBh
                        while doneA < tgt:
                            next(A, None)
                            doneA += 1
                if A is not None:
                    run_all(A)
                    sel_maskT(qb_ + 1, t2)
            run_all(B)
    S.barrier()
    with ExitStack() as es:
        phase_linear(S, oT, DT, C["dsa_w_out"][j], 0, D, "fm", make_resid_sink(S, es, xT))


def build_program(stop_after=None, dump=()):
    nc = bass.Bass("TRN2", target_bir_lowering=False)
    C = {}

    def din(name, shape, dt=F32):
        return TB(nc.dram_tensor(name, list(shape), dt, kind="ExternalInput").ap())

    x_in = din("xT_in", [D, T])
    C["norm_mix"] = [din(f"norm_mix{i}", [P, DT]) for i in range(DEPTH)]
    C["norm_ffn"] = [din(f"norm_ffn{i}", [P, DT]) for i in range(DEPTH)]
    C["norm_final"] = din("norm_final", [P, DT])
    C["gla_w_in"] = [din(f"gla_w_in{j}", [D, GLA_IN]) for j in range(2)]
    C["gla_waug"] = [din(f"gla_waug{j}", [32, 1024]) for j in range(2)]
    C["gla_gn"] = [din(f"gla_gn{j}", [P, 4]) for j in range(2)]
    C["gla_w_out"] = [din(f"gla_w_out{j}", [D, D]) for j in range(2)]
    C["dsa_w_in"] = [din(f"dsa_w_in{j}", [D, DSA_IN]) for j in range(2)]
    C["dsa_kvn"] = [din(f"dsa_kvn{j}", [P, 2]) for j in range(2)]
    C["dsa_kin"] = [din(f"dsa_kin{j}", [P, 1]) for j in range(2)]
    C["dsa_w_uv"] = [din(f"dsa_w_uv{j}", [16, 256, P]) for j in range(2)]
    C["dsa_w_out"] = [din(f"dsa_w_out{j}", [D, D]) for j in range(2)]
    C["ffn_w1"] = [din(f"ffn_w1_{i}", [D, DFF]) for i in range(DEPTH)]
    C["ffn_w3"] = [din(f"ffn_w3_{i}", [D, DFF]) for i in range(DEPTH)]
    C["ffn_w2"] = [din(f"ffn_w2_{i}", [DFF, D]) for i in range(DEPTH)]
    C["mrev"] = din("mrev", [P, P])
    C["cind"] = din("cind", [P, 2])
    C["ident"] = din("ident", [P, P])
    C["b15"] = din("b15", [P, 16])
    C["biasN"] = din("biasN", [16, P, 3, QB])
    outT = TB(nc.dram_tensor("outT", [D, T], F32, kind="ExternalOutput").ap())

    scratch = {}

    def dram(name, shape, dt):
        if name not in scratch:
            scratch[name] = TB(nc.dram_tensor("scr_" + name, list(shape), dt, kind="Internal").ap())
        return scratch[name]

    xT = dram("xT", [D, T], F32)
    hT = dram("hT", [D, T], BF16)
    with ExitStack() as es0:
        S = Sched(nc, es0)
        for k in range(DT):
            S.dma(xT.t[k * P:(k + 1) * P, :], x_in.t[k * P:(k + 1) * P, :], reads=[x_in.b], writes=[xT.b])
        S.barrier()
        nph = 0
        done = False
        for i in range(DEPTH):
            j = i // 2
            phase_norm(S, xT, hT, C["norm_mix"][i], DT, D)
            if i % 2 == 0:
                phase_gla(S, es0, hT, xT, C, j, dram)
            else:
                phase_dsa(S, es0, hT, xT, C, j, dram)
            nph += 1
            if stop_after == nph:
                done = True
                break
            phase_norm(S, xT, hT, C["norm_ffn"][i], DT, D)
            phase_ffn(S, hT, xT, C["ffn_w1"][i], C["ffn_w3"][i], C["ffn_w2"][i])
            nph += 1
            if stop_after == nph:
                done = True
                break
        for nm in dump:
            src_ = scratch[nm]
            dt_ = BF16 if nm in ("d_oT", "d_cTn", "d_qT", "d_kiTn", "d_qiT", "hT") else F32
            dst_ = TB(nc.dram_tensor("dump_" + nm, list(src_.t.shape), dt_, kind="ExternalOutput").ap())
            S.dma(dst_.t, src_.t, reads=[src_.b], writes=[dst_.b])
        if done:
            for k in range(DT):
                S.dma(outT.t[k * P:(k + 1) * P, :], xT.t[k * P:(k + 1) * P, :], reads=[xT.b], writes=[outT.b])
        else:
            phase_norm(S, xT, outT, C["norm_final"], DT, D, dst_f32=True)
        S.barrier()
    return nc


def t5_bucket_np(rel):
    nb = 16
    me = 8
    ret = (rel > 0).astype(np.int32) * nb
    n = np.abs(rel)
    large = me + (np.log(np.maximum(n, 1).astype(np.float32) / me) / math.log(128 / me) * (nb - me)).astype(np.int32)
    large = np.minimum(large, nb - 1)
    return ret + np.where(n < me, n, large)


def prep_inputs(inputs):
    f = lambda a: np.ascontiguousarray(np.asarray(a, dtype=np.float32))
    vecl = lambda v, k: f(np.asarray(v).reshape(k, P).T)
    sh = {}
    for i in range(DEPTH):
        sh[f"norm_mix{i}"] = vecl(inputs["norm_mix"][i], DT)
        sh[f"norm_ffn{i}"] = vecl(inputs["norm_ffn"][i], DT)
        sh[f"ffn_w1_{i}"] = f(inputs["ffn_w1"][i])
        sh[f"ffn_w3_{i}"] = f(inputs["ffn_w3"][i])
        sh[f"ffn_w2_{i}"] = f(inputs["ffn_w2"][i])
    sh["norm_final"] = vecl(inputs["norm_final"], DT)
    for j in range(2):
        sh[f"gla_w_in{j}"] = f(inputs["gla_w_in"][j])
        waug = np.zeros((32, 1024), np.float32)
        waug[0:16] = inputs["gla_w_a2"][j]
        waug[16] = inputs["gla_b_a"][j]
        sh[f"gla_waug{j}"] = waug
        sh[f"gla_gn{j}"] = vecl(inputs["gla_g_norm"][j], 4)
        sh[f"gla_w_out{j}"] = f(inputs["gla_w_out"][j])
        sh[f"dsa_w_in{j}"] = f(inputs["dsa_w_in"][j])
        sh[f"dsa_kvn{j}"] = vecl(inputs["dsa_kv_norm"][j], 2)
        sh[f"dsa_kin{j}"] = vecl(inputs["dsa_kidx_norm"][j], 1)
        sh[f"dsa_w_uv{j}"] = f(inputs["dsa_w_uv"][j])
        sh[f"dsa_w_out{j}"] = f(inputs["dsa_w_out"][j])
    tp = np.arange(P)
    same = (tp[:, None] // CH) == (tp[None, :] // CH)
    sh["mrev"] = (same & (tp[:, None] > tp[None, :])).astype(np.float32)
    sh["cind"] = (tp[:, None] // CH == np.arange(2)[None, :]).astype(np.float32)
    sh["ident"] = np.eye(P, dtype=np.float32)
    rb = np.asarray(inputs["rel_bias"], dtype=np.float32)
    sh["b15"] = f(np.broadcast_to(rb[15][None, :], (P, 16)))
    sl = np.arange(P)[:, None, None]
    jj = np.arange(3)[None, :, None]
    ql = np.arange(QB)[None, None, :]
    rel = (jj - 1) * P + sl - ql
    bidx = t5_bucket_np(rel)
    sh["biasN"] = f(np.transpose(rb[bidx], (3, 0, 1, 2)))
    return sh


def kernel(**inputs):
    shared = prep_inputs(inputs)
    x = np.asarray(inputs["x"], dtype=np.float32)
    in_maps = []
    for b in range(NB):
        m = dict(shared)
        m["xT_in"] = np.ascontiguousarray(x[b].T)
        in_maps.append(m)
    nc = build_program()
    res = run_bass_kernel_spmd(nc, in_maps, core_ids=list(range(NB)))
    out = np.stack([np.ascontiguousarray(res.results[b]["outT"].T) for b in range(NB)], axis=0)
    return out.astype(np.float32)
```
